# Optimizing a Trainium2 kernel written in Bass

```python
import math
import jax, jax.numpy as jnp
from jax import lax
import numpy as np

D_MODEL = 1024
BATCH = 4
SEQ = 4096
DEPTH = 1

N_META = 16
D_MIX = D_MODEL
D_CONV = D_MIX // 2
D_SSM = D_MIX - D_CONV
CONV_WIDTH = 31
SSM_GROUP = 16
N_SSM_GROUPS = D_SSM // SSM_GROUP
SSM_STATE = 64
N_DIR = 2
D_FF = 4 * D_MODEL
D_IN_PROJ = 2 * D_CONV + D_SSM
DT_MIN = 1e-3
DT_MAX = 1e-1
NORM_EPS = 1e-5

kernel_name = 'hymba_conformer_s5_bidir_block'


def _rms_norm(x, g):
    xf = x.astype(jnp.float32)
    y = xf * lax.rsqrt(jnp.mean(jnp.square(xf), axis=-1, keepdims=True) + NORM_EPS)
    return (y * g.astype(jnp.float32)).astype(x.dtype)


def _layer_norm(x, g, b):
    xf = x.astype(jnp.float32)
    xc = xf - jnp.mean(xf, axis=-1, keepdims=True)
    y = xc * lax.rsqrt(jnp.mean(jnp.square(xc), axis=-1, keepdims=True) + NORM_EPS)
    return (y * g.astype(jnp.float32) + b.astype(jnp.float32)).astype(x.dtype)


def _conv_module(v, gate, conv_w, conv_b, ln_g, ln_b):
    u = v * jax.nn.sigmoid(gate)
    pad = CONV_WIDTH // 2
    y = lax.conv_general_dilated(
        u, conv_w[:, None, :].astype(u.dtype), window_strides=(1,),
        padding=[(pad, pad)], dimension_numbers=('NWC', 'WIO', 'NWC'),
        feature_group_count=D_CONV)
    y = y + conv_b.astype(u.dtype)
    return jax.nn.silu(_layer_norm(y, ln_g, ln_b))


def _linear_recurrence_op(left, right):
    a_l, b_l = left
    a_r, b_r = right
    return a_r * a_l, a_r * b_l + b_r


def _s5_direction(u, lam_re, lam_im, log_dt, b_re, b_im, c_re, c_im, reverse):
    f32 = jnp.float32
    lam = lax.complex(lam_re.astype(f32), lam_im.astype(f32))
    dt = jnp.exp(log_dt.astype(f32))[:, None]
    lam_bar = jnp.exp(lam * dt)
    b_mat = lax.complex(b_re.astype(f32), b_im.astype(f32))
    b_bar = ((lam_bar - 1.0) / lam)[..., None] * b_mat
    c_mat = lax.complex(c_re.astype(f32), c_im.astype(f32))
    bu = jnp.einsum('gph,blgh->blgp', b_bar, u.astype(jnp.complex64))
    a = jnp.broadcast_to(lam_bar, bu.shape)
    _, states = lax.associative_scan(_linear_recurrence_op, (a, bu), reverse=reverse, axis=1)
    return jnp.einsum('ghp,blgp->blgh', c_mat, states).real


def _s5_mixer(u, lam_re, lam_im, log_dt, b_re, b_im, c_re, c_im, d_skip, glu_w, glu_b):
    bsz, length, _ = u.shape
    uf = u.astype(jnp.float32)
    ug = uf.reshape(bsz, length, N_SSM_GROUPS, SSM_GROUP)
    y = _s5_direction(ug, lam_re[0], lam_im[0], log_dt[0], b_re[0], b_im[0], c_re[0], c_im[0], False)
    for d in range(1, N_DIR):
        y = y + _s5_direction(ug, lam_re[d], lam_im[d], log_dt[d], b_re[d], b_im[d],
                              c_re[d], c_im[d], True)
    y = y.reshape(bsz, length, D_SSM) + d_skip.astype(jnp.float32) * uf
    y = jax.nn.gelu(y).astype(u.dtype)
    return y * jax.nn.sigmoid(y @ glu_w + glu_b)


def setup_inputs(seed: int = 0) -> dict:
    key = jax.random.key(seed)
    ks = jax.random.split(key, 24)
    f32 = jnp.float32

    def nrm(k, shape, scale):
        return jax.random.normal(k, shape, f32) * scale

    ssm_shape = (DEPTH, N_DIR, N_SSM_GROUPS, SSM_STATE)
    n_idx = jnp.arange(SSM_STATE, dtype=f32)
    return {
        'x': nrm(ks[0], (BATCH, SEQ, D_MODEL), 1.0),
        'meta_tokens': nrm(ks[1], (N_META, D_MODEL), 1.0),
        'norm_mix_g': 1.0 + nrm(ks[2], (DEPTH, D_MODEL), 0.01),
        'w_in': nrm(ks[3], (DEPTH, D_MODEL, D_IN_PROJ), D_MODEL ** -0.5),
        'conv_w': nrm(ks[4], (DEPTH, CONV_WIDTH, D_CONV), CONV_WIDTH ** -0.5),
        'conv_b': nrm(ks[5], (DEPTH, D_CONV), 0.01),
        'conv_ln_g': 1.0 + nrm(ks[6], (DEPTH, D_CONV), 0.01),
        'conv_ln_b': nrm(ks[7], (DEPTH, D_CONV), 0.01),
        'ssm_lam_re': -0.5 + nrm(ks[8], ssm_shape, 0.01),
        'ssm_lam_im': math.pi * n_idx + nrm(ks[9], ssm_shape, 0.01),
        'ssm_log_dt': jax.random.uniform(ks[10], (DEPTH, N_DIR, N_SSM_GROUPS), f32,
                                         math.log(DT_MIN), math.log(DT_MAX)),
        'ssm_b_re': nrm(ks[11], (DEPTH, N_DIR, N_SSM_GROUPS, SSM_STATE, SSM_GROUP), (2 * SSM_GROUP) ** -0.5),
        'ssm_b_im': nrm(ks[12], (DEPTH, N_DIR, N_SSM_GROUPS, SSM_STATE, SSM_GROUP), (2 * SSM_GROUP) ** -0.5),
        'ssm_c_re': nrm(ks[13], (DEPTH, N_DIR, N_SSM_GROUPS, SSM_GROUP, SSM_STATE), SSM_STATE ** -0.5),
        'ssm_c_im': nrm(ks[14], (DEPTH, N_DIR, N_SSM_GROUPS, SSM_GROUP, SSM_STATE), SSM_STATE ** -0.5),
        'ssm_d': nrm(ks[15], (DEPTH, D_SSM), 1.0),
        'ssm_glu_w': nrm(ks[16], (DEPTH, D_SSM, D_SSM), D_SSM ** -0.5),
        'ssm_glu_b': nrm(ks[17], (DEPTH, D_SSM), 0.01),
        'w_out': nrm(ks[18], (DEPTH, D_MIX, D_MODEL), D_MIX ** -0.5),
        'norm_ffn_g': 1.0 + nrm(ks[19], (DEPTH, D_MODEL), 0.01),
        'w_ff1': nrm(ks[20], (DEPTH, D_MODEL, D_FF), D_MODEL ** -0.5),
        'w_ff2': nrm(ks[21], (DEPTH, D_FF, D_MODEL), D_FF ** -0.5),
        'norm_final_g': 1.0 + nrm(ks[22], (D_MODEL,), 0.01),
    }


def reference(x, meta_tokens, norm_mix_g, w_in, conv_w, conv_b, conv_ln_g, conv_ln_b,
              ssm_lam_re, ssm_lam_im, ssm_log_dt, ssm_b_re, ssm_b_im, ssm_c_re, ssm_c_im,
              ssm_d, ssm_glu_w, ssm_glu_b, w_out, norm_ffn_g, w_ff1, w_ff2, norm_final_g):
    bsz = x.shape[0]
    meta = jnp.broadcast_to(meta_tokens[None].astype(x.dtype), (bsz, N_META, D_MODEL))
    h = jnp.concatenate([meta, x], axis=1)
    for layer in range(DEPTH):
        z = _rms_norm(h, norm_mix_g[layer])
        p = z @ w_in[layer]
        conv_v = p[..., :D_CONV]
        conv_gate = p[..., D_CONV:2 * D_CONV]
        ssm_u = p[..., 2 * D_CONV:]
        conv_out = _conv_module(conv_v, conv_gate, conv_w[layer], conv_b[layer],
                                conv_ln_g[layer], conv_ln_b[layer])
        ssm_out = _s5_mixer(ssm_u, ssm_lam_re[layer], ssm_lam_im[layer], ssm_log_dt[layer],
                            ssm_b_re[layer], ssm_b_im[layer], ssm_c_re[layer], ssm_c_im[layer],
                            ssm_d[layer], ssm_glu_w[layer], ssm_glu_b[layer])
        mixed = jnp.concatenate([conv_out, ssm_out.astype(conv_out.dtype)], axis=-1)
        h = h + mixed @ w_out[layer]
        z = _rms_norm(h, norm_ffn_g[layer])
        h = h + jnp.square(jax.nn.relu(z @ w_ff1[layer])) @ w_ff2[layer]
    h = _rms_norm(h, norm_final_g)
    return h[:, N_META:, :]
```

```python
import numpy as np
from contextlib import ExitStack, contextmanager
import concourse.bass as bass
import concourse.mybir as mybir
from concourse.bass_utils import run_bass_kernel_spmd

F32 = mybir.dt.float32
BF16 = mybir.dt.bfloat16
AF = mybir.ActivationFunctionType
ALU = mybir.AluOpType


class _Op:
    __slots__ = ("eng", "seq", "sigval", "dma", "sem", "target")


class Prog:
    def __init__(self, nc, same_eng_sync=True, n_dma_sems=32):
        self.nc = nc
        self.same = same_eng_sync
        self.root = ExitStack()
        self.stacks = [self.root]
        self.eng = {"pe": nc.tensor, "act": nc.scalar, "dve": nc.vector, "pool": nc.gpsimd, "sp": nc.sync}
        self.esem = {k: self.root.enter_context(nc.semaphore("es_" + k)) for k in self.eng}
        self.ecnt = {k: 0 for k in self.eng}
        self.eseq = {k: 0 for k in self.eng}
        self.esigs = {k: [] for k in self.eng}
        nd = n_dma_sems // 2
        self.dpool = {"sp": list(range(0, nd)), "act": list(range(0, nd)), "pool": list(range(nd, 2 * nd))}
        self.dsems = [self.root.enter_context(nc.semaphore("ds%d" % i)) for i in range(2 * nd)]
        self.dcum = [0] * (2 * nd)
        self.dlast = [None] * (2 * nd)
        self.dnext = {"sp": 0, "act": 0, "pool": 0}
        self.waited = {k: {} for k in self.eng}
        self.last_w = {}
        self.readers = {}
        self.finals = []
        self.nops = 0

    def sb(self, name, shape, dt):
        return self.stacks[-1].enter_context(self.nc.sbuf_tensor("s_" + name, shape, dt))

    def ps(self, name, shape, dt):
        return self.stacks[-1].enter_context(self.nc.psum_tensor("p_" + name, shape, dt))

    @contextmanager
    def scope(self):
        st = ExitStack()
        self.stacks.append(st)
        try:
            yield
        finally:
            self.fence()
            self.stacks.pop()
            st.close()

    def fence(self):
        for e in self.eng:
            for e2 in self.eng:
                if self.ecnt[e2] > 0:
                    self._wait(e, self.esem[e2], self.ecnt[e2])
            for o in self.dlast:
                if o is not None:
                    self._wait(e, o.sem, o.target)

    def _wait(self, eng, sem, val):
        d = self.waited[eng]
        if d.get(sem.name, -1) >= val:
            return
        d[sem.name] = val
        self.eng[eng].wait_ge(sem, val)

    def _wait_op(self, eng, dep):
        if dep.dma:
            self._wait(eng, dep.sem, dep.target)
            return
        if dep.eng == eng and (eng == "pe" or not self.same):
            return
        sv = dep.sigval
        if sv is None:
            for (s, v) in self.esigs[dep.eng]:
                if s >= dep.seq:
                    sv = v
                    break
            if sv is None:
                raise RuntimeError("dependency on non-signalling op with no later signal on " + dep.eng)
            dep.sigval = sv
        self._wait(eng, self.esem[dep.eng], sv)

    def _deps(self, r, w):
        deps = []
        for k in r:
            o = self.last_w.get(k)
            if o is not None:
                deps.append(o)
        for k in w:
            o = self.last_w.get(k)
            if o is not None:
                deps.append(o)
            deps.extend(self.readers.get(k, ()))
        return deps

    def _record(self, op, r, w):
        for k in r:
            self.readers.setdefault(k, []).append(op)
        for k in w:
            self.last_w[k] = op
            self.readers[k] = []

    def op(self, eng, fn, r=(), w=(), sig=True):
        deps = self._deps(r, w)
        o = _Op()
        o.eng = eng
        o.dma = False
        o.sem = None
        o.target = None
        for d in deps:
            self._wait_op(eng, d)
        ins = fn(self.eng[eng])
        self.eseq[eng] += 1
        o.seq = self.eseq[eng]
        if sig:
            self.ecnt[eng] += 1
            o.sigval = self.ecnt[eng]
            ins.then_inc(self.esem[eng], 1)
            self.esigs[eng].append((o.seq, o.sigval))
            if len(self.esigs[eng]) > 4096:
                del self.esigs[eng][:2048]
        else:
            o.sigval = None
        self._record(o, r, w)
        self.nops += 1
        return o

    def dma(self, eng, out, in_, r=(), w=(), final=False, **kw):
        deps = self._deps(r, w)
        pool_ = self.dpool[eng]
        key_ = "pool" if eng == "pool" else "sp"
        i = pool_[self.dnext[key_] % len(pool_)]
        self.dnext[key_] += 1
        if self.dlast[i] is not None:
            deps.append(self.dlast[i])
        for d in deps:
            self._wait_op(eng, d)
        o = _Op()
        o.eng = eng
        o.dma = True
        o.sem = self.dsems[i]
        self.dcum[i] += 16
        o.target = self.dcum[i]
        o.seq = None
        o.sigval = None
        self.eng[eng].dma_start(out=out, in_=in_, **kw).then_inc(o.sem, 16)
        self.dlast[i] = o
        self._record(o, r, w)
        if final:
            self.finals.append(o)
        self.nops += 1
        return o

    def emit(self):
        for o in self.finals:
            self._wait_op("sp", o)
        self.stacks[-1].close()


D = 1024
NOWN = 2048
LT = 4128
NFWD = 258
NBWD = 514
TWOPI_S = 6.283179
PV_GMIX, PV_GFFN, PV_CONVB, PV_LNG, PV_LNB, PV_SSMD, PV_GLUB = 0, 8, 16, 20, 24, 28, 32


class V:
    __slots__ = ("ap", "key")

    def __init__(self, ap, key):
        self.ap = ap
        self.key = key


def _build(stage=99):
    nc = bass.Bass("TRN2", target_bir_lowering=False)

    def din(name, shape):
        return nc.dram_tensor(name, list(shape), F32, kind="ExternalInput").ap()

    def dout(name, shape, dt=F32):
        return nc.dram_tensor(name, list(shape), dt, kind="ExternalOutput").ap()

    xl = din("xl", [LT, D])
    w_in_d = din("w_in", [D, 1536])
    w_out_d = din("w_out", [D, D])
    w_ff1_d = din("w_ff1", [D, 4096])
    w_ff2_d = din("w_ff2", [4096, D])
    glu_w_d = din("glu_w", [512, 512])
    pv_d = din("pv", [128, 36])
    cw_d = din("cw", [128, 4, 31])
    gfin_d = din("gfin", [D])
    ident_d = din("ident", [128, 128])
    lam_c_d = din("lam_c", [128, 4, 3, 2, 128])
    b_c_d = din("b_c", [128, 4, 2, 2, 128])
    lam_s_d = din("lam_s", [128, 3, 4, 2, 4])
    c_s_d = din("c_s", [128, 4, 2, 2, 4, 32])
    b_s_d = din("b_s", [128, 4, 2, 2, 4, 32])
    iota_d = din("iota", [128, NBWD])
    out_d = dout("out", [NOWN, D])

    P = Prog(nc)
    op = P.op

    ident = P.sb("ident", [128, 128], BF16)
    identf = P.sb("identf", [128, 128], F32)
    pv = P.sb("pv", [128, 36], F32)
    epst = P.sb("epst", [128, 1], F32)
    hpi = P.sb("hpi", [128, 1], F32)
    onesf = P.sb("onesf", [128, 128], F32)
    pb = [P.ps("pb%d" % i, [128, 512], F32) for i in range(8)]
    pbk = ["pb%d" % i for i in range(8)]
    P.dma("pool", ident[:], ident_d, w=["ident"])
    P.dma("sp", identf[:], ident_d, w=["identf"])
    P.dma("sp", pv[:], pv_d, w=["pv"])
    op("pool", lambda e: e.memset(epst[:], 1e-5), w=["epst"])
    op("pool", lambda e: e.memset(hpi[:], float(np.pi / 2)), w=["hpi"])
    op("pool", lambda e: e.memset(onesf[:], 1.0 / 512.0), w=["onesf"])

    mixed = P.sb("mixed", [128, 8, NOWN], BF16)
    w_out_bf = P.sb("w_out_bf", [128, 8, D], BF16)

    def rms_tile(xt_ap, nr, xkey, ss, sskey, sq, zt_ap, ztkey, eng_scale):
        op("act", lambda e: e.activation(sq[:nr, :], xt_ap, AF.Square, accum_out=ss[:nr, 0:1]),
           r=[xkey], w=["sq", sskey])
        op("act", lambda e: e.activation(ss[:nr, 1:2], ss[:nr, 0:1], AF.Sqrt, bias=epst[:nr, 0:1], scale=1.0 / D),
           r=[sskey, "epst"], w=[sskey])
        op("dve", lambda e: e.reciprocal(ss[:nr, 2:3], ss[:nr, 1:2]), r=[sskey], w=[sskey])
        if eng_scale == "dve":
            op("dve", lambda e: e.tensor_scalar(zt_ap, xt_ap, ss[:nr, 2:3], None, op0=ALU.mult), r=[xkey, sskey], w=[ztkey])
        else:
            op("act", lambda e: e.activation(zt_ap, xt_ap, AF.Identity, scale=ss[:nr, 2:3]), r=[xkey, sskey], w=[ztkey])

    def transpose_tile(zt_t, ztkey, nr, pbi, gcol, dst_ap3, dstkey):
        pbv = pb[pbi][:].bitcast(BF16)
        for k in range(8):
            op("pe", lambda e, k=k: e.transpose(pbv[:, k * 128:k * 128 + nr], zt_t[:nr, k * 128:(k + 1) * 128],
                                               ident[:nr, :nr]),
               r=[ztkey, "ident"], w=[pbk[pbi]], sig=(k == 7))
        src = pbv.rearrange("p (k t) -> p k t", t=128)[:, :, :nr]
        g3 = pv[:, gcol:gcol + 8].unsqueeze(2).to_broadcast([128, 8, nr])
        op("dve", lambda e: e.tensor_tensor(dst_ap3, src, g3, op=ALU.mult), r=[pbk[pbi], "pv"], w=[dstkey])

    dumps = []

    def dump(name, t, shape, dt, keys):
        dd = dout(name, shape, dt)
        dumps.append(P.dma("sp", dd, t, r=keys, w=["dump_" + name], final=True))

    with P.scope():
        u_s = P.sb("u_s", [128, 4, 8, 516], BF16)
        lsa = P.sb("lsa", [128, 3, 4, 2, 4], F32)
        P.dma("sp", lsa[:], lam_s_d, w=["lsa"])
        uniq = [0]

        def mk(pre, F, n):
            uniq[0] += 1
            return [V(P.sb("%s%d_%d" % (pre, uniq[0], i), [128, F], F32)[:], "%s%d_%d" % (pre, uniq[0], i)) for i in range(n)]

        def tt(eng, o, a, b, o_):
            op(eng, lambda e: e.tensor_tensor(o.ap, a.ap, b.ap, op=o_), r=[a.key, b.key], w=[o.key])

        def ts(eng, o, a, s1, o1):
            op(eng, lambda e: e.tensor_scalar(o.ap, a.ap, s1, None, op0=o1), r=[a.key], w=[o.key])

        def act(o, a, f, **kw):
            op("act", lambda e: e.activation(o.ap, a.ap, f, **kw), r=[a.key, "hpi"], w=[o.key])

        def cmul(eng, ore, oim, are, aim, bre, bim, t1, t2):
            tt(eng, t1, are, bre, ALU.mult)
            tt(eng, t2, aim, bim, ALU.mult)
            tt(eng, ore, t1, t2, ALU.subtract)
            tt(eng, t1, are, bim, ALU.mult)
            tt(eng, t2, aim, bre, ALU.mult)
            tt(eng, oim, t1, t2, ALU.add)

        def dbl(eng, c, s, t1, t2, t3):
            tt(eng, t1, c, c, ALU.mult)
            tt(eng, t2, s, s, ALU.mult)
            tt(eng, t3, c, s, ALU.mult)
            ts(eng, s, t3, 2.0, ALU.mult)
            tt(eng, c, t1, t2, ALU.subtract)

        sm_ops = []
        real_op = op
        op = lambda *a_, **k_: sm_ops.append((a_, k_))
        eng = "dve"
        lre, lim, ldt = [V(lsa[:, q].rearrange("p a b c -> p (a b c)"), "lsa") for q in range(3)]
        T = mk("sm", 32, 14)
        dt, are, th, r, s, c, t1, t2, t3, S1re, S1im, sre, sim_, a = T
        act(dt, ldt, AF.Exp)
        tt(eng, are, lre, dt, ALU.mult)
        tt(eng, th, lim, dt, ALU.mult)
        act(r, are, AF.Exp)
        act(s, th, AF.Sin, scale=1.0 / 32.0)
        act(c, th, AF.Sin, scale=1.0 / 32.0, bias=hpi[:, 0:1])
        for _ in range(5):
            dbl(eng, c, s, t1, t2, t3)
        tt(eng, S1re, r, c, ALU.mult)
        tt(eng, S1im, r, s, ALU.mult)
        ts(eng, a, S1re, -1.0, ALU.add)
        tt(eng, t1, a, lre, ALU.mult)
        tt(eng, t2, S1im, lim, ALU.mult)
        tt(eng, sre, t1, t2, ALU.add)
        tt(eng, t1, S1im, lre, ALU.mult)
        tt(eng, t2, a, lim, ALU.mult)
        tt(eng, sim_, t1, t2, ALU.subtract)
        tt(eng, t1, lre, lre, ALU.mult)
        tt(eng, t2, lim, lim, ALU.mult)
        tt(eng, t3, t1, t2, ALU.add)
        op("dve", lambda e: e.reciprocal(t3.ap, t3.ap), r=[t3.key], w=[t3.key])
        tt(eng, sre, sre, t3, ALU.mult)
        tt(eng, sim_, sim_, t3, ALU.mult)
        u1, u2 = t1, t2
        rho8 = mk("smr", 32, 1)[0]
        act(rho8, are, AF.Exp, scale=8.0)
        pw_ = mk("smp", 32, 16)
        npim = mk("smn", 32, 9)
        qk_ = mk("smq", 32, 16)
        nqim = mk("smm", 32, 8)
        Pk = {1: (S1re, S1im)}
        for k in range(2, 9):
            Pk[k] = (pw_[2 * (k - 2)], pw_[2 * (k - 2) + 1])
            cmul("dve", Pk[k][0], Pk[k][1], Pk[k - 1][0], Pk[k - 1][1], S1re, S1im, u1, u2)
        for k in range(1, 9):
            ts("dve", npim[k], Pk[k][1], -1.0, ALU.mult)
        Qk = {0: (sre, sim_)}
        for k in range(1, 8):
            Qk[k] = (qk_[2 * k], qk_[2 * k + 1])
            cmul("dve", Qk[k][0], Qk[k][1], sre, sim_, Pk[k][0], Pk[k][1], u1, u2)
        for k in range(8):
            ts("dve", nqim[k], Qk[k][1], -1.0, ALU.mult)
        sturn = mk("smt", 32, 1)[0]
        ts(eng, sturn, th, float(8.0 / (2.0 * np.pi)), ALU.mult)
        op = real_op
        sm_pos = [0]

        def sm_replay(n):
            for a_, k_ in sm_ops[sm_pos[0]:sm_pos[0] + n]:
                op(*a_, **k_)
            sm_pos[0] = min(len(sm_ops), sm_pos[0] + n)

        with P.scope():
            uc = P.sb("uc", [128, 4, 2080], BF16)
            with P.scope():
                w_in_bf = P.sb("w_in_bf", [128, 8, 1536], BF16)
                for k in range(8):
                    P.dma("pool", w_in_bf[:, k, :], w_in_d[k * 128:(k + 1) * 128, :], w=["win%d" % k])
                winkeys = ["win%d" % k for k in range(8)]
                xt = [P.sb("xt%d" % i, [128, D], F32) for i in range(4)]
                zt = [P.sb("zt%d" % i, [128, D], BF16) for i in range(4)]
                sq = P.sb("sq", [128, D], F32)
                ssa = [P.sb("ssa%d" % i, [128, 4], F32) for i in range(4)]
                zfm = [P.sb("zfm%d" % i, [128, 8, 512], BF16) for i in range(2)]
                sg = [P.sb("sg%d" % i, [128, 512], F32) for i in range(2)]
                NTILE = 33

                def norm_act(g):
                    b = g % 4
                    r0 = g * 128
                    nr = min(128, LT - r0)
                    P.dma("sp", xt[b][:nr, :], xl[r0:r0 + nr, :], w=["xt%d" % b])
                    rms_tile(xt[b][:nr, :], nr, "xt%d" % b, ssa[b], "ssa%d" % b, sq, zt[b][:nr, :], "zt%d" % b, "dve")

                def norm_tr(g):
                    b = g % 4
                    blk, ti = g // 4, g % 4
                    r0 = g * 128
                    nr = min(128, LT - r0)
                    zb = blk % 2
                    transpose_tile(zt[b], "zt%d" % b, nr, g % 2, PV_GMIX, zfm[zb][:, :, ti * 128:ti * 128 + nr],
                                   "zfm%d_%d" % (zb, ti))

                def inproj_c(blk, c):
                    t0 = blk * 512
                    nb = min(512, LT - t0)
                    zb = blk % 2
                    zf = zfm[zb]
                    zkeys = ["zfm%d_%d" % (zb, ti) for ti in range((nb + 127) // 128)]
                    nfull = nb if blk < 4 else (32 if blk == 4 else 0)
                    if True:
                        if nfull > 0:
                            pv_i, pg_i = 2 + c % 2, 4 + c % 2
                            for k in range(8):
                                op("pe", lambda e, k=k, c=c: e.matmul(pb[pv_i][:, :nfull], w_in_bf[:, k, c * 128:(c + 1) * 128],
                                                                     zf[:, k, :nfull], start=(k == 0), stop=(k == 7)),
                                   r=zkeys + winkeys, w=[pbk[pv_i]], sig=(k == 7))
                            for k in range(8):
                                op("pe", lambda e, k=k, c=c: e.matmul(pb[pg_i][:, :nfull],
                                                                     w_in_bf[:, k, 512 + c * 128:512 + (c + 1) * 128],
                                                                     zf[:, k, :nfull], start=(k == 0), stop=(k == 7)),
                                   r=zkeys + winkeys, w=[pbk[pg_i]], sig=(k == 7))
                            sgt = sg[c % 2]
                            op("act", lambda e: e.activation(sgt[:, :nfull], pb[pg_i][:, :nfull], AF.Sigmoid),
                               r=[pbk[pg_i]], w=["sg%d" % (c % 2)])
                            op("dve", lambda e, c=c: e.tensor_tensor(uc[:, c, t0:t0 + nfull], pb[pv_i][:, :nfull],
                                                                    sgt[:, :nfull], op=ALU.mult),
                               r=[pbk[pv_i], "sg%d" % (c % 2)], w=["uc%d" % c])
                        pu_i = 6 + c % 2
                        for k in range(8):
                            op("pe", lambda e, k=k, c=c: e.matmul(pb[pu_i][:, :nb],
                                                                 w_in_bf[:, k, 1024 + c * 128:1024 + (c + 1) * 128],
                                                                 zf[:, k, :nb], start=(k == 0), stop=(k == 7)),
                               r=zkeys + winkeys, w=[pbk[pu_i]], sig=(k == 7))
                        op("act", lambda e, c=c: e.copy(u_s[:, c, :, t0 // 8:t0 // 8 + nb // 8],
                                                        pb[pu_i][:, :nb].rearrange("p (c s) -> p s c", s=8)),
                           r=[pbk[pu_i]], w=["us%d" % c])
                norm_act(0)
                norm_act(1)
                for g in range(4):
                    norm_act(g + 2)
                    norm_tr(g)
                for blk in range(9):
                    for c in range(4):
                        g_tr = 4 * (blk + 1) + c
                        if c % 2 == 0:
                            if g_tr + 2 < NTILE:
                                norm_act(g_tr + 2)
                            if g_tr + 3 < NTILE:
                                norm_act(g_tr + 3)
                        if g_tr < NTILE:
                            norm_tr(g_tr)
                        inproj_c(blk, c)
                        if blk == 3 and c == 0:
                            P.dma("pool", w_out_bf[:], w_out_d.rearrange("(k p) n -> p k n", p=128), w=["wout"])
                        if blk >= 1:
                            sm_replay(24)
                sm_replay(len(sm_ops))
            if stage == 1:
                dump("d_uc", uc[:], [128, 4, 2080], BF16, ["uc%d" % c for c in range(4)])
                dump("d_us", u_s[:], [128, 4, 8, 516], BF16, ["us%d" % c for c in range(4)])
                P.emit()
                return nc

            with P.scope():
                cw = P.sb("cw", [128, 4, 31], F32)
                P.dma("sp", cw[:], cw_d, w=["cw"])
                CD = P.sb("CD", [128, 4, 31, 128], BF16)
                for c in range(4):
                    for k in range(31):
                        if k % 2:
                            op("act", lambda e, c=c, k=k: e.activation(CD[:, c, k, :], identf[:], AF.Identity, scale=cw[:, c, k:k + 1]),
                               r=["identf", "cw"], w=["CD%d" % c])
                        else:
                            op("dve", lambda e, c=c, k=k: e.tensor_scalar(CD[:, c, k, :], identf[:], cw[:, c, k:k + 1], None, op0=ALU.mult),
                               r=["identf", "cw"], w=["CD%d" % c])
                yc = [P.sb("yc%d" % i, [128, 4, 512], F32) for i in range(2)]
                ysq = [P.sb("ysq%d" % i, [128, 4, 512], F32) for i in range(2)]
                st = [[P.sb("st%d_%d" % (i, j), [128, 512], F32) for j in range(4)] for i in range(2)]
                tmp = [P.sb("ctmp%d" % i, [128, 512], F32) for i in range(2)]
                def conv_mm(blk):
                    pp = blk % 2
                    for c in range(4):
                        for k in range(31):
                            op("pe", lambda e, c=c, k=k: e.matmul(pb[c][:, :], CD[:, c, k, :],
                                                                 uc[:, c, blk * 512 + 1 + k: blk * 512 + 1 + k + 512],
                                                                 start=(k == 0), stop=(k == 30)),
                               r=["CD%d" % c, "uc%d" % c], w=[pbk[c]], sig=(k == 30))
                        op("act", lambda e, c=c: e.activation(yc[pp][:, c, :], pb[c][:, :], AF.Identity,
                                                              bias=pv[:, PV_CONVB + c:PV_CONVB + c + 1]),
                           r=[pbk[c], "pv"], w=["yc%d_%d" % (pp, c)])
                        op("pool", lambda e, c=c: e.tensor_tensor(ysq[pp][:, c, :], yc[pp][:, c, :], yc[pp][:, c, :], op=ALU.mult),
                           r=["yc%d_%d" % (pp, c)], w=["ysq%d_%d" % (pp, c)])

                def conv_post(blk):
                    pp = blk % 2
                    pm, pq = 4 + 2 * pp, 5 + 2 * pp
                    for c in range(4):
                        op("pe", lambda e, c=c: e.matmul(pb[pm][:, :], onesf[:], yc[pp][:, c, :], start=(c == 0), stop=(c == 3)),
                           r=["onesf", "yc%d_%d" % (pp, c)], w=[pbk[pm]], sig=(c == 3))
                    for c in range(4):
                        op("pe", lambda e, c=c: e.matmul(pb[pq][:, :], onesf[:], ysq[pp][:, c, :], start=(c == 0), stop=(c == 3)),
                           r=["onesf", "ysq%d_%d" % (pp, c)], w=[pbk[pq]], sig=(c == 3))
                    mean, msq, var, rstd = st[pp]
                    sk = "st%d" % pp
                    op("act", lambda e: e.copy(mean[:], pb[pm][:]), r=[pbk[pm]], w=[sk + "m"])
                    op("act", lambda e: e.activation(msq[:], pb[pm][:], AF.Square), r=[pbk[pm]], w=[sk + "q"])
                    op("dve", lambda e: e.tensor_tensor(var[:], pb[pq][:], msq[:], op=ALU.subtract), r=[pbk[pq], sk + "q"], w=[sk + "v"])
                    op("act", lambda e: e.activation(var[:], var[:], AF.Sqrt, bias=epst[:, 0:1]), r=[sk + "v", "epst"], w=[sk + "v"])
                    op("dve", lambda e: e.reciprocal(rstd[:], var[:]), r=[sk + "v"], w=[sk + "r"])
                    for c in range(4):
                        tm = tmp[c % 2]
                        tk = "ctmp%d" % (c % 2)
                        op("pool", lambda e, c=c, tm=tm: e.tensor_tensor(tm[:], yc[pp][:, c, :], mean[:], op=ALU.subtract),
                           r=["yc%d_%d" % (pp, c), sk + "m"], w=[tk])
                        op("dve", lambda e, tm=tm: e.tensor_tensor(tm[:], tm[:], rstd[:], op=ALU.mult), r=[tk, sk + "r"], w=[tk])
                        op("act", lambda e, c=c, tm=tm: e.activation(mixed[:, c, blk * 512:(blk + 1) * 512], tm[:], AF.Silu,
                                                                     bias=pv[:, PV_LNB + c:PV_LNB + c + 1],
                                                                     scale=pv[:, PV_LNG + c:PV_LNG + c + 1]),
                           r=[tk, "pv"], w=["mixed%d" % c])

                conv_mm(0)
                for blk in range(4):
                    if blk + 1 < 4:
                        conv_mm(blk + 1)
                    conv_post(blk)
        if stage == 2:
            dump("d_mixed", mixed[:, 0:4, :], [128, 4, NOWN], BF16, ["mixed%d" % c for c in range(4)])
            P.emit()
            return nc

        with P.scope():
            WBs = [P.sb("WB%d" % i, [128, 2, 8, 2, 128], BF16) for i in range(2)]
            CW = P.sb("CW", [128, 8, 2, 2, 4, 32], BF16)
            KF = P.sb("KF", [128, 2, 8, 128], BF16)
            iota = P.sb("iota", [128, NBWD], F32)
            tbt = [P.sb("tbt%d" % i, [128, NBWD], F32) for i in range(2)]
            Dd = P.sb("Dd", [128, 128], BF16)
            Sb = P.sb("Sb", [128, 2, 4, 2, NFWD], BF16)
            VAbig = P.sb("VAbig", [128, 3, 2, NBWD], F32)
            VAb = [VAbig[:, i] for i in range(3)]
            TBb = [P.sb("TBb%d" % i, [128, 2, NBWD], F32) for i in range(3)]
            DMb = [P.sb("DMb%d" % i, [128, 2, NBWD], F32) for i in range(2)]
            pt1 = P.sb("pt1", [128, NBWD], F32)
            pt2 = P.sb("pt2", [128, NBWD], F32)
            pt5 = P.sb("pt5", [128, NBWD], F32)
            pt6 = P.sb("pt6", [128, NBWD], F32)
            pt3 = P.sb("pt3", [128, NFWD], F32)
            pt4 = P.sb("pt4", [128, NFWD], F32)
            bcs = [P.sb("bc0", [128, 2, 2, 128], F32)] * 2
            cs = P.sb("cs", [128, 2, 2, 4, 32], F32)
            bs = P.sb("bs", [128, 2, 2, 4, 32], F32)
            Bb = P.sb("Bb", [128, 2, 2, 4, 32], BF16)
            C0 = P.sb("C0", [128, 2, 2, 4, 32], BF16)
            LTt = [P.sb("LTt%d" % i, [128, 4, 32], F32) for i in range(8)]
            wt1 = P.sb("wt1", [128, 2, 128], F32)
            wt2 = P.sb("wt2", [128, 2, 128], F32)
            wt3 = P.sb("wt3", [128, 2, 128], F32)
            wt4 = P.sb("wt4", [128, 2, 128], F32)
            mgc = P.sb("mgc", [128, 2], F32)
            op("pool", lambda e: e.memset(mgc[:, 0:1], 12582912.0), w=["mgc"])
            op("pool", lambda e: e.memset(mgc[:, 1:2], -12582912.0), w=["mgc"])
            gt = [P.sb("gt%d" % i, [128, 512], F32) for i in range(4)]
            xsf = VAbig[:, 0:2].rearrange("p a b c -> p (a b c)")[:, 0:NOWN]
            onet = P.sb("onet", [128, 1], F32)
            op("pool", lambda e: e.memset(onet[:], 1.0), w=["onet"])
            op("pool", lambda e: e.memset(KF[:], 0.0), w=["KF"])
            P.dma("sp", iota[:], iota_d, w=["iota"])

            big = mk("smb", 256, 4)

            def b3v(v):
                return V(v.ap.rearrange("p (a c) -> p a c", c=32), v.key)

            B1, B2, B3, B4 = [b3v(x_) for x_ in big]

            def rev(t_, ri, start, n):
                return bass.AP(t_, ri * NBWD + start, [[2 * NBWD, 128], [-1, n]])

            def revA(i3, ri, start, n):
                return bass.AP(VAbig, (i3 * 2 + ri) * NBWD + start, [[6 * NBWD, 128], [-1, n]])

            def prep_wb(jj, half=None):
                WB = WBs[jj % 2]
                wbk = "WB%d" % (jj % 2)
                bc = bcs[0]
                bck = "bc0"
                if half in (None, 0):
                    P.dma("sp", bc[:], b_c_d[:, jj], w=[bck])
                es = [e_ for e_ in range(8) if half is None or e_ // 4 == half]

                def lt_copies(e_):
                    for ri in range(2):
                        for dl in range(2):
                            li = (e_ % 2) * 4 + ri * 2 + dl
                            qv = Qk[e_][ri]
                            o0 = jj * 8 + dl * 4
                            op("act", lambda e, li=li, qv=qv, o0=o0: e.copy(
                                LTt[li][:], qv.ap[:, o0:o0 + 4].unsqueeze(2).to_broadcast([128, 4, 32])), r=[qv.key], w=["LT%d" % li])

                lt_copies(es[0])
                for e_ in es:
                    if e_ + 1 in es:
                        lt_copies(e_ + 1)
                    pw = e_ % 4
                    for ri in range(2):
                        for dl in range(2):
                            li = (e_ % 2) * 4 + ri * 2 + dl
                            lt = LTt[li]
                            ltk = "LT%d" % li
                            col = (ri * 2 + dl) * 128
                            op("pe", lambda e, lt=lt, col=col, pw=pw: e.matmul(
                                pb[pw][:, col:col + 128], lt[:].rearrange("p a b -> p (a b)"), identf[:], start=True, stop=True),
                               r=[ltk, "identf"], w=[pbk[pw]], sig=(ri == 1 and dl == 1))
                    qre_ps = pb[pw][:, 0:256].rearrange("p (a b) -> p a b", b=128)
                    qim_ps = pb[pw][:, 256:512].rearrange("p (a b) -> p a b", b=128)
                    op("dve", lambda e, qre_ps=qre_ps: e.tensor_tensor(wt1[:], qre_ps, bc[:, 0], op=ALU.mult), r=[pbk[pw], bck], w=["wt1"])
                    op("dve", lambda e, qim_ps=qim_ps: e.tensor_tensor(wt2[:], qim_ps, bc[:, 1], op=ALU.mult), r=[pbk[pw], bck], w=["wt2"])
                    op("dve", lambda e, qre_ps=qre_ps: e.tensor_tensor(wt3[:], qre_ps, bc[:, 1], op=ALU.mult), r=[pbk[pw], bck], w=["wt3"])
                    op("dve", lambda e, qim_ps=qim_ps: e.tensor_tensor(wt4[:], qim_ps, bc[:, 0], op=ALU.mult), r=[pbk[pw], bck], w=["wt4"])
                    op("dve", lambda e, e_=e_: e.tensor_tensor(WB[:, 0, e_, :, :], wt1[:], wt2[:], op=ALU.subtract), r=["wt1", "wt2"], w=[wbk])
                    op("dve", lambda e, e_=e_: e.tensor_tensor(WB[:, 1, e_, :, :], wt3[:], wt4[:], op=ALU.add), r=["wt3", "wt4"], w=[wbk + "i"])

            def prep_cwkf(jj, part=None):
                if part in (None, "a"):
                    P.dma("sp", cs[:], c_s_d[:, jj], w=["cs"])
                    P.dma("sp", bs[:], b_s_d[:, jj], w=["bs"])

                def bc8(v):
                    return V(v.ap[:, jj * 8:(jj + 1) * 8].unsqueeze(2).to_broadcast([128, 8, 32]), v.key)

                csre = V(cs[:, 0].rearrange("p a b c -> p (a b) c"), "cs")
                csim = V(cs[:, 1].rearrange("p a b c -> p (a b) c"), "cs")
                bsre = V(bs[:, 0].rearrange("p a b c -> p (a b) c"), "bs")
                bsim = V(bs[:, 1].rearrange("p a b c -> p (a b) c"), "bs")
                for k in range(1, 9):
                    if part == "b":
                        break
                    pre_, pim_ = Pk[k]
                    ce = "pool" if k % 2 == 0 else "dve"
                    Ba, Bc = (B3, B4) if k % 2 == 0 else (B1, B2)
                    cwk = "CW%d" % (k - 1)
                    tt(ce, Ba, csre, bc8(pre_), ALU.mult)
                    tt(ce, Bc, csim, bc8(pim_), ALU.mult)
                    op(ce, lambda e, k=k, Ba=Ba, Bc=Bc: e.tensor_tensor(CW[:, k - 1, 0].rearrange("p a b c -> p (a b) c"), Ba.ap, Bc.ap,
                                                                       op=ALU.subtract), r=[Ba.key, Bc.key], w=[cwk])
                    tt(ce, Ba, csre, bc8(npim[k]), ALU.mult)
                    tt(ce, Bc, csim, bc8(pre_), ALU.mult)
                    op(ce, lambda e, k=k, Ba=Ba, Bc=Bc: e.tensor_tensor(CW[:, k - 1, 1].rearrange("p a b c -> p (a b) c"), Ba.ap, Bc.ap,
                                                                       op=ALU.subtract), r=[Ba.key, Bc.key], w=[cwk])
                if part == "b":
                    pass
                else:
                  qre, qim = Qk[0]
                  tt("dve", B1, bsre, bc8(qre), ALU.mult)
                  tt("dve", B2, bsim, bc8(qim), ALU.mult)
                  op("dve", lambda e: e.tensor_tensor(Bb[:, 0].rearrange("p a b c -> p (a b) c"), B1.ap, B2.ap, op=ALU.subtract),
                     r=[B1.key, B2.key], w=["Bb"])
                  tt("dve", B1, bsre, bc8(qim), ALU.mult)
                  tt("dve", B2, bsim, bc8(qre), ALU.mult)
                  op("dve", lambda e: e.tensor_tensor(Bb[:, 1].rearrange("p a b c -> p (a b) c"), B1.ap, B2.ap, op=ALU.add),
                     r=[B1.key, B2.key], w=["Bb"])
                  op("dve", lambda e: e.tensor_copy(C0[:, 0].rearrange("p a b c -> p (a b) c"), csre.ap), r=["cs"], w=["C0"])
                  op("dve", lambda e: e.tensor_scalar(C0[:, 1].rearrange("p a b c -> p (a b) c"), csim.ap, -1.0, None, op0=ALU.mult),
                     r=["cs"], w=["C0"])
                if part == "a":
                    return
                for k in range(8):
                    for dl in range(2):
                        for q in range(4):
                            col = (dl * 8 + k) * 32
                            first = (k == 0 and dl == 0)
                            if k == 0:
                                r_re, r_im, rk = C0[:, 0, dl, q, :], C0[:, 1, dl, q, :], "C0"
                            else:
                                r_re, r_im, rk = CW[:, k - 1, 0, dl, q, :], CW[:, k - 1, 1, dl, q, :], "CW%d" % (k - 1)
                            op("pe", lambda e, dl=dl, q=q, col=col, first=first, r_re=r_re: e.matmul(
                                pb[3][32 * q:32 * q + 32, col:col + 32], Bb[:, 0, dl, q, :], r_re,
                                start=first, stop=False, skip_group_check=True, tile_position=(0, 32 * q)),
                               r=["Bb", rk], w=[pbk[3]], sig=False)
                            op("pe", lambda e, dl=dl, q=q, col=col, r_im=r_im: e.matmul(
                                pb[3][32 * q:32 * q + 32, col:col + 32], Bb[:, 1, dl, q, :], r_im,
                                start=False, stop=True, skip_group_check=True, tile_position=(0, 32 * q)),
                               r=["Bb", rk], w=[pbk[3]], sig=True)
                pk4 = pb[3][:].rearrange("p (a k c) -> p a k c", a=2, k=8)
                for q in range(4):
                    op("act", lambda e, q=q: e.copy(KF[32 * q:32 * q + 32, :, :, 32 * q:32 * q + 32], pk4[32 * q:32 * q + 32]),
                       r=[pbk[3]], w=["KF"])
                op("act", lambda e: e.activation(Dd[:], identf[:], AF.Identity, scale=pv[:, PV_SSMD + jj:PV_SSMD + jj + 1]),
                   r=["identf", "pv"], w=["Dd"])

            def make_emitters(jj):
                WB = WBs[jj % 2]
                wbk = "WB%d" % (jj % 2)

                tiles = [(dl, q) for dl in range(2) for q in range(4)]

                def emit_v(i):
                    dl, q = tiles[i]
                    base = 0 if dl == 0 else 16
                    cbs = [(0, NFWD)] if dl == 0 else [(0, 257), (257, 257)]
                    A_ = VAb[(i + 2) % 3]
                    ak = "VAb%d" % ((i + 2) % 3)
                    pc = 0
                    for ri in range(2):
                        for (c0, n) in cbs:
                            pi_ = (i * 4 + pc) % 4
                            pc += 1
                            for s_ in range(8):
                                sig_ = (7 - s_) if dl == 0 else s_
                                st_ = base + s_ + 8 * c0
                                op("pe", lambda e, q=q, ri=ri, sig_=sig_, st_=st_, n=n, s_=s_, pi_=pi_, dl=dl: e.matmul(
                                    pb[pi_][:, :n], WB[32 * q:32 * q + 32, ri, sig_, dl, :],
                                    u_s[32 * q:32 * q + 32, jj, s_, base // 8 + c0:base // 8 + c0 + n],
                                    start=(s_ == 0), stop=(s_ == 7), tile_position=(32 * q, 0)),
                                   r=[wbk, wbk + "i", "us%d" % jj], w=[pbk[pi_]], sig=(s_ == 7))
                            op("act", lambda e, ri=ri, c0=c0, n=n, pi_=pi_, A_=A_: e.copy(A_[:, ri, c0:c0 + n], pb[pi_][:, :n]),
                               r=[pbk[pi_]], w=[ak])

                def emit_table(i):
                    dl, q = tiles[i]
                    N = NFWD if dl == 0 else NBWD
                    T_ = TBb[(i + 2) % 3]
                    tk = "TBb%d" % ((i + 2) % 3)
                    o0 = jj * 8 + dl * 4 + q
                    tq, tr = tbt
                    MAGIC = 12582912.0
                    op("act", lambda e: e.activation(tq[:, 0:N], iota[:, 0:N], AF.Identity, scale=sturn.ap[:, o0:o0 + 1]),
                       r=["iota", sturn.key], w=["tbt0"])
                    op("act", lambda e: e.activation(tr[:, 0:N], tq[:, 0:N], AF.Identity, bias=mgc[:, 0:1]), r=["tbt0", "mgc"], w=["tbt1"])
                    op("act", lambda e: e.activation(tr[:, 0:N], tr[:, 0:N], AF.Identity, bias=mgc[:, 1:2]), r=["tbt1", "mgc"], w=["tbt1"])
                    op("dve", lambda e: e.tensor_tensor(tq[:, 0:N], tq[:, 0:N], tr[:, 0:N], op=ALU.subtract), r=["tbt0", "tbt1"], w=["tbt0"])
                    op("act", lambda e: e.activation(T_[:, 1, 0:N], tq[:, 0:N], AF.Sin, scale=TWOPI_S), r=["tbt0"], w=[tk])
                    op("act", lambda e: e.activation(tr[:, 0:N], tq[:, 0:N], AF.Sin, scale=TWOPI_S / 2.0), r=["tbt0"], w=["tbt1"])
                    op("act", lambda e: e.activation(tr[:, 0:N], tr[:, 0:N], AF.Square), r=["tbt1"], w=["tbt1"])
                    op("act", lambda e: e.activation(T_[:, 0, 0:N], tr[:, 0:N], AF.Identity, scale=-2.0, bias=onet[:, 0:1]),
                       r=["tbt1", "onet"], w=[tk])

                def emit_demod(i):
                    dl, q = tiles[i]
                    N = NFWD if dl == 0 else NBWD
                    A_, T_, D_ = VAb[(i + 2) % 3], TBb[(i + 2) % 3], DMb[i % 2]
                    ak, tk, dk = "VAb%d" % ((i + 2) % 3), "TBb%d" % ((i + 2) % 3), "DMb%d" % (i % 2)
                    if dl == 0:
                        wre, wim = A_[:, 0, 0:N], A_[:, 1, 0:N]
                    else:
                        wre, wim = revA((i + 2) % 3, 0, N - 1, N), revA((i + 2) % 3, 1, N - 1, N)
                    op("pool", lambda e: e.tensor_tensor(pt1[:, 0:N], T_[:, 0, 0:N], wre, op=ALU.mult), r=[tk, ak], w=["pt1"])
                    op("pool", lambda e: e.tensor_tensor(pt2[:, 0:N], T_[:, 1, 0:N], wim, op=ALU.mult), r=[tk, ak], w=["pt2"])
                    op("pool", lambda e: e.tensor_tensor(pt5[:, 0:N], T_[:, 0, 0:N], wim, op=ALU.mult), r=[tk, ak], w=["pt5"])
                    op("pool", lambda e: e.tensor_tensor(pt6[:, 0:N], T_[:, 1, 0:N], wre, op=ALU.mult), r=[tk, ak], w=["pt6"])
                    op("pool", lambda e: e.tensor_tensor(D_[:, 0, 0:N], pt1[:, 0:N], pt2[:, 0:N], op=ALU.add), r=["pt1", "pt2"], w=[dk])
                    op("pool", lambda e: e.tensor_tensor(D_[:, 1, 0:N], pt5[:, 0:N], pt6[:, 0:N], op=ALU.subtract), r=["pt5", "pt6"], w=[dk + "i"])

                def emit_scan(i):
                    dl, q = tiles[i]
                    N = NFWD if dl == 0 else NBWD
                    A_, D_ = VAb[(i + 2) % 3], DMb[i % 2]
                    ak, dk = "VAb%d" % ((i + 2) % 3), "DMb%d" % (i % 2)
                    o0 = jj * 8 + dl * 4 + q
                    rho = rho8.ap[:, o0:o0 + 1].to_broadcast([128, N])
                    for ri in range(2):
                        op("dve", lambda e, ri=ri: e.tensor_tensor_scan(A_[:, ri, 0:N], rho, D_[:, ri, 0:N], 0.0, op0=ALU.mult, op1=ALU.add),
                           r=[dk, dk + "i", rho8.key], w=[ak])

                def emit_remod(i):
                    dl, q = tiles[i]
                    N = NFWD if dl == 0 else NBWD
                    A_, T_ = VAb[(i + 2) % 3], TBb[(i + 2) % 3]
                    ak, tk = "VAb%d" % ((i + 2) % 3), "TBb%d" % ((i + 2) % 3)
                    n = NFWD
                    if dl == 0:
                        tre, tim, xre, xim = T_[:, 0, 0:n], T_[:, 1, 0:n], A_[:, 0, 0:n], A_[:, 1, 0:n]
                    else:
                        tre, tim = rev(T_, 0, N - 1, n), rev(T_, 1, N - 1, n)
                        xre, xim = revA((i + 2) % 3, 0, N - 1, n), revA((i + 2) % 3, 1, N - 1, n)
                    sk = "Sb%d_%d" % (dl, q)
                    op("pool", lambda e: e.tensor_tensor(pt1[:, 0:n], tre, xre, op=ALU.mult), r=[tk, ak], w=["pt1"])
                    op("pool", lambda e: e.tensor_tensor(pt2[:, 0:n], tim, xim, op=ALU.mult), r=[tk, ak], w=["pt2"])
                    op("pool", lambda e: e.tensor_tensor(Sb[:, dl, q, 0, :], pt1[:, 0:n], pt2[:, 0:n], op=ALU.subtract), r=["pt1", "pt2"], w=[sk + "r"])
                    op("dve", lambda e: e.tensor_tensor(pt3[:, 0:n], tre, xim, op=ALU.mult), r=[tk, ak], w=["pt3"])
                    op("dve", lambda e: e.tensor_tensor(pt4[:, 0:n], tim, xre, op=ALU.mult), r=[tk, ak], w=["pt4"])
                    op("dve", lambda e: e.tensor_tensor(Sb[:, dl, q, 1, :], pt3[:, 0:n], pt4[:, 0:n], op=ALU.add), r=["pt3", "pt4"], w=[sk + "i"])

                return emit_v, emit_table, emit_demod, emit_scan, emit_remod

            def output_phase(jj):
                sbkeys = ["Sb%d_%d%s" % (dl, q, x_) for dl in range(2) for q in range(4) for x_ in ("r", "i")]

                def ycol(t):
                    return pb[4 + t // 2], pbk[4 + t // 2], (t % 2) * 256

                def ucol(s_):
                    return u_s[:, jj, s_, 2:258]

                for t in range(8):
                    pt_, pk_, c0 = ycol(t)
                    op("pe", lambda e, pt_=pt_, c0=c0, t=t: e.matmul(pt_[:, c0:c0 + 256], Dd[:], ucol(t), start=(t % 2 == 0), stop=False,
                                                                    skip_group_check=True),
                       r=["Dd", "us%d" % jj], w=[pk_], sig=False)
                for dl in range(2):
                    for t in range(8):
                        pt_, pk_, c0 = ycol(t)
                        srange = range(0, t + 1) if dl == 0 else range(t, 8)
                        for s_ in srange:
                            k = abs(t - s_)
                            op("pe", lambda e, dl=dl, s_=s_, k=k, pt_=pt_, c0=c0: e.matmul(
                                pt_[:, c0:c0 + 256], KF[:, dl, k, :], ucol(s_),
                                start=False, stop=False, skip_group_check=True),
                               r=["KF", "us%d" % jj], w=[pk_], sig=False)
                for t in range(8):
                    pt_, pk_, c0 = ycol(t)
                    cnt = 0
                    for dl in range(2):
                        eidx = t if dl == 0 else 7 - t
                        for q in range(4):
                            for ri in range(2):
                                cnt += 1
                                last = (cnt == 16)
                                op("pe", lambda e, dl=dl, q=q, ri=ri, eidx=eidx, pt_=pt_, c0=c0, last=last: e.matmul(
                                    pt_[32 * q:32 * q + 32, c0:c0 + 256], CW[:, eidx, ri, dl, q, :],
                                    Sb[:, dl, q, ri, 1:257],
                                    start=False, stop=last, skip_group_check=True, tile_position=(0, 32 * q)),
                                   r=["CW%d" % eidx] + sbkeys, w=[pk_], sig=last)
                xview = xsf[:].rearrange("p (c t) -> p t c", t=8)
                for b_ in range(4):
                    op("act", lambda e, b_=b_: e.copy(xview[:, 2 * b_:2 * b_ + 2, :], pb[4 + b_][:].rearrange("p (t c) -> p t c", t=2)),
                       r=[pbk[4 + b_]], w=["xsf", "VAb0", "VAb1"])
                for bp in range(2):
                    bs_ = [2 * bp, 2 * bp + 1]
                    G1 = {b_: gt[b_ % 2] for b_ in bs_}
                    G2 = {b_: gt[2 + b_ % 2] for b_ in bs_}
                    K1 = {b_: "gt%d" % (b_ % 2) for b_ in bs_}
                    K2 = {b_: "gt%d" % (2 + b_ % 2) for b_ in bs_}
                    XS = {b_: xsf[:, b_ * 512:(b_ + 1) * 512] for b_ in bs_}
                    xk_ = ["xsf", "VAb0", "VAb1"]
                    for b_ in bs_:
                        op("act", lambda e, b_=b_: e.activation(G1[b_][:], XS[b_], AF.Square), r=xk_, w=[K1[b_]])
                    for b_ in bs_:
                        op("act", lambda e, b_=b_: e.activation(G1[b_][:], G1[b_][:], AF.Identity, scale=0.044715, bias=onet[:, 0:1]),
                           r=[K1[b_], "onet"], w=[K1[b_]])
                    for b_ in bs_:
                        op("dve", lambda e, b_=b_: e.tensor_tensor(G1[b_][:], G1[b_][:], XS[b_], op=ALU.mult), r=[K1[b_]] + xk_, w=[K1[b_]])
                    for b_ in bs_:
                        op("act", lambda e, b_=b_: e.activation(G2[b_][:], G1[b_][:], AF.Sigmoid, scale=1.5957691216057308), r=[K1[b_]], w=[K2[b_]])
                    for b_ in bs_:
                        op("dve", lambda e, b_=b_: e.tensor_tensor(mixed[:, 4 + jj, b_ * 512:(b_ + 1) * 512], XS[b_], G2[b_][:], op=ALU.mult),
                           r=xk_ + [K2[b_]], w=["Y%d_b%d" % (4 + jj, b_)])
            ems = [make_emitters(jj) for jj in range(4)]
            prep_wb(0)
            ems[0][0](0)
            ems[0][1](0)
            for jj in range(4):
                emit_v, emit_table, emit_demod, emit_scan, emit_remod = ems[jj]
                for i in range(9):
                    if i + 1 < 8:
                        emit_v(i + 1)
                    if i < 8:
                        emit_demod(i)
                    if i >= 1:
                        emit_remod(i - 1)
                    if i + 1 < 8:
                        emit_table(i + 1)
                    if i < 8:
                        emit_scan(i)
                    if i == 1:
                        prep_cwkf(jj, "a")
                    if i == 6:
                        prep_cwkf(jj, "b")
                    if i == 3 and jj < 3:
                        prep_wb(jj + 1, 0)
                    if i == 5 and jj < 3:
                        prep_wb(jj + 1, 1)
                if jj < 3:
                    ems[jj + 1][0](0)
                    ems[jj + 1][1](0)
                output_phase(jj)
            if stage == 4:
                dump("d_ygb", mixed[:, 4:8, :], [128, 4, NOWN], BF16, ["Y%d_b%d" % (c, b_) for c in range(4, 8) for b_ in range(4)])
                P.emit()
                return nc
            gluw = P.sb("gluw", [128, 4, 512], BF16)
            P.dma("pool", gluw[:], glu_w_d.rearrange("(k p) n -> p k n", p=128), w=["gluw"])
            for blk in range(4):
                for m in range(4):
                    for k in range(4):
                        op("pe", lambda e, m=m, k=k: e.matmul(pb[m][:, :], gluw[:, k, m * 128:(m + 1) * 128],
                                                             mixed[:, 4 + k, blk * 512:(blk + 1) * 512], start=(k == 0), stop=(k == 3)),
                           r=["gluw"] + ["Y%d_b%d" % (c, blk) for c in range(4, 8)], w=[pbk[m]], sig=(k == 3))
                    op("act", lambda e, m=m: e.activation(gt[m][:], pb[m][:, :], AF.Sigmoid, bias=pv[:, PV_GLUB + m:PV_GLUB + m + 1]),
                       r=[pbk[m], "pv"], w=["gt%d" % m])
                for m in range(4):
                    op("dve", lambda e, m=m: e.tensor_tensor(mixed[:, 4 + m, blk * 512:(blk + 1) * 512],
                                                            mixed[:, 4 + m, blk * 512:(blk + 1) * 512], gt[m][:], op=ALU.mult),
                       r=["gt%d" % m, "Y%d_b%d" % (4 + m, blk)], w=["Y%d_b%d" % (4 + m, blk)])
    if stage == 5:
        dump("d_mixed", mixed[:], [128, 8, NOWN], BF16, ["mixed%d" % c for c in range(4)] + ["Y%d_b%d" % (c, b_) for c in range(4, 8) for b_ in range(4)])
        P.emit()
        return nc

    h = P.sb("h", [128, 16, D], F32)
    zffn = P.sb("zffn", [128, 8, NOWN], BF16)
    ssb = [P.sb("ssb%d" % i, [128, 4], F32) for i in range(2)]
    sq2 = P.sb("sq2", [128, D], F32)
    zt2 = [P.sb("zt2_%d" % i, [128, D], BF16) for i in range(2)]
    w1c = [P.sb("w1c%d" % i, [128, 8, 512], BF16) for i in range(2)]
    w2c = [P.sb("w2c%d" % i, [128, 4, D], BF16) for i in range(2)]
    w1v = w_ff1_d.rearrange("(k p) n -> p k n", p=128)
    w2v = w_ff2_d.rearrange("(c f p) n -> c p f n", f=4, p=128)
    if True:
        P.dma("pool", w1c[0][:], w1v[:, :, 0:512], w=["w1c0"])
        P.dma("pool", w2c[0][:], w2v[0], w=["w2c0"])
        OPB = [(0, 1), (2, 3), (6, 7)]

        def outproj(tt_):
            hk = "h%d" % tt_
            P.dma("sp", h[:, tt_, :], xl[16 + tt_ * 128:16 + (tt_ + 1) * 128, :], w=[hk])
            for half in range(2):
                pi_ = OPB[tt_ % 3][half]
                for k in range(8):
                    op("pe", lambda e, k=k, half=half: e.matmul(pb[pi_][:, :], mixed[:, k, tt_ * 128:(tt_ + 1) * 128],
                                                               w_out_bf[:, k, half * 512:(half + 1) * 512],
                                                               start=(k == 0), stop=(k == 7)),
                       r=["wout"] + ["mixed%d" % c for c in range(4)] + ["Y%d_b%d" % (c, tt_ // 4) for c in range(4, 8)],
                       w=[pbk[pi_]], sig=(k == 7))

        def resid_add(tt_):
            hk = "h%d" % tt_
            for half in range(2):
                pi_ = OPB[tt_ % 3][half]
                op("dve", lambda e, half=half: e.tensor_tensor(h[:, tt_, half * 512:(half + 1) * 512],
                                                               h[:, tt_, half * 512:(half + 1) * 512], pb[pi_][:, :], op=ALU.add),
                   r=[hk, pbk[pi_]], w=[hk])

        def norm_chain(tt_):
            b = tt_ % 2
            hk = "h%d" % tt_
            rms_tile(h[:, tt_, :], 128, hk, ssb[b], "ssb%d" % b, sq2, zt2[b][:, :], "zt2_%d" % b, "pool")

        def tr_ffn(tt_):
            b = tt_ % 2
            transpose_tile(zt2[b], "zt2_%d" % b, 128, 4 + tt_ % 2, PV_GFFN, zffn[:, :, tt_ * 128:(tt_ + 1) * 128], "zffn%d" % tt_)

        outproj(0)
        outproj(1)
        resid_add(0)
        for tt_ in range(16):
            norm_chain(tt_)
            if tt_ + 1 < 16:
                resid_add(tt_ + 1)
            if tt_ + 2 < 16:
                outproj(tt_ + 2)
            tr_ffn(tt_)
    if stage == 6:
        dump("d_h", h[:], [128, 16, D], F32, ["h%d" % i for i in range(16)])
        dump("d_zffn", zffn[:], [128, 8, NOWN], BF16, ["zffn%d" % i for i in range(16)])
        P.emit()
        return nc

    with P.scope():
        hid = [P.sb("hid%d" % i, [128, 4, 512], BF16) for i in range(2)]
        rl = [P.sb("rl%d" % i, [128, 512], F32) for i in range(2)]
        gfin = P.sb("gfin", [128, D], F32)
        P.dma("sp", gfin[:], gfin_d.partition_broadcast(128), w=["gfin"])
        def load_w(fc):
            wb_ = fc % 2
            P.dma("pool", w1c[wb_][:], w1v[:, :, fc * 512:(fc + 1) * 512], w=["w1c%d" % wb_])
            P.dma("pool", w2c[wb_][:], w2v[fc], w=["w2c%d" % wb_])

        cnts = {"h": 0, "o": 0}

        def ffn_hidden(step):
            fc, blk = step // 4, step % 4
            wb_ = fc % 2
            hb = step % 2
            for f in range(4):
                pi_ = cnts["h"] % 4
                rb = cnts["h"] % 2
                cnts["h"] += 1
                for k in range(8):
                    op("pe", lambda e, k=k, f=f: e.matmul(pb[pi_][:, :], w1c[wb_][:, k, f * 128:(f + 1) * 128],
                                                         zffn[:, k, blk * 512:(blk + 1) * 512], start=(k == 0), stop=(k == 7)),
                       r=["w1c%d" % wb_] + ["zffn%d" % (blk * 4 + i) for i in range(4)], w=[pbk[pi_]], sig=(k == 7))
                op("act", lambda e: e.activation(rl[rb][:], pb[pi_][:, :], AF.Relu), r=[pbk[pi_]], w=["rl%d" % rb])
                op("pool", lambda e, f=f: e.tensor_tensor(hid[hb][:, f, :], rl[rb][:], rl[rb][:], op=ALU.mult),
                   r=["rl%d" % rb], w=["hid%d_%d" % (hb, f)])

        def ffn_out(step):
            fc, blk = step // 4, step % 4
            wb_ = fc % 2
            hb = step % 2
            for ti in range(4):
                tt_ = blk * 4 + ti
                hk = "h%d" % tt_
                for half in range(2):
                    pi_ = 4 + cnts["o"] % 4
                    cnts["o"] += 1
                    for f in range(4):
                        op("pe", lambda e, f=f, half=half: e.matmul(pb[pi_][:, :], hid[hb][:, f, ti * 128:(ti + 1) * 128],
                                                                   w2c[wb_][:, f, half * 512:(half + 1) * 512],
                                                                   start=(f == 0), stop=(f == 3)),
                           r=["w2c%d" % wb_] + ["hid%d_%d" % (hb, i) for i in range(4)], w=[pbk[pi_]], sig=(f == 3))
                    op("dve", lambda e, half=half: e.tensor_tensor(h[:, tt_, half * 512:(half + 1) * 512],
                                                                   h[:, tt_, half * 512:(half + 1) * 512], pb[pi_][:, :], op=ALU.add),
                       r=[hk, pbk[pi_]], w=[hk])

        def final_norm(tt_):
            b = tt_ % 2
            hk = "h%d" % tt_
            ss = ssb[b]
            sk = "ssb%d" % b
            op("act", lambda e: e.activation(sq2[:, :], h[:, tt_, :], AF.Square, accum_out=ss[:, 0:1]), r=[hk], w=["sq2", sk])
            op("act", lambda e: e.activation(ss[:, 1:2], ss[:, 0:1], AF.Sqrt, bias=epst[:, 0:1], scale=1.0 / D), r=[sk, "epst"], w=[sk])
            op("dve", lambda e: e.reciprocal(ss[:, 2:3], ss[:, 1:2]), r=[sk], w=[sk])
            op("dve", lambda e: e.scalar_tensor_tensor(h[:, tt_, :], h[:, tt_, :], ss[:, 2:3], gfin[:], op0=ALU.mult, op1=ALU.mult),
               r=[hk, sk, "gfin"], w=[hk])
            P.dma("sp", out_d[tt_ * 128:(tt_ + 1) * 128, :], h[:, tt_, :], r=[hk], w=["out%d" % tt_], final=True)

        ffn_hidden(0)
        for step in range(32):
            if step % 4 == 0 and step // 4 + 1 < 8:
                load_w(step // 4 + 1)
            if step + 1 < 32:
                ffn_hidden(step + 1)
            ffn_out(step)
            if step >= 28:
                for ti in range(4):
                    final_norm((step - 28) * 4 + ti)
    P.emit()
    return nc


def _core_inputs(inp, b, half):
    f32 = np.float32
    x = inp["x"][b]
    meta = inp["meta_tokens"]
    z16 = np.zeros((16, D), f32)
    if half == 0:
        xl = np.concatenate([meta, x, z16], axis=0)
        dirs = [0, 1]
        convw = inp["conv_w"][0]
    else:
        full = np.concatenate([meta, x], axis=0)[::-1]
        xl = np.concatenate([z16, full], axis=0)
        dirs = [1, 0]
        convw = inp["conv_w"][0][::-1]
    xl = np.ascontiguousarray(xl, dtype=f32)

    def col8(v):
        return np.ascontiguousarray(v.reshape(8, 128).T)

    def col4(v):
        return np.ascontiguousarray(v.reshape(4, 128).T)

    pv = np.concatenate([col8(inp["norm_mix_g"][0]), col8(inp["norm_ffn_g"][0]), col4(inp["conv_b"][0]),
                         col4(inp["conv_ln_g"][0]), col4(inp["conv_ln_b"][0]), col4(inp["ssm_d"][0]),
                         col4(inp["ssm_glu_b"][0])], axis=1).astype(f32)
    cw = np.ascontiguousarray(convw.T.reshape(4, 128, 31).transpose(1, 0, 2)).astype(f32)

    lam = [inp["ssm_lam_re"][0][dirs], inp["ssm_lam_im"][0][dirs]]
    ldt = np.broadcast_to(inp["ssm_log_dt"][0][dirs][:, :, None], (2, 32, 64))
    lam3 = np.stack([lam[0], lam[1], ldt], axis=0)
    bri = np.stack([inp["ssm_b_re"][0][dirs], inp["ssm_b_im"][0][dirs]], axis=0)
    cri = np.stack([inp["ssm_c_re"][0][dirs], inp["ssm_c_im"][0][dirs]], axis=0)

    l5 = lam3.reshape(3, 2, 4, 8, 64)
    lam_c = np.broadcast_to(l5.transpose(3, 2, 0, 1, 4)[:, None, :, :, :, None, :],
                            (8, 16, 4, 3, 2, 2, 64)).reshape(128, 4, 3, 2, 128)
    b6 = bri.reshape(2, 2, 4, 8, 64, 16)
    b_c = np.zeros((8, 16, 4, 2, 2, 2, 64), f32)
    for gl in range(8):
        b_c[gl, :, :, :, :, gl % 2, :] = b6[:, :, :, gl, :, :].transpose(4, 2, 0, 1, 3)
    b_c = b_c.reshape(128, 4, 2, 2, 128)
    l6 = lam3.reshape(3, 2, 4, 4, 2, 64)
    lam_s = np.ascontiguousarray(l6.transpose(4, 5, 0, 2, 1, 3)).reshape(128, 3, 4, 2, 4)
    c7 = cri.reshape(2, 2, 4, 4, 2, 16, 64)
    c_s = np.zeros((2, 64, 4, 2, 2, 4, 2, 16), f32)
    b7 = bri.reshape(2, 2, 4, 4, 2, 64, 16)
    b_s = np.zeros((2, 64, 4, 2, 2, 4, 2, 16), f32)
    for gp in range(2):
        c_s[gp, :, :, :, :, :, gp, :] = c7[:, :, :, :, gp, :, :].transpose(5, 2, 0, 1, 3, 4)
        b_s[gp, :, :, :, :, :, gp, :] = b7[:, :, :, :, gp, :, :].transpose(4, 2, 0, 1, 3, 5)
    c_s = c_s.reshape(128, 4, 2, 2, 4, 32)
    b_s = b_s.reshape(128, 4, 2, 2, 4, 32)
    return {
        "xl": xl,
        "w_in": np.ascontiguousarray(inp["w_in"][0], dtype=f32),
        "w_out": np.ascontiguousarray(inp["w_out"][0], dtype=f32),
        "w_ff1": np.ascontiguousarray(inp["w_ff1"][0], dtype=f32),
        "w_ff2": np.ascontiguousarray(inp["w_ff2"][0], dtype=f32),
        "glu_w": np.ascontiguousarray(inp["ssm_glu_w"][0], dtype=f32),
        "pv": np.ascontiguousarray(pv),
        "cw": cw,
        "gfin": np.ascontiguousarray(inp["norm_final_g"], dtype=f32),
        "ident": np.eye(128, dtype=f32),
        "iota": np.ascontiguousarray(np.broadcast_to(np.arange(NBWD, dtype=f32)[None, :], (128, NBWD))),
        "lam_c": np.ascontiguousarray(lam_c, dtype=f32),
        "b_c": np.ascontiguousarray(b_c, dtype=f32),
        "lam_s": np.ascontiguousarray(lam_s, dtype=f32),
        "c_s": np.ascontiguousarray(c_s, dtype=f32),
        "b_s": np.ascontiguousarray(b_s, dtype=f32),
    }


_NC_CACHE = {}


def kernel(**inputs):
    inp = {k: np.asarray(v) for k, v in inputs.items()}
    if "nc" not in _NC_CACHE:
        _NC_CACHE["nc"] = _build()
    nc = _NC_CACHE["nc"]
    in_maps = [_core_inputs(inp, c // 2, c % 2) for c in range(8)]
    res = run_bass_kernel_spmd(nc, in_maps, core_ids=list(range(8)))
    out = np.empty((4, 4096, D), np.float32)
    for c in range(8):
        o = np.asarray(res.results[c]["out"])
        if c % 2 == 0:
            out[c // 2, 0:2048] = o
        else:
            out[c // 2, 2048:4096] = o[::-1]
    return out
```

```python
import numpy as np
from contextlib import ExitStack, contextmanager
import concourse.bass as bass
import concourse.mybir as mybir
from concourse.bass_utils import run_bass_kernel_spmd

F32 = mybir.dt.float32
BF16 = mybir.dt.bfloat16
AF = mybir.ActivationFunctionType
ALU = mybir.AluOpType


class _Op:
    __slots__ = ("eng", "seq", "sigval", "dma", "sem", "target")


class Prog:
    def __init__(self, nc, same_eng_sync=True, n_dma_sems=32):
        self.nc = nc
        self.same = same_eng_sync
        self.root = ExitStack()
        self.stacks = [self.root]
        self.eng = {"pe": nc.tensor, "act": nc.scalar, "dve": nc.vector, "pool": nc.gpsimd, "sp": nc.sync}
        self.esem = {k: self.root.enter_context(nc.semaphore("es_" + k)) for k in self.eng}
        self.ecnt = {k: 0 for k in self.eng}
        self.eseq = {k: 0 for k in self.eng}
        self.esigs = {k: [] for k in self.eng}
        nd = n_dma_sems // 2
        self.dpool = {"sp": list(range(0, nd)), "act": list(range(0, nd)), "pool": list(range(nd, 2 * nd))}
        self.dsems = [self.root.enter_context(nc.semaphore("ds%d" % i)) for i in range(2 * nd)]
        self.dcum = [0] * (2 * nd)
        self.dlast = [None] * (2 * nd)
        self.dnext = {"sp": 0, "act": 0, "pool": 0}
        self.waited = {k: {} for k in self.eng}
        self.last_w = {}
        self.readers = {}
        self.finals = []
        self.nops = 0

    def sb(self, name, shape, dt):
        return self.stacks[-1].enter_context(self.nc.sbuf_tensor("s_" + name, shape, dt))

    def ps(self, name, shape, dt):
        return self.stacks[-1].enter_context(self.nc.psum_tensor("p_" + name, shape, dt))

    @contextmanager
    def scope(self):
        st = ExitStack()
        self.stacks.append(st)
        try:
            yield
        finally:
            self.fence()
            self.stacks.pop()
            st.close()

    def fence(self):
        for e in self.eng:
            for e2 in self.eng:
                if self.ecnt[e2] > 0:
                    self._wait(e, self.esem[e2], self.ecnt[e2])
            for o in self.dlast:
                if o is not None:
                    self._wait(e, o.sem, o.target)

    def _wait(self, eng, sem, val):
        d = self.waited[eng]
        if d.get(sem.name, -1) >= val:
            return
        d[sem.name] = val
        self.eng[eng].wait_ge(sem, val)

    def _wait_op(self, eng, dep):
        if dep.dma:
            self._wait(eng, dep.sem, dep.target)
            return
        if dep.eng == eng and (eng == "pe" or not self.same):
            return
        sv = dep.sigval
        if sv is None:
            for (s, v) in self.esigs[dep.eng]:
                if s >= dep.seq:
                    sv = v
                    break
            if sv is None:
                raise RuntimeError("dependency on non-signalling op with no later signal on " + dep.eng)
            dep.sigval = sv
        self._wait(eng, self.esem[dep.eng], sv)

    def _deps(self, r, w):
        deps = []
        for k in r:
            o = self.last_w.get(k)
            if o is not None:
                deps.append(o)
        for k in w:
            o = self.last_w.get(k)
            if o is not None:
                deps.append(o)
            deps.extend(self.readers.get(k, ()))
        return deps

    def _record(self, op, r, w):
        for k in r:
            self.readers.setdefault(k, []).append(op)
        for k in w:
            self.last_w[k] = op
            self.readers[k] = []

    def op(self, eng, fn, r=(), w=(), sig=True):
        deps = self._deps(r, w)
        o = _Op()
        o.eng = eng
        o.dma = False
        o.sem = None
        o.target = None
        for d in deps:
            self._wait_op(eng, d)
        ins = fn(self.eng[eng])
        self.eseq[eng] += 1
        o.seq = self.eseq[eng]
        if sig:
            self.ecnt[eng] += 1
            o.sigval = self.ecnt[eng]
            ins.then_inc(self.esem[eng], 1)
            self.esigs[eng].append((o.seq, o.sigval))
            if len(self.esigs[eng]) > 4096:
                del self.esigs[eng][:2048]
        else:
            o.sigval = None
        self._record(o, r, w)
        self.nops += 1
        return o

    def dma(self, eng, out, in_, r=(), w=(), final=False, **kw):
        deps = self._deps(r, w)
        pool_ = self.dpool[eng]
        key_ = "pool" if eng == "pool" else "sp"
        i = pool_[self.dnext[key_] % len(pool_)]
        self.dnext[key_] += 1
        if self.dlast[i] is not None:
            deps.append(self.dlast[i])
        for d in deps:
            self._wait_op(eng, d)
        o = _Op()
        o.eng = eng
        o.dma = True
        o.sem = self.dsems[i]
        self.dcum[i] += 16
        o.target = self.dcum[i]
        o.seq = None
        o.sigval = None
        self.eng[eng].dma_start(out=out, in_=in_, **kw).then_inc(o.sem, 16)
        self.dlast[i] = o
        self._record(o, r, w)
        if final:
            self.finals.append(o)
        self.nops += 1
        return o

    def emit(self):
        for o in self.finals:
            self._wait_op("sp", o)
        self.stacks[-1].close()


D = 1024
NOWN = 2048
LT = 4128
NFWD = 258
NBWD = 514
TWOPI_S = 6.283179
PV_GMIX, PV_GFFN, PV_CONVB, PV_LNG, PV_LNB, PV_SSMD, PV_GLUB = 0, 8, 16, 20, 24, 28, 32


class V:
    __slots__ = ("ap", "key")

    def __init__(self, ap, key):
        self.ap = ap
        self.key = key


def _build(stage=99):
    nc = bass.Bass("TRN2", target_bir_lowering=False)

    def din(name, shape):
        return nc.dram_tensor(name, list(shape), F32, kind="ExternalInput").ap()

    def dout(name, shape, dt=F32):
        return nc.dram_tensor(name, list(shape), dt, kind="ExternalOutput").ap()

    xl = din("xl", [LT, D])
    w_in_d = din("w_in", [D, 1536])
    w_out_d = din("w_out", [D, D])
    w_ff1_d = din("w_ff1", [D, 4096])
    w_ff2_d = din("w_ff2", [4096, D])
    glu_w_d = din("glu_w", [512, 512])
    pv_d = din("pv", [128, 36])
    cw_d = din("cw", [128, 4, 31])
    gfin_d = din("gfin", [D])
    ident_d = din("ident", [128, 128])
    lam_c_d = din("lam_c", [128, 4, 3, 2, 128])
    b_c_d = din("b_c", [128, 4, 2, 2, 128])
    lam_s_d = din("lam_s", [128, 3, 4, 2, 4])
    c_s_d = din("c_s", [128, 4, 2, 2, 4, 32])
    b_s_d = din("b_s", [128, 4, 2, 2, 4, 32])
    iota_d = din("iota", [128, NBWD])
    out_d = dout("out", [NOWN, D])

    P = Prog(nc)
    op = P.op

    ident = P.sb("ident", [128, 128], BF16)
    identf = P.sb("identf", [128, 128], F32)
    pv = P.sb("pv", [128, 36], F32)
    epst = P.sb("epst", [128, 1], F32)
    hpi = P.sb("hpi", [128, 1], F32)
    onesf = P.sb("onesf", [128, 128], F32)
    pb = [P.ps("pb%d" % i, [128, 512], F32) for i in range(8)]
    pbk = ["pb%d" % i for i in range(8)]
    P.dma("pool", ident[:], ident_d, w=["ident"])
    P.dma("sp", identf[:], ident_d, w=["identf"])
    P.dma("sp", pv[:], pv_d, w=["pv"])
    op("pool", lambda e: e.memset(epst[:], 1e-5), w=["epst"])
    op("pool", lambda e: e.memset(hpi[:], float(np.pi / 2)), w=["hpi"])
    op("pool", lambda e: e.memset(onesf[:], 1.0 / 512.0), w=["onesf"])

    mixed = P.sb("mixed", [128, 8, NOWN], BF16)
    w_out_bf = P.sb("w_out_bf", [128, 8, D], BF16)

    def rms_tile(xt_ap, nr, xkey, ss, sskey, sq, zt_ap, ztkey, eng_scale):
        op("act", lambda e: e.activation(sq[:nr, :], xt_ap, AF.Square, accum_out=ss[:nr, 0:1]),
           r=[xkey], w=["sq", sskey])
        op("act", lambda e: e.activation(ss[:nr, 1:2], ss[:nr, 0:1], AF.Sqrt, bias=epst[:nr, 0:1], scale=1.0 / D),
           r=[sskey, "epst"], w=[sskey])
        op("dve", lambda e: e.reciprocal(ss[:nr, 2:3], ss[:nr, 1:2]), r=[sskey], w=[sskey])
        if eng_scale == "dve":
            op("dve", lambda e: e.tensor_scalar(zt_ap, xt_ap, ss[:nr, 2:3], None, op0=ALU.mult), r=[xkey, sskey], w=[ztkey])
        else:
            op("act", lambda e: e.activation(zt_ap, xt_ap, AF.Identity, scale=ss[:nr, 2:3]), r=[xkey, sskey], w=[ztkey])

    def transpose_tile(zt_t, ztkey, nr, pbi, gcol, dst_ap3, dstkey):
        pbv = pb[pbi][:].bitcast(BF16)
        for k in range(8):
            op("pe", lambda e, k=k: e.transpose(pbv[:, k * 128:k * 128 + nr], zt_t[:nr, k * 128:(k + 1) * 128],
                                               ident[:nr, :nr]),
               r=[ztkey, "ident"], w=[pbk[pbi]], sig=(k == 7))
        src = pbv.rearrange("p (k t) -> p k t", t=128)[:, :, :nr]
        g3 = pv[:, gcol:gcol + 8].unsqueeze(2).to_broadcast([128, 8, nr])
        op("dve", lambda e: e.tensor_tensor(dst_ap3, src, g3, op=ALU.mult), r=[pbk[pbi], "pv"], w=[dstkey])

    dumps = []

    def dump(name, t, shape, dt, keys):
        dd = dout(name, shape, dt)
        dumps.append(P.dma("sp", dd, t, r=keys, w=["dump_" + name], final=True))

    with P.scope():
        u_s = P.sb("u_s", [128, 4, 8, 516], BF16)
        lsa = P.sb("lsa", [128, 3, 4, 2, 4], F32)
        P.dma("sp", lsa[:], lam_s_d, w=["lsa"])
        uniq = [0]

        def mk(pre, F, n):
            uniq[0] += 1
            return [V(P.sb("%s%d_%d" % (pre, uniq[0], i), [128, F], F32)[:], "%s%d_%d" % (pre, uniq[0], i)) for i in range(n)]

        def tt(eng, o, a, b, o_):
            op(eng, lambda e: e.tensor_tensor(o.ap, a.ap, b.ap, op=o_), r=[a.key, b.key], w=[o.key])

        def ts(eng, o, a, s1, o1):
            op(eng, lambda e: e.tensor_scalar(o.ap, a.ap, s1, None, op0=o1), r=[a.key], w=[o.key])

        def act(o, a, f, **kw):
            op("act", lambda e: e.activation(o.ap, a.ap, f, **kw), r=[a.key, "hpi"], w=[o.key])

        def cmul(eng, ore, oim, are, aim, bre, bim, t1, t2):
            tt(eng, t1, are, bre, ALU.mult)
            tt(eng, t2, aim, bim, ALU.mult)
            tt(eng, ore, t1, t2, ALU.subtract)
            tt(eng, t1, are, bim, ALU.mult)
            tt(eng, t2, aim, bre, ALU.mult)
            tt(eng, oim, t1, t2, ALU.add)

        def dbl(eng, c, s, t1, t2, t3):
            tt(eng, t1, c, c, ALU.mult)
            tt(eng, t2, s, s, ALU.mult)
            tt(eng, t3, c, s, ALU.mult)
            ts(eng, s, t3, 2.0, ALU.mult)
            tt(eng, c, t1, t2, ALU.subtract)

        sm_ops = []
        real_op = op
        op = lambda *a_, **k_: sm_ops.append((a_, k_))
        eng = "dve"
        lre, lim, ldt = [V(lsa[:, q].rearrange("p a b c -> p (a b c)"), "lsa") for q in range(3)]
        T = mk("sm", 32, 14)
        dt, are, th, r, s, c, t1, t2, t3, S1re, S1im, sre, sim_, a = T
        act(dt, ldt, AF.Exp)
        tt(eng, are, lre, dt, ALU.mult)
        tt(eng, th, lim, dt, ALU.mult)
        act(r, are, AF.Exp)
        act(s, th, AF.Sin, scale=1.0 / 32.0)
        act(c, th, AF.Sin, scale=1.0 / 32.0, bias=hpi[:, 0:1])
        for _ in range(5):
            dbl(eng, c, s, t1, t2, t3)
        tt(eng, S1re, r, c, ALU.mult)
        tt(eng, S1im, r, s, ALU.mult)
        ts(eng, a, S1re, -1.0, ALU.add)
        tt(eng, t1, a, lre, ALU.mult)
        tt(eng, t2, S1im, lim, ALU.mult)
        tt(eng, sre, t1, t2, ALU.add)
        tt(eng, t1, S1im, lre, ALU.mult)
        tt(eng, t2, a, lim, ALU.mult)
        tt(eng, sim_, t1, t2, ALU.subtract)
        tt(eng, t1, lre, lre, ALU.mult)
        tt(eng, t2, lim, lim, ALU.mult)
        tt(eng, t3, t1, t2, ALU.add)
        op("dve", lambda e: e.reciprocal(t3.ap, t3.ap), r=[t3.key], w=[t3.key])
        tt(eng, sre, sre, t3, ALU.mult)
        tt(eng, sim_, sim_, t3, ALU.mult)
        u1, u2 = t1, t2
        rho8 = mk("smr", 32, 1)[0]
        act(rho8, are, AF.Exp, scale=8.0)
        pw_ = mk("smp", 32, 16)
        npim = mk("smn", 32, 9)
        qk_ = mk("smq", 32, 16)
        nqim = mk("smm", 32, 8)
        Pk = {1: (S1re, S1im)}
        for k in range(2, 9):
            Pk[k] = (pw_[2 * (k - 2)], pw_[2 * (k - 2) + 1])
            cmul("dve", Pk[k][0], Pk[k][1], Pk[k - 1][0], Pk[k - 1][1], S1re, S1im, u1, u2)
        for k in range(1, 9):
            ts("dve", npim[k], Pk[k][1], -1.0, ALU.mult)
        Qk = {0: (sre, sim_)}
        for k in range(1, 8):
            Qk[k] = (qk_[2 * k], qk_[2 * k + 1])
            cmul("dve", Qk[k][0], Qk[k][1], sre, sim_, Pk[k][0], Pk[k][1], u1, u2)
        for k in range(8):
            ts("dve", nqim[k], Qk[k][1], -1.0, ALU.mult)
        sturn = mk("smt", 32, 1)[0]
        ts(eng, sturn, th, float(8.0 / (2.0 * np.pi)), ALU.mult)
        op = real_op
        sm_pos = [0]

        def sm_replay(n):
            for a_, k_ in sm_ops[sm_pos[0]:sm_pos[0] + n]:
                op(*a_, **k_)
            sm_pos[0] = min(len(sm_ops), sm_pos[0] + n)

        with P.scope():
            uc = P.sb("uc", [128, 4, 2080], BF16)
            with P.scope():
                w_in_bf = P.sb("w_in_bf", [128, 8, 1536], BF16)
                for k in range(8):
                    P.dma("pool", w_in_bf[:, k, :], w_in_d[k * 128:(k + 1) * 128, :], w=["win%d" % k])
                winkeys = ["win%d" % k for k in range(8)]
                xt = [P.sb("xt%d" % i, [128, D], F32) for i in range(4)]
                zt = [P.sb("zt%d" % i, [128, D], BF16) for i in range(4)]
                sq = P.sb("sq", [128, D], F32)
                ssa = [P.sb("ssa%d" % i, [128, 4], F32) for i in range(4)]
                zfm = [P.sb("zfm%d" % i, [128, 8, 512], BF16) for i in range(2)]
                sg = [P.sb("sg%d" % i, [128, 512], F32) for i in range(2)]
                NTILE = 33

                def norm_act(g):
                    b = g % 4
                    r0 = g * 128
                    nr = min(128, LT - r0)
                    P.dma("sp", xt[b][:nr, :], xl[r0:r0 + nr, :], w=["xt%d" % b])
                    rms_tile(xt[b][:nr, :], nr, "xt%d" % b, ssa[b], "ssa%d" % b, sq, zt[b][:nr, :], "zt%d" % b, "dve")

                def norm_tr(g):
                    b = g % 4
                    blk, ti = g // 4, g % 4
                    r0 = g * 128
                    nr = min(128, LT - r0)
                    zb = blk % 2
                    transpose_tile(zt[b], "zt%d" % b, nr, g % 2, PV_GMIX, zfm[zb][:, :, ti * 128:ti * 128 + nr],
                                   "zfm%d_%d" % (zb, ti))

                def inproj_c(blk, c):
                    t0 = blk * 512
                    nb = min(512, LT - t0)
                    zb = blk % 2
                    zf = zfm[zb]
                    zkeys = ["zfm%d_%d" % (zb, ti) for ti in range((nb + 127) // 128)]
                    nfull = nb if blk < 4 else (32 if blk == 4 else 0)
                    if True:
                        if nfull > 0:
                            pv_i, pg_i = 2 + c % 2, 4 + c % 2
                            for k in range(8):
                                op("pe", lambda e, k=k, c=c: e.matmul(pb[pv_i][:, :nfull], w_in_bf[:, k, c * 128:(c + 1) * 128],
                                                                     zf[:, k, :nfull], start=(k == 0), stop=(k == 7)),
                                   r=zkeys + winkeys, w=[pbk[pv_i]], sig=(k == 7))
                            for k in range(8):
                                op("pe", lambda e, k=k, c=c: e.matmul(pb[pg_i][:, :nfull],
                                                                     w_in_bf[:, k, 512 + c * 128:512 + (c + 1) * 128],
                                                                     zf[:, k, :nfull], start=(k == 0), stop=(k == 7)),
                                   r=zkeys + winkeys, w=[pbk[pg_i]], sig=(k == 7))
                            sgt = sg[c % 2]
                            op("act", lambda e: e.activation(sgt[:, :nfull], pb[pg_i][:, :nfull], AF.Sigmoid),
                               r=[pbk[pg_i]], w=["sg%d" % (c % 2)])
                            op("dve", lambda e, c=c: e.tensor_tensor(uc[:, c, t0:t0 + nfull], pb[pv_i][:, :nfull],
                                                                    sgt[:, :nfull], op=ALU.mult),
                               r=[pbk[pv_i], "sg%d" % (c % 2)], w=["uc%d" % c])
                        pu_i = 6 + c % 2
                        for k in range(8):
                            op("pe", lambda e, k=k, c=c: e.matmul(pb[pu_i][:, :nb],
                                                                 w_in_bf[:, k, 1024 + c * 128:1024 + (c + 1) * 128],
                                                                 zf[:, k, :nb], start=(k == 0), stop=(k == 7)),
                               r=zkeys + winkeys, w=[pbk[pu_i]], sig=(k == 7))
                        op("act", lambda e, c=c: e.copy(u_s[:, c, :, t0 // 8:t0 // 8 + nb // 8],
                                                        pb[pu_i][:, :nb].rearrange("p (c s) -> p s c", s=8)),
                           r=[pbk[pu_i]], w=["us%d" % c])
                norm_act(0)
                norm_act(1)
                for g in range(4):
                    norm_act(g + 2)
                    norm_tr(g)
                for blk in range(9):
                    for c in range(4):
                        g_tr = 4 * (blk + 1) + c
                        if c % 2 == 0:
                            if g_tr + 2 < NTILE:
                                norm_act(g_tr + 2)
                            if g_tr + 3 < NTILE:
                                norm_act(g_tr + 3)
                        if g_tr < NTILE:
                            norm_tr(g_tr)
                        inproj_c(blk, c)
                        if blk == 3 and c == 0:
                            P.dma("pool", w_out_bf[:], w_out_d.rearrange("(k p) n -> p k n", p=128), w=["wout"])
                        if blk >= 1:
                            sm_replay(24)
                sm_replay(len(sm_ops))
            if stage == 1:
                dump("d_uc", uc[:], [128, 4, 2080], BF16, ["uc%d" % c for c in range(4)])
                dump("d_us", u_s[:], [128, 4, 8, 516], BF16, ["us%d" % c for c in range(4)])
                P.emit()
                return nc

            with P.scope():
                cw = P.sb("cw", [128, 4, 31], F32)
                P.dma("sp", cw[:], cw_d, w=["cw"])
                CD = P.sb("CD", [128, 4, 31, 128], BF16)
                for c in range(4):
                    for k in range(31):
                        if k % 2:
                            op("act", lambda e, c=c, k=k: e.activation(CD[:, c, k, :], identf[:], AF.Identity, scale=cw[:, c, k:k + 1]),
                               r=["identf", "cw"], w=["CD%d" % c])
                        else:
                            op("dve", lambda e, c=c, k=k: e.tensor_scalar(CD[:, c, k, :], identf[:], cw[:, c, k:k + 1], None, op0=ALU.mult),
                               r=["identf", "cw"], w=["CD%d" % c])
                yc = [P.sb("yc%d" % i, [128, 4, 512], F32) for i in range(2)]
                ysq = [P.sb("ysq%d" % i, [128, 4, 512], F32) for i in range(2)]
                st = [[P.sb("st%d_%d" % (i, j), [128, 512], F32) for j in range(4)] for i in range(2)]
                tmp = [P.sb("ctmp%d" % i, [128, 512], F32) for i in range(2)]
                def conv_mm(blk):
                    pp = blk % 2
                    for c in range(4):
                        for k in range(31):
                            op("pe", lambda e, c=c, k=k: e.matmul(pb[c][:, :], CD[:, c, k, :],
                                                                 uc[:, c, blk * 512 + 1 + k: blk * 512 + 1 + k + 512],
                                                                 start=(k == 0), stop=(k == 30)),
                               r=["CD%d" % c, "uc%d" % c], w=[pbk[c]], sig=(k == 30))
                        op("act", lambda e, c=c: e.activation(yc[pp][:, c, :], pb[c][:, :], AF.Identity,
                                                              bias=pv[:, PV_CONVB + c:PV_CONVB + c + 1]),
                           r=[pbk[c], "pv"], w=["yc%d_%d" % (pp, c)])
                        op("pool", lambda e, c=c: e.tensor_tensor(ysq[pp][:, c, :], yc[pp][:, c, :], yc[pp][:, c, :], op=ALU.mult),
                           r=["yc%d_%d" % (pp, c)], w=["ysq%d_%d" % (pp, c)])

                def conv_post(blk):
                    pp = blk % 2
                    pm, pq = 4 + 2 * pp, 5 + 2 * pp
                    for c in range(4):
                        op("pe", lambda e, c=c: e.matmul(pb[pm][:, :], onesf[:], yc[pp][:, c, :], start=(c == 0), stop=(c == 3)),
                           r=["onesf", "yc%d_%d" % (pp, c)], w=[pbk[pm]], sig=(c == 3))
                    for c in range(4):
                        op("pe", lambda e, c=c: e.matmul(pb[pq][:, :], onesf[:], ysq[pp][:, c, :], start=(c == 0), stop=(c == 3)),
                           r=["onesf", "ysq%d_%d" % (pp, c)], w=[pbk[pq]], sig=(c == 3))
                    mean, msq, var, rstd = st[pp]
                    sk = "st%d" % pp
                    op("act", lambda e: e.copy(mean[:], pb[pm][:]), r=[pbk[pm]], w=[sk + "m"])
                    op("act", lambda e: e.activation(msq[:], pb[pm][:], AF.Square), r=[pbk[pm]], w=[sk + "q"])
                    op("dve", lambda e: e.tensor_tensor(var[:], pb[pq][:], msq[:], op=ALU.subtract), r=[pbk[pq], sk + "q"], w=[sk + "v"])
                    op("act", lambda e: e.activation(var[:], var[:], AF.Sqrt, bias=epst[:, 0:1]), r=[sk + "v", "epst"], w=[sk + "v"])
                    op("dve", lambda e: e.reciprocal(rstd[:], var[:]), r=[sk + "v"], w=[sk + "r"])
                    for c in range(4):
                        tm = tmp[c % 2]
                        tk = "ctmp%d" % (c % 2)
                        op("pool", lambda e, c=c, tm=tm: e.tensor_tensor(tm[:], yc[pp][:, c, :], mean[:], op=ALU.subtract),
                           r=["yc%d_%d" % (pp, c), sk + "m"], w=[tk])
                        op("dve", lambda e, tm=tm: e.tensor_tensor(tm[:], tm[:], rstd[:], op=ALU.mult), r=[tk, sk + "r"], w=[tk])
                        op("act", lambda e, c=c, tm=tm: e.activation(mixed[:, c, blk * 512:(blk + 1) * 512], tm[:], AF.Silu,
                                                                     bias=pv[:, PV_LNB + c:PV_LNB + c + 1],
                                                                     scale=pv[:, PV_LNG + c:PV_LNG + c + 1]),
                           r=[tk, "pv"], w=["mixed%d" % c])

                conv_mm(0)
                for blk in range(4):
                    if blk + 1 < 4:
                        conv_mm(blk + 1)
                    conv_post(blk)
        if stage == 2:
            dump("d_mixed", mixed[:, 0:4, :], [128, 4, NOWN], BF16, ["mixed%d" % c for c in range(4)])
            P.emit()
            return nc

        with P.scope():
            WBs = [P.sb("WB%d" % i, [128, 2, 8, 2, 128], BF16) for i in range(2)]
            CW = P.sb("CW", [128, 8, 2, 2, 4, 32], BF16)
            KF = P.sb("KF", [128, 2, 8, 128], BF16)
            iota = P.sb("iota", [128, NBWD], F32)
            tbt = [P.sb("tbt%d" % i, [128, NBWD], F32) for i in range(2)]
            Dd = P.sb("Dd", [128, 128], BF16)
            Sb = P.sb("Sb", [128, 2, 4, 2, NFWD], BF16)
            VAbig = P.sb("VAbig", [128, 3, 2, NBWD], F32)
            VAb = [VAbig[:, i] for i in range(3)]
            TBb = [P.sb("TBb%d" % i, [128, 2, NBWD], F32) for i in range(3)]
            DMb = [P.sb("DMb%d" % i, [128, 2, NBWD], F32) for i in range(2)]
            pt1 = P.sb("pt1", [128, NBWD], F32)
            pt2 = P.sb("pt2", [128, NBWD], F32)
            pt5 = P.sb("pt5", [128, NBWD], F32)
            pt6 = P.sb("pt6", [128, NBWD], F32)
            pt3 = P.sb("pt3", [128, NFWD], F32)
            pt4 = P.sb("pt4", [128, NFWD], F32)
            bcs = [P.sb("bc0", [128, 2, 2, 128], F32)] * 2
            cs = P.sb("cs", [128, 2, 2, 4, 32], F32)
            bs = P.sb("bs", [128, 2, 2, 4, 32], F32)
            Bb = P.sb("Bb", [128, 2, 2, 4, 32], BF16)
            C0 = P.sb("C0", [128, 2, 2, 4, 32], BF16)
            LTt = [P.sb("LTt%d" % i, [128, 4, 32], F32) for i in range(8)]
            wt1 = P.sb("wt1", [128, 2, 128], F32)
            wt2 = P.sb("wt2", [128, 2, 128], F32)
            wt3 = P.sb("wt3", [128, 2, 128], F32)
            wt4 = P.sb("wt4", [128, 2, 128], F32)
            mgc = P.sb("mgc", [128, 2], F32)
            op("pool", lambda e: e.memset(mgc[:, 0:1], 12582912.0), w=["mgc"])
            op("pool", lambda e: e.memset(mgc[:, 1:2], -12582912.0), w=["mgc"])
            gt = [P.sb("gt%d" % i, [128, 512], F32) for i in range(4)]
            xsf = VAbig[:, 0:2].rearrange("p a b c -> p (a b c)")[:, 0:NOWN]
            onet = P.sb("onet", [128, 1], F32)
            op("pool", lambda e: e.memset(onet[:], 1.0), w=["onet"])
            op("pool", lambda e: e.memset(KF[:], 0.0), w=["KF"])
            P.dma("sp", iota[:], iota_d, w=["iota"])

            big = mk("smb", 256, 4)

            def b3v(v):
                return V(v.ap.rearrange("p (a c) -> p a c", c=32), v.key)

            B1, B2, B3, B4 = [b3v(x_) for x_ in big]

            def rev(t_, ri, start, n):
                return bass.AP(t_, ri * NBWD + start, [[2 * NBWD, 128], [-1, n]])

            def revA(i3, ri, start, n):
                return bass.AP(VAbig, (i3 * 2 + ri) * NBWD + start, [[6 * NBWD, 128], [-1, n]])

            def prep_wb(jj, half=None):
                WB = WBs[jj % 2]
                wbk = "WB%d" % (jj % 2)
                bc = bcs[0]
                bck = "bc0"
                if half in (None, 0):
                    P.dma("sp", bc[:], b_c_d[:, jj], w=[bck])
                es = [e_ for e_ in range(8) if half is None or e_ // 4 == half]

                def lt_copies(e_):
                    for ri in range(2):
                        for dl in range(2):
                            li = (e_ % 2) * 4 + ri * 2 + dl
                            qv = Qk[e_][ri]
                            o0 = jj * 8 + dl * 4
                            op("act", lambda e, li=li, qv=qv, o0=o0: e.copy(
                                LTt[li][:], qv.ap[:, o0:o0 + 4].unsqueeze(2).to_broadcast([128, 4, 32])), r=[qv.key], w=["LT%d" % li])

                lt_copies(es[0])
                for e_ in es:
                    if e_ + 1 in es:
                        lt_copies(e_ + 1)
                    pw = e_ % 4
                    for ri in range(2):
                        for dl in range(2):
                            li = (e_ % 2) * 4 + ri * 2 + dl
                            lt = LTt[li]
                            ltk = "LT%d" % li
                            col = (ri * 2 + dl) * 128
                            op("pe", lambda e, lt=lt, col=col, pw=pw: e.matmul(
                                pb[pw][:, col:col + 128], lt[:].rearrange("p a b -> p (a b)"), identf[:], start=True, stop=True),
                               r=[ltk, "identf"], w=[pbk[pw]], sig=(ri == 1 and dl == 1))
                    qre_ps = pb[pw][:, 0:256].rearrange("p (a b) -> p a b", b=128)
                    qim_ps = pb[pw][:, 256:512].rearrange("p (a b) -> p a b", b=128)
                    op("dve", lambda e, qre_ps=qre_ps: e.tensor_tensor(wt1[:], qre_ps, bc[:, 0], op=ALU.mult), r=[pbk[pw], bck], w=["wt1"])
                    op("dve", lambda e, qim_ps=qim_ps: e.tensor_tensor(wt2[:], qim_ps, bc[:, 1], op=ALU.mult), r=[pbk[pw], bck], w=["wt2"])
                    op("dve", lambda e, qre_ps=qre_ps: e.tensor_tensor(wt3[:], qre_ps, bc[:, 1], op=ALU.mult), r=[pbk[pw], bck], w=["wt3"])
                    op("dve", lambda e, qim_ps=qim_ps: e.tensor_tensor(wt4[:], qim_ps, bc[:, 0], op=ALU.mult), r=[pbk[pw], bck], w=["wt4"])
                    op("dve", lambda e, e_=e_: e.tensor_tensor(WB[:, 0, e_, :, :], wt1[:], wt2[:], op=ALU.subtract), r=["wt1", "wt2"], w=[wbk])
                    op("dve", lambda e, e_=e_: e.tensor_tensor(WB[:, 1, e_, :, :], wt3[:], wt4[:], op=ALU.add), r=["wt3", "wt4"], w=[wbk + "i"])

            def prep_cwkf(jj, part=None):
                if part in (None, "a"):
                    P.dma("sp", cs[:], c_s_d[:, jj], w=["cs"])
                    P.dma("sp", bs[:], b_s_d[:, jj], w=["bs"])

                def bc8(v):
                    return V(v.ap[:, jj * 8:(jj + 1) * 8].unsqueeze(2).to_broadcast([128, 8, 32]), v.key)

                csre = V(cs[:, 0].rearrange("p a b c -> p (a b) c"), "cs")
                csim = V(cs[:, 1].rearrange("p a b c -> p (a b) c"), "cs")
                bsre = V(bs[:, 0].rearrange("p a b c -> p (a b) c"), "bs")
                bsim = V(bs[:, 1].rearrange("p a b c -> p (a b) c"), "bs")
                for k in range(1, 9):
                    if part == "b":
                        break
                    pre_, pim_ = Pk[k]
                    ce = "pool" if k % 2 == 0 else "dve"
                    Ba, Bc = (B3, B4) if k % 2 == 0 else (B1, B2)
                    cwk = "CW%d" % (k - 1)
                    tt(ce, Ba, csre, bc8(pre_), ALU.mult)
                    tt(ce, Bc, csim, bc8(pim_), ALU.mult)
                    op(ce, lambda e, k=k, Ba=Ba, Bc=Bc: e.tensor_tensor(CW[:, k - 1, 0].rearrange("p a b c -> p (a b) c"), Ba.ap, Bc.ap,
                                                                       op=ALU.subtract), r=[Ba.key, Bc.key], w=[cwk])
                    tt(ce, Ba, csre, bc8(npim[k]), ALU.mult)
                    tt(ce, Bc, csim, bc8(pre_), ALU.mult)
                    op(ce, lambda e, k=k, Ba=Ba, Bc=Bc: e.tensor_tensor(CW[:, k - 1, 1].rearrange("p a b c -> p (a b) c"), Ba.ap, Bc.ap,
                                                                       op=ALU.subtract), r=[Ba.key, Bc.key], w=[cwk])
                if part == "b":
                    pass
                else:
                  qre, qim = Qk[0]
                  tt("dve", B1, bsre, bc8(qre), ALU.mult)
                  tt("dve", B2, bsim, bc8(qim), ALU.mult)
                  op("dve", lambda e: e.tensor_tensor(Bb[:, 0].rearrange("p a b c -> p (a b) c"), B1.ap, B2.ap, op=ALU.subtract),
                     r=[B1.key, B2.key], w=["Bb"])
                  tt("dve", B1, bsre, bc8(qim), ALU.mult)
                  tt("dve", B2, bsim, bc8(qre), ALU.mult)
                  op("dve", lambda e: e.tensor_tensor(Bb[:, 1].rearrange("p a b c -> p (a b) c"), B1.ap, B2.ap, op=ALU.add),
                     r=[B1.key, B2.key], w=["Bb"])
                  op("dve", lambda e: e.tensor_copy(C0[:, 0].rearrange("p a b c -> p (a b) c"), csre.ap), r=["cs"], w=["C0"])
                  op("dve", lambda e: e.tensor_scalar(C0[:, 1].rearrange("p a b c -> p (a b) c"), csim.ap, -1.0, None, op0=ALU.mult),
                     r=["cs"], w=["C0"])
                if part == "a":
                    return
                for k in range(8):
                    for dl in range(2):
                        for q in range(4):
                            col = (dl * 8 + k) * 32
                            first = (k == 0 and dl == 0)
                            if k == 0:
                                r_re, r_im, rk = C0[:, 0, dl, q, :], C0[:, 1, dl, q, :], "C0"
                            else:
                                r_re, r_im, rk = CW[:, k - 1, 0, dl, q, :], CW[:, k - 1, 1, dl, q, :], "CW%d" % (k - 1)
                            op("pe", lambda e, dl=dl, q=q, col=col, first=first, r_re=r_re: e.matmul(
                                pb[3][32 * q:32 * q + 32, col:col + 32], Bb[:, 0, dl, q, :], r_re,
                                start=first, stop=False, skip_group_check=True, tile_position=(0, 32 * q)),
                               r=["Bb", rk], w=[pbk[3]], sig=False)
                            op("pe", lambda e, dl=dl, q=q, col=col, r_im=r_im: e.matmul(
                                pb[3][32 * q:32 * q + 32, col:col + 32], Bb[:, 1, dl, q, :], r_im,
                                start=False, stop=True, skip_group_check=True, tile_position=(0, 32 * q)),
                               r=["Bb", rk], w=[pbk[3]], sig=True)
                pk4 = pb[3][:].rearrange("p (a k c) -> p a k c", a=2, k=8)
                for q in range(4):
                    op("act", lambda e, q=q: e.copy(KF[32 * q:32 * q + 32, :, :, 32 * q:32 * q + 32], pk4[32 * q:32 * q + 32]),
                       r=[pbk[3]], w=["KF"])
                op("act", lambda e: e.activation(Dd[:], identf[:], AF.Identity, scale=pv[:, PV_SSMD + jj:PV_SSMD + jj + 1]),
                   r=["identf", "pv"], w=["Dd"])

            def make_emitters(jj):
                WB = WBs[jj % 2]
                wbk = "WB%d" % (jj % 2)

                tiles = [(dl, q) for dl in range(2) for q in range(4)]

                def emit_v(i):
                    dl, q = tiles[i]
                    base = 0 if dl == 0 else 16
                    cbs = [(0, NFWD)] if dl == 0 else [(0, 257), (257, 257)]
                    A_ = VAb[(i + 2) % 3]
                    ak = "VAb%d" % ((i + 2) % 3)
                    pc = 0
                    for ri in range(2):
                        for (c0, n) in cbs:
                            pi_ = (i * 4 + pc) % 4
                            pc += 1
                            for s_ in range(8):
                                sig_ = (7 - s_) if dl == 0 else s_
                                st_ = base + s_ + 8 * c0
                                op("pe", lambda e, q=q, ri=ri, sig_=sig_, st_=st_, n=n, s_=s_, pi_=pi_, dl=dl: e.matmul(
                                    pb[pi_][:, :n], WB[32 * q:32 * q + 32, ri, sig_, dl, :],
                                    u_s[32 * q:32 * q + 32, jj, s_, base // 8 + c0:base // 8 + c0 + n],
                                    start=(s_ == 0), stop=(s_ == 7), tile_position=(32 * q, 0)),
                                   r=[wbk, wbk + "i", "us%d" % jj], w=[pbk[pi_]], sig=(s_ == 7))
                            op("act", lambda e, ri=ri, c0=c0, n=n, pi_=pi_, A_=A_: e.copy(A_[:, ri, c0:c0 + n], pb[pi_][:, :n]),
                               r=[pbk[pi_]], w=[ak])

                def emit_table(i):
                    dl, q = tiles[i]
                    N = NFWD if dl == 0 else NBWD
                    T_ = TBb[(i + 2) % 3]
                    tk = "TBb%d" % ((i + 2) % 3)
                    o0 = jj * 8 + dl * 4 + q
                    tq, tr = tbt
                    MAGIC = 12582912.0
                    op("act", lambda e: e.activation(tq[:, 0:N], iota[:, 0:N], AF.Identity, scale=sturn.ap[:, o0:o0 + 1]),
                       r=["iota", sturn.key], w=["tbt0"])
                    op("act", lambda e: e.activation(tr[:, 0:N], tq[:, 0:N], AF.Identity, bias=mgc[:, 0:1]), r=["tbt0", "mgc"], w=["tbt1"])
                    op("act", lambda e: e.activation(tr[:, 0:N], tr[:, 0:N], AF.Identity, bias=mgc[:, 1:2]), r=["tbt1", "mgc"], w=["tbt1"])
                    op("dve", lambda e: e.tensor_tensor(tq[:, 0:N], tq[:, 0:N], tr[:, 0:N], op=ALU.subtract), r=["tbt0", "tbt1"], w=["tbt0"])
                    op("act", lambda e: e.activation(T_[:, 1, 0:N], tq[:, 0:N], AF.Sin, scale=TWOPI_S), r=["tbt0"], w=[tk])
                    op("act", lambda e: e.activation(tr[:, 0:N], tq[:, 0:N], AF.Sin, scale=TWOPI_S / 2.0), r=["tbt0"], w=["tbt1"])
                    op("act", lambda e: e.activation(tr[:, 0:N], tr[:, 0:N], AF.Square), r=["tbt1"], w=["tbt1"])
                    op("act", lambda e: e.activation(T_[:, 0, 0:N], tr[:, 0:N], AF.Identity, scale=-2.0, bias=onet[:, 0:1]),
                       r=["tbt1", "onet"], w=[tk])

                def emit_demod(i):
                    dl, q = tiles[i]
                    N = NFWD if dl == 0 else NBWD
                    A_, T_, D_ = VAb[(i + 2) % 3], TBb[(i + 2) % 3], DMb[i % 2]
                    ak, tk, dk = "VAb%d" % ((i + 2) % 3), "TBb%d" % ((i + 2) % 3), "DMb%d" % (i % 2)
                    if dl == 0:
                        wre, wim = A_[:, 0, 0:N], A_[:, 1, 0:N]
                    else:
                        wre, wim = revA((i + 2) % 3, 0, N - 1, N), revA((i + 2) % 3, 1, N - 1, N)
                    op("pool", lambda e: e.tensor_tensor(pt1[:, 0:N], T_[:, 0, 0:N], wre, op=ALU.mult), r=[tk, ak], w=["pt1"])
                    op("pool", lambda e: e.tensor_tensor(pt2[:, 0:N], T_[:, 1, 0:N], wim, op=ALU.mult), r=[tk, ak], w=["pt2"])
                    op("pool", lambda e: e.tensor_tensor(pt5[:, 0:N], T_[:, 0, 0:N], wim, op=ALU.mult), r=[tk, ak], w=["pt5"])
                    op("pool", lambda e: e.tensor_tensor(pt6[:, 0:N], T_[:, 1, 0:N], wre, op=ALU.mult), r=[tk, ak], w=["pt6"])
                    op("pool", lambda e: e.tensor_tensor(D_[:, 0, 0:N], pt1[:, 0:N], pt2[:, 0:N], op=ALU.add), r=["pt1", "pt2"], w=[dk])
                    op("pool", lambda e: e.tensor_tensor(D_[:, 1, 0:N], pt5[:, 0:N], pt6[:, 0:N], op=ALU.subtract), r=["pt5", "pt6"], w=[dk + "i"])

                def emit_scan(i):
                    dl, q = tiles[i]
                    N = NFWD if dl == 0 else NBWD
                    A_, D_ = VAb[(i + 2) % 3], DMb[i % 2]
                    ak, dk = "VAb%d" % ((i + 2) % 3), "DMb%d" % (i % 2)
                    o0 = jj * 8 + dl * 4 + q
                    rho = rho8.ap[:, o0:o0 + 1].to_broadcast([128, N])
                    for ri in range(2):
                        op("dve", lambda e, ri=ri: e.tensor_tensor_scan(A_[:, ri, 0:N], rho, D_[:, ri, 0:N], 0.0, op0=ALU.mult, op1=ALU.add),
                           r=[dk, dk + "i", rho8.key], w=[ak])

                def emit_remod(i):
                    dl, q = tiles[i]
                    N = NFWD if dl == 0 else NBWD
                    A_, T_ = VAb[(i + 2) % 3], TBb[(i + 2) % 3]
                    ak, tk = "VAb%d" % ((i + 2) % 3), "TBb%d" % ((i + 2) % 3)
                    n = NFWD
                    if dl == 0:
                        tre, tim, xre, xim = T_[:, 0, 0:n], T_[:, 1, 0:n], A_[:, 0, 0:n], A_[:, 1, 0:n]
                    else:
                        tre, tim = rev(T_, 0, N - 1, n), rev(T_, 1, N - 1, n)
                        xre, xim = revA((i + 2) % 3, 0, N - 1, n), revA((i + 2) % 3, 1, N - 1, n)
                    sk = "Sb%d_%d" % (dl, q)
                    op("pool", lambda e: e.tensor_tensor(pt1[:, 0:n], tre, xre, op=ALU.mult), r=[tk, ak], w=["pt1"])
                    op("pool", lambda e: e.tensor_tensor(pt2[:, 0:n], tim, xim, op=ALU.mult), r=[tk, ak], w=["pt2"])
                    op("pool", lambda e: e.tensor_tensor(Sb[:, dl, q, 0, :], pt1[:, 0:n], pt2[:, 0:n], op=ALU.subtract), r=["pt1", "pt2"], w=[sk + "r"])
                    op("dve", lambda e: e.tensor_tensor(pt3[:, 0:n], tre, xim, op=ALU.mult), r=[tk, ak], w=["pt3"])
                    op("dve", lambda e: e.tensor_tensor(pt4[:, 0:n], tim, xre, op=ALU.mult), r=[tk, ak], w=["pt4"])
                    op("dve", lambda e: e.tensor_tensor(Sb[:, dl, q, 1, :], pt3[:, 0:n], pt4[:, 0:n], op=ALU.add), r=["pt3", "pt4"], w=[sk + "i"])

                return emit_v, emit_table, emit_demod, emit_scan, emit_remod

            def output_phase(jj):
                sbkeys = ["Sb%d_%d%s" % (dl, q, x_) for dl in range(2) for q in range(4) for x_ in ("r", "i")]

                def ycol(t):
                    return pb[4 + t // 2], pbk[4 + t // 2], (t % 2) * 256

                def ucol(s_):
                    return u_s[:, jj, s_, 2:258]

                for t in range(8):
                    pt_, pk_, c0 = ycol(t)
                    op("pe", lambda e, pt_=pt_, c0=c0, t=t: e.matmul(pt_[:, c0:c0 + 256], Dd[:], ucol(t), start=(t % 2 == 0), stop=False,
                                                                    skip_group_check=True),
                       r=["Dd", "us%d" % jj], w=[pk_], sig=False)
                for dl in range(2):
                    for t in range(8):
                        pt_, pk_, c0 = ycol(t)
                        srange = range(0, t + 1) if dl == 0 else range(t, 8)
                        for s_ in srange:
                            k = abs(t - s_)
                            op("pe", lambda e, dl=dl, s_=s_, k=k, pt_=pt_, c0=c0: e.matmul(
                                pt_[:, c0:c0 + 256], KF[:, dl, k, :], ucol(s_),
                                start=False, stop=False, skip_group_check=True),
                               r=["KF", "us%d" % jj], w=[pk_], sig=False)
                for t in range(8):
                    pt_, pk_, c0 = ycol(t)
                    cnt = 0
                    for dl in range(2):
                        eidx = t if dl == 0 else 7 - t
                        for q in range(4):
                            for ri in range(2):
                                cnt += 1
                                last = (cnt == 16)
                                op("pe", lambda e, dl=dl, q=q, ri=ri, eidx=eidx, pt_=pt_, c0=c0, last=last: e.matmul(
                                    pt_[32 * q:32 * q + 32, c0:c0 + 256], CW[:, eidx, ri, dl, q, :],
                                    Sb[:, dl, q, ri, 1:257],
                                    start=False, stop=last, skip_group_check=True, tile_position=(0, 32 * q)),
                                   r=["CW%d" % eidx] + sbkeys, w=[pk_], sig=last)
                xview = xsf[:].rearrange("p (c t) -> p t c", t=8)
                for b_ in range(4):
                    op("act", lambda e, b_=b_: e.copy(xview[:, 2 * b_:2 * b_ + 2, :], pb[4 + b_][:].rearrange("p (t c) -> p t c", t=2)),
                       r=[pbk[4 + b_]], w=["xsf", "VAb0", "VAb1"])
                for bp in range(2):
                    bs_ = [2 * bp, 2 * bp + 1]
                    G1 = {b_: gt[b_ % 2] for b_ in bs_}
                    G2 = {b_: gt[2 + b_ % 2] for b_ in bs_}
                    K1 = {b_: "gt%d" % (b_ % 2) for b_ in bs_}
                    K2 = {b_: "gt%d" % (2 + b_ % 2) for b_ in bs_}
                    XS = {b_: xsf[:, b_ * 512:(b_ + 1) * 512] for b_ in bs_}
                    xk_ = ["xsf", "VAb0", "VAb1"]
                    for b_ in bs_:
                        op("act", lambda e, b_=b_: e.activation(G1[b_][:], XS[b_], AF.Square), r=xk_, w=[K1[b_]])
                    for b_ in bs_:
                        op("act", lambda e, b_=b_: e.activation(G1[b_][:], G1[b_][:], AF.Identity, scale=0.044715, bias=onet[:, 0:1]),
                           r=[K1[b_], "onet"], w=[K1[b_]])
                    for b_ in bs_:
                        op("dve", lambda e, b_=b_: e.tensor_tensor(G1[b_][:], G1[b_][:], XS[b_], op=ALU.mult), r=[K1[b_]] + xk_, w=[K1[b_]])
                    for b_ in bs_:
                        op("act", lambda e, b_=b_: e.activation(G2[b_][:], G1[b_][:], AF.Sigmoid, scale=1.5957691216057308), r=[K1[b_]], w=[K2[b_]])
                    for b_ in bs_:
                        op("dve", lambda e, b_=b_: e.tensor_tensor(mixed[:, 4 + jj, b_ * 512:(b_ + 1) * 512], XS[b_], G2[b_][:], op=ALU.mult),
                           r=xk_ + [K2[b_]], w=["Y%d_b%d" % (4 + jj, b_)])
            ems = [make_emitters(jj) for jj in range(4)]
            prep_wb(0)
            ems[0][0](0)
            ems[0][1](0)
            for jj in range(4):
                emit_v, emit_table, emit_demod, emit_scan, emit_remod = ems[jj]
                for i in range(9):
                    if i + 1 < 8:
                        emit_v(i + 1)
                    if i < 8:
                        emit_demod(i)
                    if i >= 1:
                        emit_remod(i - 1)
                    if i + 1 < 8:
                        emit_table(i + 1)
                    if i < 8:
                        emit_scan(i)
                    if i == 1:
                        prep_cwkf(jj, "a")
                    if i == 6:
                        prep_cwkf(jj, "b")
                    if i == 3 and jj < 3:
                        prep_wb(jj + 1, 0)
                    if i == 5 and jj < 3:
                        prep_wb(jj + 1, 1)
                if jj < 3:
                    ems[jj + 1][0](0)
                    ems[jj + 1][1](0)
                output_phase(jj)
            if stage == 4:
                dump("d_ygb", mixed[:, 4:8, :], [128, 4, NOWN], BF16, ["Y%d_b%d" % (c, b_) for c in range(4, 8) for b_ in range(4)])
                P.emit()
                return nc
            gluw = P.sb("gluw", [128, 4, 512], BF16)
            P.dma("pool", gluw[:], glu_w_d.rearrange("(k p) n -> p k n", p=128), w=["gluw"])
            for blk in range(4):
                for m in range(4):
                    for k in range(4):
                        op("pe", lambda e, m=m, k=k, gb=(blk % 2) * 4 + m: e.matmul(pb[gb][:, :], gluw[:, k, m * 128:(m + 1) * 128],
                                                             mixed[:, 4 + k, blk * 512:(blk + 1) * 512], start=(k == 0), stop=(k == 3)),
                           r=["gluw"] + ["Y%d_b%d" % (c, blk) for c in range(4, 8)], w=[pbk[(blk % 2) * 4 + m]], sig=(k == 3))
                    op("act", lambda e, m=m, gb=(blk % 2) * 4 + m: e.activation(gt[m][:], pb[gb][:, :], AF.Sigmoid,
                                                                                bias=pv[:, PV_GLUB + m:PV_GLUB + m + 1]),
                       r=[pbk[(blk % 2) * 4 + m], "pv"], w=["gt%d" % m])
                for m in range(4):
                    op("dve", lambda e, m=m: e.tensor_tensor(mixed[:, 4 + m, blk * 512:(blk + 1) * 512],
                                                            mixed[:, 4 + m, blk * 512:(blk + 1) * 512], gt[m][:], op=ALU.mult),
                       r=["gt%d" % m, "Y%d_b%d" % (4 + m, blk)], w=["Y%d_b%d" % (4 + m, blk)])
    if stage == 5:
        dump("d_mixed", mixed[:], [128, 8, NOWN], BF16, ["mixed%d" % c for c in range(4)] + ["Y%d_b%d" % (c, b_) for c in range(4, 8) for b_ in range(4)])
        P.emit()
        return nc

    h = P.sb("h", [128, 16, D], F32)
    zffn = P.sb("zffn", [128, 8, NOWN], BF16)
    ssb = [P.sb("ssb%d" % i, [128, 4], F32) for i in range(2)]
    sq2 = P.sb("sq2", [128, D], F32)
    zt2 = [P.sb("zt2_%d" % i, [128, D], BF16) for i in range(2)]
    w1c = [P.sb("w1c%d" % i, [128, 8, 512], BF16) for i in range(2)]
    w2c = [P.sb("w2c%d" % i, [128, 4, D], BF16) for i in range(2)]
    w1v = w_ff1_d.rearrange("(k p) n -> p k n", p=128)
    w2v = w_ff2_d.rearrange("(c f p) n -> c p f n", f=4, p=128)
    if True:
        P.dma("pool", w1c[0][:], w1v[:, :, 0:512], w=["w1c0"])
        P.dma("pool", w2c[0][:], w2v[0], w=["w2c0"])
        OPB = [(0, 1), (2, 3), (6, 7)]

        def outproj(tt_):
            hk = "h%d" % tt_
            P.dma("sp", h[:, tt_, :], xl[16 + tt_ * 128:16 + (tt_ + 1) * 128, :], w=[hk])
            for half in range(2):
                pi_ = OPB[tt_ % 3][half]
                for k in range(8):
                    op("pe", lambda e, k=k, half=half: e.matmul(pb[pi_][:, :], mixed[:, k, tt_ * 128:(tt_ + 1) * 128],
                                                               w_out_bf[:, k, half * 512:(half + 1) * 512],
                                                               start=(k == 0), stop=(k == 7)),
                       r=["wout"] + ["mixed%d" % c for c in range(4)] + ["Y%d_b%d" % (c, tt_ // 4) for c in range(4, 8)],
                       w=[pbk[pi_]], sig=(k == 7))

        def resid_add(tt_):
            hk = "h%d" % tt_
            for half in range(2):
                pi_ = OPB[tt_ % 3][half]
                op("dve", lambda e, half=half: e.tensor_tensor(h[:, tt_, half * 512:(half + 1) * 512],
                                                               h[:, tt_, half * 512:(half + 1) * 512], pb[pi_][:, :], op=ALU.add),
                   r=[hk, pbk[pi_]], w=[hk])

        def norm_chain(tt_):
            b = tt_ % 2
            hk = "h%d" % tt_
            rms_tile(h[:, tt_, :], 128, hk, ssb[b], "ssb%d" % b, sq2, zt2[b][:, :], "zt2_%d" % b, "pool")

        def tr_ffn(tt_):
            b = tt_ % 2
            transpose_tile(zt2[b], "zt2_%d" % b, 128, 4 + tt_ % 2, PV_GFFN, zffn[:, :, tt_ * 128:(tt_ + 1) * 128], "zffn%d" % tt_)

        outproj(0)
        outproj(1)
        resid_add(0)
        for tt_ in range(16):
            norm_chain(tt_)
            if tt_ + 1 < 16:
                resid_add(tt_ + 1)
            if tt_ + 2 < 16:
                outproj(tt_ + 2)
            tr_ffn(tt_)
    if stage == 6:
        dump("d_h", h[:], [128, 16, D], F32, ["h%d" % i for i in range(16)])
        dump("d_zffn", zffn[:], [128, 8, NOWN], BF16, ["zffn%d" % i for i in range(16)])
        P.emit()
        return nc

    with P.scope():
        hid = [P.sb("hid%d" % i, [128, 4, 512], BF16) for i in range(2)]
        rl = [P.sb("rl%d" % i, [128, 512], F32) for i in range(2)]
        gfin = P.sb("gfin", [128, D], F32)
        P.dma("sp", gfin[:], gfin_d.partition_broadcast(128), w=["gfin"])
        def load_w(fc):
            wb_ = fc % 2
            P.dma("pool", w1c[wb_][:], w1v[:, :, fc * 512:(fc + 1) * 512], w=["w1c%d" % wb_])
            P.dma("pool", w2c[wb_][:], w2v[fc], w=["w2c%d" % wb_])

        cnts = {"h": 0, "o": 0}

        def ffn_hidden(step):
            fc, blk = step // 4, step % 4
            wb_ = fc % 2
            hb = step % 2
            for f in range(4):
                pi_ = cnts["h"] % 4
                rb = cnts["h"] % 2
                cnts["h"] += 1
                for k in range(8):
                    op("pe", lambda e, k=k, f=f: e.matmul(pb[pi_][:, :], w1c[wb_][:, k, f * 128:(f + 1) * 128],
                                                         zffn[:, k, blk * 512:(blk + 1) * 512], start=(k == 0), stop=(k == 7)),
                       r=["w1c%d" % wb_] + ["zffn%d" % (blk * 4 + i) for i in range(4)], w=[pbk[pi_]], sig=(k == 7))
                op("act", lambda e: e.activation(rl[rb][:], pb[pi_][:, :], AF.Relu), r=[pbk[pi_]], w=["rl%d" % rb])
                op("pool", lambda e, f=f: e.tensor_tensor(hid[hb][:, f, :], rl[rb][:], rl[rb][:], op=ALU.mult),
                   r=["rl%d" % rb], w=["hid%d_%d" % (hb, f)])

        def ffn_out(step):
            fc, blk = step // 4, step % 4
            wb_ = fc % 2
            hb = step % 2
            for ti in range(4):
                tt_ = blk * 4 + ti
                hk = "h%d" % tt_
                for half in range(2):
                    pi_ = 4 + cnts["o"] % 4
                    cnts["o"] += 1
                    for f in range(4):
                        op("pe", lambda e, f=f, half=half: e.matmul(pb[pi_][:, :], hid[hb][:, f, ti * 128:(ti + 1) * 128],
                                                                   w2c[wb_][:, f, half * 512:(half + 1) * 512],
                                                                   start=(f == 0), stop=(f == 3)),
                           r=["w2c%d" % wb_] + ["hid%d_%d" % (hb, i) for i in range(4)], w=[pbk[pi_]], sig=(f == 3))
                    op("dve", lambda e, half=half: e.tensor_tensor(h[:, tt_, half * 512:(half + 1) * 512],
                                                                   h[:, tt_, half * 512:(half + 1) * 512], pb[pi_][:, :], op=ALU.add),
                       r=[hk, pbk[pi_]], w=[hk])

        def final_norm(tt_):
            b = tt_ % 2
            hk = "h%d" % tt_
            ss = ssb[b]
            sk = "ssb%d" % b
            op("act", lambda e: e.activation(sq2[:, :], h[:, tt_, :], AF.Square, accum_out=ss[:, 0:1]), r=[hk], w=["sq2", sk])
            op("act", lambda e: e.activation(ss[:, 1:2], ss[:, 0:1], AF.Sqrt, bias=epst[:, 0:1], scale=1.0 / D), r=[sk, "epst"], w=[sk])
            op("dve", lambda e: e.reciprocal(ss[:, 2:3], ss[:, 1:2]), r=[sk], w=[sk])
            op("dve", lambda e: e.scalar_tensor_tensor(h[:, tt_, :], h[:, tt_, :], ss[:, 2:3], gfin[:], op0=ALU.mult, op1=ALU.mult),
               r=[hk, sk, "gfin"], w=[hk])
            P.dma("sp", out_d[tt_ * 128:(tt_ + 1) * 128, :], h[:, tt_, :], r=[hk], w=["out%d" % tt_], final=True)

        ffn_hidden(0)
        for step in range(32):
            if step % 4 == 0 and step // 4 + 1 < 8:
                load_w(step // 4 + 1)
            if step + 1 < 32:
                ffn_hidden(step + 1)
            ffn_out(step)
            if step >= 28:
                for ti in range(4):
                    final_norm((step - 28) * 4 + ti)
    P.emit()
    return nc


def _core_inputs(inp, b, half):
    f32 = np.float32
    x = inp["x"][b]
    meta = inp["meta_tokens"]
    z16 = np.zeros((16, D), f32)
    if half == 0:
        xl = np.concatenate([meta, x, z16], axis=0)
        dirs = [0, 1]
        convw = inp["conv_w"][0]
    else:
        full = np.concatenate([meta, x], axis=0)[::-1]
        xl = np.concatenate([z16, full], axis=0)
        dirs = [1, 0]
        convw = inp["conv_w"][0][::-1]
    xl = np.ascontiguousarray(xl, dtype=f32)

    def col8(v):
        return np.ascontiguousarray(v.reshape(8, 128).T)

    def col4(v):
        return np.ascontiguousarray(v.reshape(4, 128).T)

    pv = np.concatenate([col8(inp["norm_mix_g"][0]), col8(inp["norm_ffn_g"][0]), col4(inp["conv_b"][0]),
                         col4(inp["conv_ln_g"][0]), col4(inp["conv_ln_b"][0]), col4(inp["ssm_d"][0]),
                         col4(inp["ssm_glu_b"][0])], axis=1).astype(f32)
    cw = np.ascontiguousarray(convw.T.reshape(4, 128, 31).transpose(1, 0, 2)).astype(f32)

    lam = [inp["ssm_lam_re"][0][dirs], inp["ssm_lam_im"][0][dirs]]
    ldt = np.broadcast_to(inp["ssm_log_dt"][0][dirs][:, :, None], (2, 32, 64))
    lam3 = np.stack([lam[0], lam[1], ldt], axis=0)
    bri = np.stack([inp["ssm_b_re"][0][dirs], inp["ssm_b_im"][0][dirs]], axis=0)
    cri = np.stack([inp["ssm_c_re"][0][dirs], inp["ssm_c_im"][0][dirs]], axis=0)

    l5 = lam3.reshape(3, 2, 4, 8, 64)
    lam_c = np.broadcast_to(l5.transpose(3, 2, 0, 1, 4)[:, None, :, :, :, None, :],
                            (8, 16, 4, 3, 2, 2, 64)).reshape(128, 4, 3, 2, 128)
    b6 = bri.reshape(2, 2, 4, 8, 64, 16)
    b_c = np.zeros((8, 16, 4, 2, 2, 2, 64), f32)
    for gl in range(8):
        b_c[gl, :, :, :, :, gl % 2, :] = b6[:, :, :, gl, :, :].transpose(4, 2, 0, 1, 3)
    b_c = b_c.reshape(128, 4, 2, 2, 128)
    l6 = lam3.reshape(3, 2, 4, 4, 2, 64)
    lam_s = np.ascontiguousarray(l6.transpose(4, 5, 0, 2, 1, 3)).reshape(128, 3, 4, 2, 4)
    c7 = cri.reshape(2, 2, 4, 4, 2, 16, 64)
    c_s = np.zeros((2, 64, 4, 2, 2, 4, 2, 16), f32)
    b7 = bri.reshape(2, 2, 4, 4, 2, 64, 16)
    b_s = np.zeros((2, 64, 4, 2, 2, 4, 2, 16), f32)
    for gp in range(2):
        c_s[gp, :, :, :, :, :, gp, :] = c7[:, :, :, :, gp, :, :].transpose(5, 2, 0, 1, 3, 4)
        b_s[gp, :, :, :, :, :, gp, :] = b7[:, :, :, :, gp, :, :].transpose(4, 2, 0, 1, 3, 5)
    c_s = c_s.reshape(128, 4, 2, 2, 4, 32)
    b_s = b_s.reshape(128, 4, 2, 2, 4, 32)
    return {
        "xl": xl,
        "w_in": np.ascontiguousarray(inp["w_in"][0], dtype=f32),
        "w_out": np.ascontiguousarray(inp["w_out"][0], dtype=f32),
        "w_ff1": np.ascontiguousarray(inp["w_ff1"][0], dtype=f32),
        "w_ff2": np.ascontiguousarray(inp["w_ff2"][0], dtype=f32),
        "glu_w": np.ascontiguousarray(inp["ssm_glu_w"][0], dtype=f32),
        "pv": np.ascontiguousarray(pv),
        "cw": cw,
        "gfin": np.ascontiguousarray(inp["norm_final_g"], dtype=f32),
        "ident": np.eye(128, dtype=f32),
        "iota": np.ascontiguousarray(np.broadcast_to(np.arange(NBWD, dtype=f32)[None, :], (128, NBWD))),
        "lam_c": np.ascontiguousarray(lam_c, dtype=f32),
        "b_c": np.ascontiguousarray(b_c, dtype=f32),
        "lam_s": np.ascontiguousarray(lam_s, dtype=f32),
        "c_s": np.ascontiguousarray(c_s, dtype=f32),
        "b_s": np.ascontiguousarray(b_s, dtype=f32),
    }


_NC_CACHE = {}


def kernel(**inputs):
    inp = {k: np.asarray(v) for k, v in inputs.items()}
    if "nc" not in _NC_CACHE:
        _NC_CACHE["nc"] = _build()
    nc = _NC_CACHE["nc"]
    in_maps = [_core_inputs(inp, c // 2, c % 2) for c in range(8)]
    res = run_bass_kernel_spmd(nc, in_maps, core_ids=list(range(8)))
    out = np.empty((4, 4096, D), np.float32)
    for c in range(8):
        o = np.asarray(res.results[c]["out"])
        if c % 2 == 0:
            out[c // 2, 0:2048] = o
        else:
            out[c // 2, 2048:4096] = o[::-1]
    return out
```

```python
import numpy as np
from contextlib import ExitStack, contextmanager
import concourse.bass as bass
import concourse.mybir as mybir
from concourse.bass_utils import run_bass_kernel_spmd

F32 = mybir.dt.float32
BF16 = mybir.dt.bfloat16
AF = mybir.ActivationFunctionType
ALU = mybir.AluOpType


class _Op:
    __slots__ = ("eng", "seq", "sigval", "dma", "sem", "target")


class Prog:
    def __init__(self, nc, same_eng_sync=True, n_dma_sems=32):
        self.nc = nc
        self.same = same_eng_sync
        self.root = ExitStack()
        self.stacks = [self.root]
        self.eng = {"pe": nc.tensor, "act": nc.scalar, "dve": nc.vector, "pool": nc.gpsimd, "sp": nc.sync}
        self.esem = {k: self.root.enter_context(nc.semaphore("es_" + k)) for k in self.eng}
        self.ecnt = {k: 0 for k in self.eng}
        self.eseq = {k: 0 for k in self.eng}
        self.esigs = {k: [] for k in self.eng}
        nd = n_dma_sems // 2
        self.dpool = {"sp": list(range(0, nd)), "act": list(range(0, nd)), "pool": list(range(nd, 2 * nd))}
        self.dsems = [self.root.enter_context(nc.semaphore("ds%d" % i)) for i in range(2 * nd)]
        self.dcum = [0] * (2 * nd)
        self.dlast = [None] * (2 * nd)
        self.dnext = {"sp": 0, "act": 0, "pool": 0}
        self.waited = {k: {} for k in self.eng}
        self.last_w = {}
        self.readers = {}
        self.finals = []
        self.nops = 0

    def sb(self, name, shape, dt):
        return self.stacks[-1].enter_context(self.nc.sbuf_tensor("s_" + name, shape, dt))

    def ps(self, name, shape, dt):
        return self.stacks[-1].enter_context(self.nc.psum_tensor("p_" + name, shape, dt))

    @contextmanager
    def scope(self):
        st = ExitStack()
        self.stacks.append(st)
        try:
            yield
        finally:
            self.fence()
            self.stacks.pop()
            st.close()

    def fence(self):
        for e in self.eng:
            for e2 in self.eng:
                if self.ecnt[e2] > 0:
                    self._wait(e, self.esem[e2], self.ecnt[e2])
            for o in self.dlast:
                if o is not None:
                    self._wait(e, o.sem, o.target)

    def _wait(self, eng, sem, val):
        d = self.waited[eng]
        if d.get(sem.name, -1) >= val:
            return
        d[sem.name] = val
        self.eng[eng].wait_ge(sem, val)

    def _wait_op(self, eng, dep):
        if dep.dma:
            self._wait(eng, dep.sem, dep.target)
            return
        if dep.eng == eng and (eng == "pe" or not self.same):
            return
        sv = dep.sigval
        if sv is None:
            for (s, v) in self.esigs[dep.eng]:
                if s >= dep.seq:
                    sv = v
                    break
            if sv is None:
                raise RuntimeError("dependency on non-signalling op with no later signal on " + dep.eng)
            dep.sigval = sv
        self._wait(eng, self.esem[dep.eng], sv)

    def _deps(self, r, w):
        deps = []
        for k in r:
            o = self.last_w.get(k)
            if o is not None:
                deps.append(o)
        for k in w:
            o = self.last_w.get(k)
            if o is not None:
                deps.append(o)
            deps.extend(self.readers.get(k, ()))
        return deps

    def _record(self, op, r, w):
        for k in r:
            self.readers.setdefault(k, []).append(op)
        for k in w:
            self.last_w[k] = op
            self.readers[k] = []

    def op(self, eng, fn, r=(), w=(), sig=True):
        deps = self._deps(r, w)
        o = _Op()
        o.eng = eng
        o.dma = False
        o.sem = None
        o.target = None
        for d in deps:
            self._wait_op(eng, d)
        ins = fn(self.eng[eng])
        self.eseq[eng] += 1
        o.seq = self.eseq[eng]
        if sig:
            self.ecnt[eng] += 1
            o.sigval = self.ecnt[eng]
            ins.then_inc(self.esem[eng], 1)
            self.esigs[eng].append((o.seq, o.sigval))
            if len(self.esigs[eng]) > 4096:
                del self.esigs[eng][:2048]
        else:
            o.sigval = None
        self._record(o, r, w)
        self.nops += 1
        return o

    def dma(self, eng, out, in_, r=(), w=(), final=False, **kw):
        deps = self._deps(r, w)
        pool_ = self.dpool[eng]
        key_ = "pool" if eng == "pool" else "sp"
        i = pool_[self.dnext[key_] % len(pool_)]
        self.dnext[key_] += 1
        if self.dlast[i] is not None:
            deps.append(self.dlast[i])
        for d in deps:
            self._wait_op(eng, d)
        o = _Op()
        o.eng = eng
        o.dma = True
        o.sem = self.dsems[i]
        self.dcum[i] += 16
        o.target = self.dcum[i]
        o.seq = None
        o.sigval = None
        self.eng[eng].dma_start(out=out, in_=in_, **kw).then_inc(o.sem, 16)
        self.dlast[i] = o
        self._record(o, r, w)
        if final:
            self.finals.append(o)
        self.nops += 1
        return o

    def emit(self):
        for o in self.finals:
            self._wait_op("sp", o)
        self.stacks[-1].close()


D = 1024
NOWN = 2048
LT = 4128
NFWD = 258
NBWD = 514
TWOPI_S = 6.283179
PV_GMIX, PV_GFFN, PV_CONVB, PV_LNG, PV_LNB, PV_SSMD, PV_GLUB = 0, 8, 16, 20, 24, 28, 32


class V:
    __slots__ = ("ap", "key")

    def __init__(self, ap, key):
        self.ap = ap
        self.key = key


def _build(stage=99):
    nc = bass.Bass("TRN2", target_bir_lowering=False)

    def din(name, shape):
        return nc.dram_tensor(name, list(shape), F32, kind="ExternalInput").ap()

    def dout(name, shape, dt=F32):
        return nc.dram_tensor(name, list(shape), dt, kind="ExternalOutput").ap()

    xl = din("xl", [LT, D])
    w_in_d = din("w_in", [D, 1536])
    w_out_d = din("w_out", [D, D])
    w_ff1_d = din("w_ff1", [D, 4096])
    w_ff2_d = din("w_ff2", [4096, D])
    glu_w_d = din("glu_w", [512, 512])
    pv_d = din("pv", [128, 36])
    cw_d = din("cw", [128, 4, 31])
    gfin_d = din("gfin", [D])
    ident_d = din("ident", [128, 128])
    lam_c_d = din("lam_c", [128, 4, 3, 2, 128])
    b_c_d = din("b_c", [128, 4, 2, 2, 128])
    lam_s_d = din("lam_s", [128, 3, 4, 2, 4])
    c_s_d = din("c_s", [128, 4, 2, 2, 4, 32])
    b_s_d = din("b_s", [128, 4, 2, 2, 4, 32])
    iota_d = din("iota", [128, NBWD])
    out_d = dout("out", [NOWN, D])

    P = Prog(nc)
    op = P.op

    ident = P.sb("ident", [128, 128], BF16)
    identf = P.sb("identf", [128, 128], F32)
    pv = P.sb("pv", [128, 36], F32)
    epst = P.sb("epst", [128, 1], F32)
    hpi = P.sb("hpi", [128, 1], F32)
    onesf = P.sb("onesf", [128, 128], F32)
    pb = [P.ps("pb%d" % i, [128, 512], F32) for i in range(8)]
    pbk = ["pb%d" % i for i in range(8)]
    P.dma("pool", ident[:], ident_d, w=["ident"])
    P.dma("sp", identf[:], ident_d, w=["identf"])
    P.dma("sp", pv[:], pv_d, w=["pv"])
    op("pool", lambda e: e.memset(epst[:], 1e-5), w=["epst"])
    op("pool", lambda e: e.memset(hpi[:], float(np.pi / 2)), w=["hpi"])
    op("pool", lambda e: e.memset(onesf[:], 1.0 / 512.0), w=["onesf"])

    mixed = P.sb("mixed", [128, 8, NOWN], BF16)
    w_out_bf = P.sb("w_out_bf", [128, 8, D], BF16)

    def rms_tile(xt_ap, nr, xkey, ss, sskey, sq, zt_ap, ztkey, eng_scale):
        op("act", lambda e: e.activation(sq[:nr, :], xt_ap, AF.Square, accum_out=ss[:nr, 0:1]),
           r=[xkey], w=["sq", sskey])
        op("act", lambda e: e.activation(ss[:nr, 1:2], ss[:nr, 0:1], AF.Sqrt, bias=epst[:nr, 0:1], scale=1.0 / D),
           r=[sskey, "epst"], w=[sskey])
        op("dve", lambda e: e.reciprocal(ss[:nr, 2:3], ss[:nr, 1:2]), r=[sskey], w=[sskey])
        if eng_scale == "dve":
            op("dve", lambda e: e.tensor_scalar(zt_ap, xt_ap, ss[:nr, 2:3], None, op0=ALU.mult), r=[xkey, sskey], w=[ztkey])
        else:
            op("act", lambda e: e.activation(zt_ap, xt_ap, AF.Identity, scale=ss[:nr, 2:3]), r=[xkey, sskey], w=[ztkey])

    def transpose_tile(zt_t, ztkey, nr, pbi, gcol, dst_ap3, dstkey):
        pbv = pb[pbi][:].bitcast(BF16)
        for k in range(8):
            op("pe", lambda e, k=k: e.transpose(pbv[:, k * 128:k * 128 + nr], zt_t[:nr, k * 128:(k + 1) * 128],
                                               ident[:nr, :nr]),
               r=[ztkey, "ident"], w=[pbk[pbi]], sig=(k == 7))
        src = pbv.rearrange("p (k t) -> p k t", t=128)[:, :, :nr]
        g3 = pv[:, gcol:gcol + 8].unsqueeze(2).to_broadcast([128, 8, nr])
        op("dve", lambda e: e.tensor_tensor(dst_ap3, src, g3, op=ALU.mult), r=[pbk[pbi], "pv"], w=[dstkey])

    dumps = []

    def dump(name, t, shape, dt, keys):
        dd = dout(name, shape, dt)
        dumps.append(P.dma("sp", dd, t, r=keys, w=["dump_" + name], final=True))

    with P.scope():
        u_s = P.sb("u_s", [128, 4, 8, 516], BF16)
        lsa = P.sb("lsa", [128, 3, 4, 2, 4], F32)
        P.dma("sp", lsa[:], lam_s_d, w=["lsa"])
        uniq = [0]

        def mk(pre, F, n):
            uniq[0] += 1
            return [V(P.sb("%s%d_%d" % (pre, uniq[0], i), [128, F], F32)[:], "%s%d_%d" % (pre, uniq[0], i)) for i in range(n)]

        def tt(eng, o, a, b, o_):
            op(eng, lambda e: e.tensor_tensor(o.ap, a.ap, b.ap, op=o_), r=[a.key, b.key], w=[o.key])

        def ts(eng, o, a, s1, o1):
            op(eng, lambda e: e.tensor_scalar(o.ap, a.ap, s1, None, op0=o1), r=[a.key], w=[o.key])

        def act(o, a, f, **kw):
            op("act", lambda e: e.activation(o.ap, a.ap, f, **kw), r=[a.key, "hpi"], w=[o.key])

        def cmul(eng, ore, oim, are, aim, bre, bim, t1, t2):
            tt(eng, t1, are, bre, ALU.mult)
            tt(eng, t2, aim, bim, ALU.mult)
            tt(eng, ore, t1, t2, ALU.subtract)
            tt(eng, t1, are, bim, ALU.mult)
            tt(eng, t2, aim, bre, ALU.mult)
            tt(eng, oim, t1, t2, ALU.add)

        def dbl(eng, c, s, t1, t2, t3):
            tt(eng, t1, c, c, ALU.mult)
            tt(eng, t2, s, s, ALU.mult)
            tt(eng, t3, c, s, ALU.mult)
            ts(eng, s, t3, 2.0, ALU.mult)
            tt(eng, c, t1, t2, ALU.subtract)

        sm_ops = []
        real_op = op
        op = lambda *a_, **k_: sm_ops.append((a_, k_))
        eng = "dve"
        lre, lim, ldt = [V(lsa[:, q].rearrange("p a b c -> p (a b c)"), "lsa") for q in range(3)]
        T = mk("sm", 32, 14)
        dt, are, th, r, s, c, t1, t2, t3, S1re, S1im, sre, sim_, a = T
        act(dt, ldt, AF.Exp)
        tt(eng, are, lre, dt, ALU.mult)
        tt(eng, th, lim, dt, ALU.mult)
        act(r, are, AF.Exp)
        act(s, th, AF.Sin, scale=1.0 / 32.0)
        act(c, th, AF.Sin, scale=1.0 / 32.0, bias=hpi[:, 0:1])
        for _ in range(5):
            dbl(eng, c, s, t1, t2, t3)
        tt(eng, S1re, r, c, ALU.mult)
        tt(eng, S1im, r, s, ALU.mult)
        ts(eng, a, S1re, -1.0, ALU.add)
        tt(eng, t1, a, lre, ALU.mult)
        tt(eng, t2, S1im, lim, ALU.mult)
        tt(eng, sre, t1, t2, ALU.add)
        tt(eng, t1, S1im, lre, ALU.mult)
        tt(eng, t2, a, lim, ALU.mult)
        tt(eng, sim_, t1, t2, ALU.subtract)
        tt(eng, t1, lre, lre, ALU.mult)
        tt(eng, t2, lim, lim, ALU.mult)
        tt(eng, t3, t1, t2, ALU.add)
        op("dve", lambda e: e.reciprocal(t3.ap, t3.ap), r=[t3.key], w=[t3.key])
        tt(eng, sre, sre, t3, ALU.mult)
        tt(eng, sim_, sim_, t3, ALU.mult)
        u1, u2 = t1, t2
        rho8 = mk("smr", 32, 1)[0]
        act(rho8, are, AF.Exp, scale=8.0)
        pw_ = mk("smp", 32, 16)
        npim = mk("smn", 32, 9)
        qk_ = mk("smq", 32, 16)
        nqim = mk("smm", 32, 8)
        Pk = {1: (S1re, S1im)}
        for k in range(2, 9):
            Pk[k] = (pw_[2 * (k - 2)], pw_[2 * (k - 2) + 1])
            cmul("dve", Pk[k][0], Pk[k][1], Pk[k - 1][0], Pk[k - 1][1], S1re, S1im, u1, u2)
        for k in range(1, 9):
            ts("dve", npim[k], Pk[k][1], -1.0, ALU.mult)
        Qk = {0: (sre, sim_)}
        for k in range(1, 8):
            Qk[k] = (qk_[2 * k], qk_[2 * k + 1])
            cmul("dve", Qk[k][0], Qk[k][1], sre, sim_, Pk[k][0], Pk[k][1], u1, u2)
        for k in range(8):
            ts("dve", nqim[k], Qk[k][1], -1.0, ALU.mult)
        sturn = mk("smt", 32, 1)[0]
        ts(eng, sturn, th, float(8.0 / (2.0 * np.pi)), ALU.mult)
        op = real_op
        sm_pos = [0]

        def sm_replay(n):
            for a_, k_ in sm_ops[sm_pos[0]:sm_pos[0] + n]:
                op(*a_, **k_)
            sm_pos[0] = min(len(sm_ops), sm_pos[0] + n)

        with P.scope():
            uc = P.sb("uc", [128, 4, 2080], BF16)
            with P.scope():
                w_in_bf = P.sb("w_in_bf", [128, 8, 1536], BF16)
                for k in range(8):
                    P.dma("pool", w_in_bf[:, k, :], w_in_d[k * 128:(k + 1) * 128, :], w=["win%d" % k])
                winkeys = ["win%d" % k for k in range(8)]
                xt = [P.sb("xt%d" % i, [128, D], F32) for i in range(4)]
                zt = [P.sb("zt%d" % i, [128, D], BF16) for i in range(4)]
                sq = P.sb("sq", [128, D], F32)
                ssa = [P.sb("ssa%d" % i, [128, 4], F32) for i in range(4)]
                zfm = [P.sb("zfm%d" % i, [128, 8, 512], BF16) for i in range(2)]
                sg = [P.sb("sg%d" % i, [128, 512], F32) for i in range(2)]
                NTILE = 33

                def norm_act(g):
                    b = g % 4
                    r0 = g * 128
                    nr = min(128, LT - r0)
                    P.dma("sp", xt[b][:nr, :], xl[r0:r0 + nr, :], w=["xt%d" % b])
                    rms_tile(xt[b][:nr, :], nr, "xt%d" % b, ssa[b], "ssa%d" % b, sq, zt[b][:nr, :], "zt%d" % b, "dve")

                def norm_tr(g):
                    b = g % 4
                    blk, ti = g // 4, g % 4
                    r0 = g * 128
                    nr = min(128, LT - r0)
                    zb = blk % 2
                    transpose_tile(zt[b], "zt%d" % b, nr, g % 2, PV_GMIX, zfm[zb][:, :, ti * 128:ti * 128 + nr],
                                   "zfm%d_%d" % (zb, ti))

                def inproj_c(blk, c):
                    t0 = blk * 512
                    nb = min(512, LT - t0)
                    zb = blk % 2
                    zf = zfm[zb]
                    zkeys = ["zfm%d_%d" % (zb, ti) for ti in range((nb + 127) // 128)]
                    nfull = nb if blk < 4 else (32 if blk == 4 else 0)
                    if True:
                        if nfull > 0:
                            pv_i, pg_i = 2 + c % 2, 4 + c % 2
                            for k in range(8):
                                op("pe", lambda e, k=k, c=c: e.matmul(pb[pv_i][:, :nfull], w_in_bf[:, k, c * 128:(c + 1) * 128],
                                                                     zf[:, k, :nfull], start=(k == 0), stop=(k == 7)),
                                   r=zkeys + winkeys, w=[pbk[pv_i]], sig=(k == 7))
                            for k in range(8):
                                op("pe", lambda e, k=k, c=c: e.matmul(pb[pg_i][:, :nfull],
                                                                     w_in_bf[:, k, 512 + c * 128:512 + (c + 1) * 128],
                                                                     zf[:, k, :nfull], start=(k == 0), stop=(k == 7)),
                                   r=zkeys + winkeys, w=[pbk[pg_i]], sig=(k == 7))
                            sgt = sg[c % 2]
                            op("act", lambda e: e.activation(sgt[:, :nfull], pb[pg_i][:, :nfull], AF.Sigmoid),
                               r=[pbk[pg_i]], w=["sg%d" % (c % 2)])
                            op("dve", lambda e, c=c: e.tensor_tensor(uc[:, c, t0:t0 + nfull], pb[pv_i][:, :nfull],
                                                                    sgt[:, :nfull], op=ALU.mult),
                               r=[pbk[pv_i], "sg%d" % (c % 2)], w=["uc%d" % c])
                        pu_i = 6 + c % 2
                        for k in range(8):
                            op("pe", lambda e, k=k, c=c: e.matmul(pb[pu_i][:, :nb],
                                                                 w_in_bf[:, k, 1024 + c * 128:1024 + (c + 1) * 128],
                                                                 zf[:, k, :nb], start=(k == 0), stop=(k == 7)),
                               r=zkeys + winkeys, w=[pbk[pu_i]], sig=(k == 7))
                        op("act", lambda e, c=c: e.copy(u_s[:, c, :, t0 // 8:t0 // 8 + nb // 8],
                                                        pb[pu_i][:, :nb].rearrange("p (c s) -> p s c", s=8)),
                           r=[pbk[pu_i]], w=["us%d" % c])
                norm_act(0)
                norm_act(1)
                for g in range(4):
                    norm_act(g + 2)
                    norm_tr(g)
                for blk in range(9):
                    for c in range(4):
                        g_tr = 4 * (blk + 1) + c
                        if c % 2 == 0:
                            if g_tr + 2 < NTILE:
                                norm_act(g_tr + 2)
                            if g_tr + 3 < NTILE:
                                norm_act(g_tr + 3)
                        if g_tr < NTILE:
                            norm_tr(g_tr)
                        inproj_c(blk, c)
                        if blk == 3 and c == 0:
                            P.dma("pool", w_out_bf[:], w_out_d.rearrange("(k p) n -> p k n", p=128), w=["wout"])
                        if blk >= 1:
                            sm_replay(24)
                sm_replay(len(sm_ops))
            if stage == 1:
                dump("d_uc", uc[:], [128, 4, 2080], BF16, ["uc%d" % c for c in range(4)])
                dump("d_us", u_s[:], [128, 4, 8, 516], BF16, ["us%d" % c for c in range(4)])
                P.emit()
                return nc

            with P.scope():
                cw = P.sb("cw", [128, 4, 31], F32)
                P.dma("sp", cw[:], cw_d, w=["cw"])
                CD = P.sb("CD", [128, 4, 31, 128], BF16)
                for c in range(4):
                    for k in range(31):
                        if k % 2:
                            op("act", lambda e, c=c, k=k: e.activation(CD[:, c, k, :], identf[:], AF.Identity, scale=cw[:, c, k:k + 1]),
                               r=["identf", "cw"], w=["CD%d" % c])
                        else:
                            op("dve", lambda e, c=c, k=k: e.tensor_scalar(CD[:, c, k, :], identf[:], cw[:, c, k:k + 1], None, op0=ALU.mult),
                               r=["identf", "cw"], w=["CD%d" % c])
                yc = [P.sb("yc%d" % i, [128, 4, 512], F32) for i in range(2)]
                ysq = [P.sb("ysq%d" % i, [128, 4, 512], F32) for i in range(2)]
                st = [[P.sb("st%d_%d" % (i, j), [128, 512], F32) for j in range(4)] for i in range(2)]
                tmp = [P.sb("ctmp%d" % i, [128, 512], F32) for i in range(2)]
                def conv_mm(blk):
                    pp = blk % 2
                    for c in range(4):
                        for k in range(31):
                            op("pe", lambda e, c=c, k=k: e.matmul(pb[c][:, :], CD[:, c, k, :],
                                                                 uc[:, c, blk * 512 + 1 + k: blk * 512 + 1 + k + 512],
                                                                 start=(k == 0), stop=(k == 30)),
                               r=["CD%d" % c, "uc%d" % c], w=[pbk[c]], sig=(k == 30))
                        op("act", lambda e, c=c: e.activation(yc[pp][:, c, :], pb[c][:, :], AF.Identity,
                                                              bias=pv[:, PV_CONVB + c:PV_CONVB + c + 1]),
                           r=[pbk[c], "pv"], w=["yc%d_%d" % (pp, c)])
                        op("pool", lambda e, c=c: e.tensor_tensor(ysq[pp][:, c, :], yc[pp][:, c, :], yc[pp][:, c, :], op=ALU.mult),
                           r=["yc%d_%d" % (pp, c)], w=["ysq%d_%d" % (pp, c)])

                def conv_post(blk):
                    pp = blk % 2
                    pm, pq = 4 + 2 * pp, 5 + 2 * pp
                    for c in range(4):
                        op("pe", lambda e, c=c: e.matmul(pb[pm][:, :], onesf[:], yc[pp][:, c, :], start=(c == 0), stop=(c == 3)),
                           r=["onesf", "yc%d_%d" % (pp, c)], w=[pbk[pm]], sig=(c == 3))
                    for c in range(4):
                        op("pe", lambda e, c=c: e.matmul(pb[pq][:, :], onesf[:], ysq[pp][:, c, :], start=(c == 0), stop=(c == 3)),
                           r=["onesf", "ysq%d_%d" % (pp, c)], w=[pbk[pq]], sig=(c == 3))
                    mean, msq, var, rstd = st[pp]
                    sk = "st%d" % pp
                    op("act", lambda e: e.copy(mean[:], pb[pm][:]), r=[pbk[pm]], w=[sk + "m"])
                    op("act", lambda e: e.activation(msq[:], pb[pm][:], AF.Square), r=[pbk[pm]], w=[sk + "q"])
                    op("dve", lambda e: e.tensor_tensor(var[:], pb[pq][:], msq[:], op=ALU.subtract), r=[pbk[pq], sk + "q"], w=[sk + "v"])
                    op("act", lambda e: e.activation(var[:], var[:], AF.Sqrt, bias=epst[:, 0:1]), r=[sk + "v", "epst"], w=[sk + "v"])
                    op("dve", lambda e: e.reciprocal(rstd[:], var[:]), r=[sk + "v"], w=[sk + "r"])
                    for c in range(4):
                        tm = tmp[c % 2]
                        tk = "ctmp%d" % (c % 2)
                        op("pool", lambda e, c=c, tm=tm: e.tensor_tensor(tm[:], yc[pp][:, c, :], mean[:], op=ALU.subtract),
                           r=["yc%d_%d" % (pp, c), sk + "m"], w=[tk])
                        op("dve", lambda e, tm=tm: e.tensor_tensor(tm[:], tm[:], rstd[:], op=ALU.mult), r=[tk, sk + "r"], w=[tk])
                        op("act", lambda e, c=c, tm=tm: e.activation(mixed[:, c, blk * 512:(blk + 1) * 512], tm[:], AF.Silu,
                                                                     bias=pv[:, PV_LNB + c:PV_LNB + c + 1],
                                                                     scale=pv[:, PV_LNG + c:PV_LNG + c + 1]),
                           r=[tk, "pv"], w=["mixed%d" % c])

                conv_mm(0)
                for blk in range(4):
                    if blk + 1 < 4:
                        conv_mm(blk + 1)
                    conv_post(blk)
        if stage == 2:
            dump("d_mixed", mixed[:, 0:4, :], [128, 4, NOWN], BF16, ["mixed%d" % c for c in range(4)])
            P.emit()
            return nc

        with P.scope():
            WBs = [P.sb("WB%d" % i, [128, 2, 8, 2, 128], BF16) for i in range(2)]
            CW = P.sb("CW", [128, 8, 2, 2, 4, 32], BF16)
            KF = P.sb("KF", [128, 2, 8, 128], BF16)
            iota = P.sb("iota", [128, NBWD], F32)
            tbt = [P.sb("tbt%d" % i, [128, NBWD], F32) for i in range(2)]
            Dd = P.sb("Dd", [128, 128], BF16)
            Sb = P.sb("Sb", [128, 2, 4, 2, NFWD], BF16)
            VAbig = P.sb("VAbig", [128, 3, 2, NBWD], F32)
            VAb = [VAbig[:, i] for i in range(3)]
            TBb = [P.sb("TBb%d" % i, [128, 2, NBWD], F32) for i in range(3)]
            DMb = [P.sb("DMb%d" % i, [128, 2, NBWD], F32) for i in range(2)]
            pt1 = P.sb("pt1", [128, NBWD], F32)
            pt2 = P.sb("pt2", [128, NBWD], F32)
            pt5 = P.sb("pt5", [128, NBWD], F32)
            pt6 = P.sb("pt6", [128, NBWD], F32)
            pt3 = P.sb("pt3", [128, NFWD], F32)
            pt4 = P.sb("pt4", [128, NFWD], F32)
            bcs = [P.sb("bc0", [128, 2, 2, 128], F32)] * 2
            cs = P.sb("cs", [128, 2, 2, 4, 32], F32)
            bs = P.sb("bs", [128, 2, 2, 4, 32], F32)
            Bb = P.sb("Bb", [128, 2, 2, 4, 32], BF16)
            C0 = P.sb("C0", [128, 2, 2, 4, 32], BF16)
            LTt = [P.sb("LTt%d" % i, [128, 4, 32], F32) for i in range(8)]
            wt1 = P.sb("wt1", [128, 2, 128], F32)
            wt2 = P.sb("wt2", [128, 2, 128], F32)
            wt3 = P.sb("wt3", [128, 2, 128], F32)
            wt4 = P.sb("wt4", [128, 2, 128], F32)
            mgc = P.sb("mgc", [128, 2], F32)
            op("pool", lambda e: e.memset(mgc[:, 0:1], 12582912.0), w=["mgc"])
            op("pool", lambda e: e.memset(mgc[:, 1:2], -12582912.0), w=["mgc"])
            gt = [P.sb("gt%d" % i, [128, 512], F32) for i in range(4)]
            xsf = VAbig[:, 0:2].rearrange("p a b c -> p (a b c)")[:, 0:NOWN]
            onet = P.sb("onet", [128, 1], F32)
            op("pool", lambda e: e.memset(onet[:], 1.0), w=["onet"])
            op("pool", lambda e: e.memset(KF[:], 0.0), w=["KF"])
            P.dma("sp", iota[:], iota_d, w=["iota"])

            big = mk("smb", 256, 4)

            def b3v(v):
                return V(v.ap.rearrange("p (a c) -> p a c", c=32), v.key)

            B1, B2, B3, B4 = [b3v(x_) for x_ in big]

            def rev(t_, ri, start, n):
                return bass.AP(t_, ri * NBWD + start, [[2 * NBWD, 128], [-1, n]])

            def revA(i3, ri, start, n):
                return bass.AP(VAbig, (i3 * 2 + ri) * NBWD + start, [[6 * NBWD, 128], [-1, n]])

            def prep_wb(jj, half=None):
                WB = WBs[jj % 2]
                wbk = "WB%d" % (jj % 2)
                bc = bcs[0]
                bck = "bc0"
                if half in (None, 0):
                    P.dma("sp", bc[:], b_c_d[:, jj], w=[bck])
                es = [e_ for e_ in range(8) if half is None or e_ // 4 == half]

                def lt_copies(e_):
                    for ri in range(2):
                        for dl in range(2):
                            li = (e_ % 2) * 4 + ri * 2 + dl
                            qv = Qk[e_][ri]
                            o0 = jj * 8 + dl * 4
                            op("act", lambda e, li=li, qv=qv, o0=o0: e.copy(
                                LTt[li][:], qv.ap[:, o0:o0 + 4].unsqueeze(2).to_broadcast([128, 4, 32])), r=[qv.key], w=["LT%d" % li])

                lt_copies(es[0])
                for e_ in es:
                    if e_ + 1 in es:
                        lt_copies(e_ + 1)
                    pw = e_ % 4
                    for ri in range(2):
                        for dl in range(2):
                            li = (e_ % 2) * 4 + ri * 2 + dl
                            lt = LTt[li]
                            ltk = "LT%d" % li
                            col = (ri * 2 + dl) * 128
                            op("pe", lambda e, lt=lt, col=col, pw=pw: e.matmul(
                                pb[pw][:, col:col + 128], lt[:].rearrange("p a b -> p (a b)"), identf[:], start=True, stop=True),
                               r=[ltk, "identf"], w=[pbk[pw]], sig=(ri == 1 and dl == 1))
                    qre_ps = pb[pw][:, 0:256].rearrange("p (a b) -> p a b", b=128)
                    qim_ps = pb[pw][:, 256:512].rearrange("p (a b) -> p a b", b=128)
                    op("dve", lambda e, qre_ps=qre_ps: e.tensor_tensor(wt1[:], qre_ps, bc[:, 0], op=ALU.mult), r=[pbk[pw], bck], w=["wt1"])
                    op("dve", lambda e, qim_ps=qim_ps: e.tensor_tensor(wt2[:], qim_ps, bc[:, 1], op=ALU.mult), r=[pbk[pw], bck], w=["wt2"])
                    op("dve", lambda e, qre_ps=qre_ps: e.tensor_tensor(wt3[:], qre_ps, bc[:, 1], op=ALU.mult), r=[pbk[pw], bck], w=["wt3"])
                    op("dve", lambda e, qim_ps=qim_ps: e.tensor_tensor(wt4[:], qim_ps, bc[:, 0], op=ALU.mult), r=[pbk[pw], bck], w=["wt4"])
                    op("dve", lambda e, e_=e_: e.tensor_tensor(WB[:, 0, e_, :, :], wt1[:], wt2[:], op=ALU.subtract), r=["wt1", "wt2"], w=[wbk])
                    op("dve", lambda e, e_=e_: e.tensor_tensor(WB[:, 1, e_, :, :], wt3[:], wt4[:], op=ALU.add), r=["wt3", "wt4"], w=[wbk + "i"])

            def prep_cwkf(jj, part=None):
                if part in (None, "a"):
                    P.dma("sp", cs[:], c_s_d[:, jj], w=["cs"])
                    P.dma("sp", bs[:], b_s_d[:, jj], w=["bs"])

                def bc8(v):
                    return V(v.ap[:, jj * 8:(jj + 1) * 8].unsqueeze(2).to_broadcast([128, 8, 32]), v.key)

                csre = V(cs[:, 0].rearrange("p a b c -> p (a b) c"), "cs")
                csim = V(cs[:, 1].rearrange("p a b c -> p (a b) c"), "cs")
                bsre = V(bs[:, 0].rearrange("p a b c -> p (a b) c"), "bs")
                bsim = V(bs[:, 1].rearrange("p a b c -> p (a b) c"), "bs")
                for k in range(1, 9):
                    if part == "b":
                        break
                    pre_, pim_ = Pk[k]
                    ce = "pool" if k % 2 == 0 else "dve"
                    Ba, Bc = (B3, B4) if k % 2 == 0 else (B1, B2)
                    cwk = "CW%d" % (k - 1)
                    tt(ce, Ba, csre, bc8(pre_), ALU.mult)
                    tt(ce, Bc, csim, bc8(pim_), ALU.mult)
                    op(ce, lambda e, k=k, Ba=Ba, Bc=Bc: e.tensor_tensor(CW[:, k - 1, 0].rearrange("p a b c -> p (a b) c"), Ba.ap, Bc.ap,
                                                                       op=ALU.subtract), r=[Ba.key, Bc.key], w=[cwk])
                    tt(ce, Ba, csre, bc8(npim[k]), ALU.mult)
                    tt(ce, Bc, csim, bc8(pre_), ALU.mult)
                    op(ce, lambda e, k=k, Ba=Ba, Bc=Bc: e.tensor_tensor(CW[:, k - 1, 1].rearrange("p a b c -> p (a b) c"), Ba.ap, Bc.ap,
                                                                       op=ALU.subtract), r=[Ba.key, Bc.key], w=[cwk])
                if part == "b":
                    pass
                else:
                  qre, qim = Qk[0]
                  tt("dve", B1, bsre, bc8(qre), ALU.mult)
                  tt("dve", B2, bsim, bc8(qim), ALU.mult)
                  op("dve", lambda e: e.tensor_tensor(Bb[:, 0].rearrange("p a b c -> p (a b) c"), B1.ap, B2.ap, op=ALU.subtract),
                     r=[B1.key, B2.key], w=["Bb"])
                  tt("dve", B1, bsre, bc8(qim), ALU.mult)
                  tt("dve", B2, bsim, bc8(qre), ALU.mult)
                  op("dve", lambda e: e.tensor_tensor(Bb[:, 1].rearrange("p a b c -> p (a b) c"), B1.ap, B2.ap, op=ALU.add),
                     r=[B1.key, B2.key], w=["Bb"])
                  op("dve", lambda e: e.tensor_copy(C0[:, 0].rearrange("p a b c -> p (a b) c"), csre.ap), r=["cs"], w=["C0"])
                  op("dve", lambda e: e.tensor_scalar(C0[:, 1].rearrange("p a b c -> p (a b) c"), csim.ap, -1.0, None, op0=ALU.mult),
                     r=["cs"], w=["C0"])
                if part == "a":
                    return
                for k in range(8):
                    for dl in range(2):
                        for q in range(4):
                            col = (dl * 8 + k) * 32
                            first = (k == 0 and dl == 0)
                            if k == 0:
                                r_re, r_im, rk = C0[:, 0, dl, q, :], C0[:, 1, dl, q, :], "C0"
                            else:
                                r_re, r_im, rk = CW[:, k - 1, 0, dl, q, :], CW[:, k - 1, 1, dl, q, :], "CW%d" % (k - 1)
                            op("pe", lambda e, dl=dl, q=q, col=col, first=first, r_re=r_re: e.matmul(
                                pb[3][32 * q:32 * q + 32, col:col + 32], Bb[:, 0, dl, q, :], r_re,
                                start=first, stop=False, skip_group_check=True, tile_position=(0, 32 * q)),
                               r=["Bb", rk], w=[pbk[3]], sig=False)
                            op("pe", lambda e, dl=dl, q=q, col=col, r_im=r_im: e.matmul(
                                pb[3][32 * q:32 * q + 32, col:col + 32], Bb[:, 1, dl, q, :], r_im,
                                start=False, stop=True, skip_group_check=True, tile_position=(0, 32 * q)),
                               r=["Bb", rk], w=[pbk[3]], sig=True)
                pk4 = pb[3][:].rearrange("p (a k c) -> p a k c", a=2, k=8)
                for q in range(4):
                    op("act", lambda e, q=q: e.copy(KF[32 * q:32 * q + 32, :, :, 32 * q:32 * q + 32], pk4[32 * q:32 * q + 32]),
                       r=[pbk[3]], w=["KF"])
                op("act", lambda e: e.activation(Dd[:], identf[:], AF.Identity, scale=pv[:, PV_SSMD + jj:PV_SSMD + jj + 1]),
                   r=["identf", "pv"], w=["Dd"])

            def make_emitters(jj):
                WB = WBs[jj % 2]
                wbk = "WB%d" % (jj % 2)

                tiles = [(dl, q) for dl in range(2) for q in range(4)]

                def emit_v(i):
                    dl, q = tiles[i]
                    base = 0 if dl == 0 else 16
                    cbs = [(0, NFWD)] if dl == 0 else [(0, 257), (257, 257)]
                    A_ = VAb[(i + 2) % 3]
                    ak = "VAb%d" % ((i + 2) % 3)
                    pc = 0
                    for ri in range(2):
                        for (c0, n) in cbs:
                            pi_ = (i * 4 + pc) % 4
                            pc += 1
                            for s_ in range(8):
                                sig_ = (7 - s_) if dl == 0 else s_
                                st_ = base + s_ + 8 * c0
                                op("pe", lambda e, q=q, ri=ri, sig_=sig_, st_=st_, n=n, s_=s_, pi_=pi_, dl=dl: e.matmul(
                                    pb[pi_][:, :n], WB[32 * q:32 * q + 32, ri, sig_, dl, :],
                                    u_s[32 * q:32 * q + 32, jj, s_, base // 8 + c0:base // 8 + c0 + n],
                                    start=(s_ == 0), stop=(s_ == 7), tile_position=(32 * q, 0)),
                                   r=[wbk, wbk + "i", "us%d" % jj], w=[pbk[pi_]], sig=(s_ == 7))
                            op("act", lambda e, ri=ri, c0=c0, n=n, pi_=pi_, A_=A_: e.copy(A_[:, ri, c0:c0 + n], pb[pi_][:, :n]),
                               r=[pbk[pi_]], w=[ak])

                def emit_table(i):
                    dl, q = tiles[i]
                    N = NFWD if dl == 0 else NBWD
                    T_ = TBb[(i + 2) % 3]
                    tk = "TBb%d" % ((i + 2) % 3)
                    o0 = jj * 8 + dl * 4 + q
                    tq, tr = tbt
                    MAGIC = 12582912.0
                    op("act", lambda e: e.activation(tq[:, 0:N], iota[:, 0:N], AF.Identity, scale=sturn.ap[:, o0:o0 + 1]),
                       r=["iota", sturn.key], w=["tbt0"])
                    op("act", lambda e: e.activation(tr[:, 0:N], tq[:, 0:N], AF.Identity, bias=mgc[:, 0:1]), r=["tbt0", "mgc"], w=["tbt1"])
                    op("act", lambda e: e.activation(tr[:, 0:N], tr[:, 0:N], AF.Identity, bias=mgc[:, 1:2]), r=["tbt1", "mgc"], w=["tbt1"])
                    op("dve", lambda e: e.tensor_tensor(tq[:, 0:N], tq[:, 0:N], tr[:, 0:N], op=ALU.subtract), r=["tbt0", "tbt1"], w=["tbt0"])
                    op("act", lambda e: e.activation(T_[:, 1, 0:N], tq[:, 0:N], AF.Sin, scale=TWOPI_S), r=["tbt0"], w=[tk])
                    op("act", lambda e: e.activation(tr[:, 0:N], tq[:, 0:N], AF.Sin, scale=TWOPI_S / 2.0), r=["tbt0"], w=["tbt1"])
                    op("act", lambda e: e.activation(tr[:, 0:N], tr[:, 0:N], AF.Square), r=["tbt1"], w=["tbt1"])
                    op("act", lambda e: e.activation(T_[:, 0, 0:N], tr[:, 0:N], AF.Identity, scale=-2.0, bias=onet[:, 0:1]),
                       r=["tbt1", "onet"], w=[tk])

                def emit_demod(i):
                    dl, q = tiles[i]
                    N = NFWD if dl == 0 else NBWD
                    A_, T_, D_ = VAb[(i + 2) % 3], TBb[(i + 2) % 3], DMb[i % 2]
                    ak, tk, dk = "VAb%d" % ((i + 2) % 3), "TBb%d" % ((i + 2) % 3), "DMb%d" % (i % 2)
                    if dl == 0:
                        wre, wim = A_[:, 0, 0:N], A_[:, 1, 0:N]
                    else:
                        wre, wim = revA((i + 2) % 3, 0, N - 1, N), revA((i + 2) % 3, 1, N - 1, N)
                    op("pool", lambda e: e.tensor_tensor(pt1[:, 0:N], T_[:, 0, 0:N], wre, op=ALU.mult), r=[tk, ak], w=["pt1"])
                    op("pool", lambda e: e.tensor_tensor(pt2[:, 0:N], T_[:, 1, 0:N], wim, op=ALU.mult), r=[tk, ak], w=["pt2"])
                    op("pool", lambda e: e.tensor_tensor(pt5[:, 0:N], T_[:, 0, 0:N], wim, op=ALU.mult), r=[tk, ak], w=["pt5"])
                    op("pool", lambda e: e.tensor_tensor(pt6[:, 0:N], T_[:, 1, 0:N], wre, op=ALU.mult), r=[tk, ak], w=["pt6"])
                    op("pool", lambda e: e.tensor_tensor(D_[:, 0, 0:N], pt1[:, 0:N], pt2[:, 0:N], op=ALU.add), r=["pt1", "pt2"], w=[dk])
                    op("pool", lambda e: e.tensor_tensor(D_[:, 1, 0:N], pt5[:, 0:N], pt6[:, 0:N], op=ALU.subtract), r=["pt5", "pt6"], w=[dk + "i"])

                def emit_scan(i):
                    dl, q = tiles[i]
                    N = NFWD if dl == 0 else NBWD
                    A_, D_ = VAb[(i + 2) % 3], DMb[i % 2]
                    ak, dk = "VAb%d" % ((i + 2) % 3), "DMb%d" % (i % 2)
                    o0 = jj * 8 + dl * 4 + q
                    rho = rho8.ap[:, o0:o0 + 1].to_broadcast([128, N])
                    for ri in range(2):
                        op("dve", lambda e, ri=ri: e.tensor_tensor_scan(A_[:, ri, 0:N], rho, D_[:, ri, 0:N], 0.0, op0=ALU.mult, op1=ALU.add),
                           r=[dk, dk + "i", rho8.key], w=[ak])

                def emit_remod(i):
                    dl, q = tiles[i]
                    N = NFWD if dl == 0 else NBWD
                    A_, T_ = VAb[(i + 2) % 3], TBb[(i + 2) % 3]
                    ak, tk = "VAb%d" % ((i + 2) % 3), "TBb%d" % ((i + 2) % 3)
                    n = NFWD
                    if dl == 0:
                        tre, tim, xre, xim = T_[:, 0, 0:n], T_[:, 1, 0:n], A_[:, 0, 0:n], A_[:, 1, 0:n]
                    else:
                        tre, tim = rev(T_, 0, N - 1, n), rev(T_, 1, N - 1, n)
                        xre, xim = revA((i + 2) % 3, 0, N - 1, n), revA((i + 2) % 3, 1, N - 1, n)
                    sk = "Sb%d_%d" % (dl, q)
                    op("pool", lambda e: e.tensor_tensor(pt1[:, 0:n], tre, xre, op=ALU.mult), r=[tk, ak], w=["pt1"])
                    op("pool", lambda e: e.tensor_tensor(pt2[:, 0:n], tim, xim, op=ALU.mult), r=[tk, ak], w=["pt2"])
                    op("pool", lambda e: e.tensor_tensor(Sb[:, dl, q, 0, :], pt1[:, 0:n], pt2[:, 0:n], op=ALU.subtract), r=["pt1", "pt2"], w=[sk + "r"])
                    op("dve", lambda e: e.tensor_tensor(pt3[:, 0:n], tre, xim, op=ALU.mult), r=[tk, ak], w=["pt3"])
                    op("dve", lambda e: e.tensor_tensor(pt4[:, 0:n], tim, xre, op=ALU.mult), r=[tk, ak], w=["pt4"])
                    op("dve", lambda e: e.tensor_tensor(Sb[:, dl, q, 1, :], pt3[:, 0:n], pt4[:, 0:n], op=ALU.add), r=["pt3", "pt4"], w=[sk + "i"])

                return emit_v, emit_table, emit_demod, emit_scan, emit_remod

            def output_phase(jj):
                sbkeys = ["Sb%d_%d%s" % (dl, q, x_) for dl in range(2) for q in range(4) for x_ in ("r", "i")]

                def ycol(t):
                    return pb[4 + t // 2], pbk[4 + t // 2], (t % 2) * 256

                def ucol(s_):
                    return u_s[:, jj, s_, 2:258]

                for t in range(8):
                    pt_, pk_, c0 = ycol(t)
                    op("pe", lambda e, pt_=pt_, c0=c0, t=t: e.matmul(pt_[:, c0:c0 + 256], Dd[:], ucol(t), start=(t % 2 == 0), stop=False,
                                                                    skip_group_check=True),
                       r=["Dd", "us%d" % jj], w=[pk_], sig=False)
                for dl in range(2):
                    for t in range(8):
                        pt_, pk_, c0 = ycol(t)
                        srange = range(0, t + 1) if dl == 0 else range(t, 8)
                        for s_ in srange:
                            k = abs(t - s_)
                            op("pe", lambda e, dl=dl, s_=s_, k=k, pt_=pt_, c0=c0: e.matmul(
                                pt_[:, c0:c0 + 256], KF[:, dl, k, :], ucol(s_),
                                start=False, stop=False, skip_group_check=True),
                               r=["KF", "us%d" % jj], w=[pk_], sig=False)
                for t in range(8):
                    pt_, pk_, c0 = ycol(t)
                    cnt = 0
                    for dl in range(2):
                        eidx = t if dl == 0 else 7 - t
                        for q in range(4):
                            for ri in range(2):
                                cnt += 1
                                last = (cnt == 16)
                                op("pe", lambda e, dl=dl, q=q, ri=ri, eidx=eidx, pt_=pt_, c0=c0, last=last: e.matmul(
                                    pt_[32 * q:32 * q + 32, c0:c0 + 256], CW[:, eidx, ri, dl, q, :],
                                    Sb[:, dl, q, ri, 1:257],
                                    start=False, stop=last, skip_group_check=True, tile_position=(0, 32 * q)),
                                   r=["CW%d" % eidx] + sbkeys, w=[pk_], sig=last)
                xview = xsf[:].rearrange("p (c t) -> p t c", t=8)
                for b_ in range(4):
                    op("act", lambda e, b_=b_: e.copy(xview[:, 2 * b_:2 * b_ + 2, :], pb[4 + b_][:].rearrange("p (t c) -> p t c", t=2)),
                       r=[pbk[4 + b_]], w=["xsf", "VAb0", "VAb1"])
                for bp in range(2):
                    bs_ = [2 * bp, 2 * bp + 1]
                    G1 = {b_: gt[b_ % 2] for b_ in bs_}
                    G2 = {b_: gt[2 + b_ % 2] for b_ in bs_}
                    K1 = {b_: "gt%d" % (b_ % 2) for b_ in bs_}
                    K2 = {b_: "gt%d" % (2 + b_ % 2) for b_ in bs_}
                    XS = {b_: xsf[:, b_ * 512:(b_ + 1) * 512] for b_ in bs_}
                    xk_ = ["xsf", "VAb0", "VAb1"]
                    for b_ in bs_:
                        op("act", lambda e, b_=b_: e.activation(G1[b_][:], XS[b_], AF.Square), r=xk_, w=[K1[b_]])
                    for b_ in bs_:
                        op("act", lambda e, b_=b_: e.activation(G1[b_][:], G1[b_][:], AF.Identity, scale=0.044715, bias=onet[:, 0:1]),
                           r=[K1[b_], "onet"], w=[K1[b_]])
                    for b_ in bs_:
                        op("dve", lambda e, b_=b_: e.tensor_tensor(G1[b_][:], G1[b_][:], XS[b_], op=ALU.mult), r=[K1[b_]] + xk_, w=[K1[b_]])
                    for b_ in bs_:
                        op("act", lambda e, b_=b_: e.activation(G2[b_][:], G1[b_][:], AF.Sigmoid, scale=1.5957691216057308), r=[K1[b_]], w=[K2[b_]])
                    for b_ in bs_:
                        op("dve", lambda e, b_=b_: e.tensor_tensor(mixed[:, 4 + jj, b_ * 512:(b_ + 1) * 512], XS[b_], G2[b_][:], op=ALU.mult),
                           r=xk_ + [K2[b_]], w=["Y%d_b%d" % (4 + jj, b_)])
            gluw = P.sb("gluw", [128, 4, 512], BF16)
            P.dma("pool", gluw[:], glu_w_d.rearrange("(k p) n -> p k n", p=128), w=["gluw"])
            ems = [make_emitters(jj) for jj in range(4)]
            prep_wb(0)
            ems[0][0](0)
            ems[0][1](0)
            for jj in range(4):
                emit_v, emit_table, emit_demod, emit_scan, emit_remod = ems[jj]
                for i in range(9):
                    if i + 1 < 8:
                        emit_v(i + 1)
                    if i < 8:
                        emit_demod(i)
                    if i >= 1:
                        emit_remod(i - 1)
                    if i + 1 < 8:
                        emit_table(i + 1)
                    if i < 8:
                        emit_scan(i)
                    if i == 1:
                        prep_cwkf(jj, "a")
                    if i == 6:
                        prep_cwkf(jj, "b")
                    if i == 3 and jj < 3:
                        prep_wb(jj + 1, 0)
                    if i == 5 and jj < 3:
                        prep_wb(jj + 1, 1)
                if jj < 3:
                    ems[jj + 1][0](0)
                    ems[jj + 1][1](0)
                output_phase(jj)
            if stage == 4:
                dump("d_ygb", mixed[:, 4:8, :], [128, 4, NOWN], BF16, ["Y%d_b%d" % (c, b_) for c in range(4, 8) for b_ in range(4)])
                P.emit()
                return nc
            for blk in range(4):
                for m in range(4):
                    for k in range(4):
                        op("pe", lambda e, m=m, k=k, gb=(blk % 2) * 4 + m: e.matmul(pb[gb][:, :], gluw[:, k, m * 128:(m + 1) * 128],
                                                             mixed[:, 4 + k, blk * 512:(blk + 1) * 512], start=(k == 0), stop=(k == 3)),
                           r=["gluw"] + ["Y%d_b%d" % (c, blk) for c in range(4, 8)], w=[pbk[(blk % 2) * 4 + m]], sig=(k == 3))
                    op("act", lambda e, m=m, gb=(blk % 2) * 4 + m: e.activation(gt[m][:], pb[gb][:, :], AF.Sigmoid,
                                                                                bias=pv[:, PV_GLUB + m:PV_GLUB + m + 1]),
                       r=[pbk[(blk % 2) * 4 + m], "pv"], w=["gt%d" % m])
                for m in range(4):
                    op("dve", lambda e, m=m: e.tensor_tensor(mixed[:, 4 + m, blk * 512:(blk + 1) * 512],
                                                            mixed[:, 4 + m, blk * 512:(blk + 1) * 512], gt[m][:], op=ALU.mult),
                       r=["gt%d" % m, "Y%d_b%d" % (4 + m, blk)], w=["Y%d_b%d" % (4 + m, blk)])
    if stage == 5:
        dump("d_mixed", mixed[:], [128, 8, NOWN], BF16, ["mixed%d" % c for c in range(4)] + ["Y%d_b%d" % (c, b_) for c in range(4, 8) for b_ in range(4)])
        P.emit()
        return nc

    h = P.sb("h", [128, 16, D], F32)
    zffn = P.sb("zffn", [128, 8, NOWN], BF16)
    ssb = [P.sb("ssb%d" % i, [128, 4], F32) for i in range(2)]
    sq2 = P.sb("sq2", [128, D], F32)
    zt2 = [P.sb("zt2_%d" % i, [128, D], BF16) for i in range(2)]
    w1c = [P.sb("w1c%d" % i, [128, 8, 512], BF16) for i in range(2)]
    w2c = [P.sb("w2c%d" % i, [128, 4, D], BF16) for i in range(2)]
    w1v = w_ff1_d.rearrange("(k p) n -> p k n", p=128)
    w2v = w_ff2_d.rearrange("(c f p) n -> c p f n", f=4, p=128)
    if True:
        P.dma("pool", w1c[0][:], w1v[:, :, 0:512], w=["w1c0"])
        P.dma("pool", w2c[0][:], w2v[0], w=["w2c0"])
        OPB = [(0, 1), (2, 3), (6, 7)]

        def outproj(tt_):
            hk = "h%d" % tt_
            P.dma("sp", h[:, tt_, :], xl[16 + tt_ * 128:16 + (tt_ + 1) * 128, :], w=[hk])
            for half in range(2):
                pi_ = OPB[tt_ % 3][half]
                for k in range(8):
                    op("pe", lambda e, k=k, half=half: e.matmul(pb[pi_][:, :], mixed[:, k, tt_ * 128:(tt_ + 1) * 128],
                                                               w_out_bf[:, k, half * 512:(half + 1) * 512],
                                                               start=(k == 0), stop=(k == 7)),
                       r=["wout"] + ["mixed%d" % c for c in range(4)] + ["Y%d_b%d" % (c, tt_ // 4) for c in range(4, 8)],
                       w=[pbk[pi_]], sig=(k == 7))

        def resid_add(tt_):
            hk = "h%d" % tt_
            for half in range(2):
                pi_ = OPB[tt_ % 3][half]
                op("dve", lambda e, half=half: e.tensor_tensor(h[:, tt_, half * 512:(half + 1) * 512],
                                                               h[:, tt_, half * 512:(half + 1) * 512], pb[pi_][:, :], op=ALU.add),
                   r=[hk, pbk[pi_]], w=[hk])

        def norm_chain(tt_):
            b = tt_ % 2
            hk = "h%d" % tt_
            rms_tile(h[:, tt_, :], 128, hk, ssb[b], "ssb%d" % b, sq2, zt2[b][:, :], "zt2_%d" % b, "pool")

        def tr_ffn(tt_):
            b = tt_ % 2
            transpose_tile(zt2[b], "zt2_%d" % b, 128, 4 + tt_ % 2, PV_GFFN, zffn[:, :, tt_ * 128:(tt_ + 1) * 128], "zffn%d" % tt_)

        outproj(0)
        outproj(1)
        resid_add(0)
        for tt_ in range(16):
            norm_chain(tt_)
            if tt_ + 1 < 16:
                resid_add(tt_ + 1)
            if tt_ + 2 < 16:
                outproj(tt_ + 2)
            tr_ffn(tt_)
    if stage == 6:
        dump("d_h", h[:], [128, 16, D], F32, ["h%d" % i for i in range(16)])
        dump("d_zffn", zffn[:], [128, 8, NOWN], BF16, ["zffn%d" % i for i in range(16)])
        P.emit()
        return nc

    with P.scope():
        hid = [P.sb("hid%d" % i, [128, 4, 512], BF16) for i in range(2)]
        rl = [P.sb("rl%d" % i, [128, 512], F32) for i in range(2)]
        gfin = P.sb("gfin", [128, D], F32)
        P.dma("sp", gfin[:], gfin_d.partition_broadcast(128), w=["gfin"])
        def load_w(fc):
            wb_ = fc % 2
            P.dma("pool", w1c[wb_][:], w1v[:, :, fc * 512:(fc + 1) * 512], w=["w1c%d" % wb_])
            P.dma("pool", w2c[wb_][:], w2v[fc], w=["w2c%d" % wb_])

        cnts = {"h": 0, "o": 0}

        def ffn_hidden(step):
            fc, blk = step // 4, step % 4
            wb_ = fc % 2
            hb = step % 2
            for f in range(4):
                pi_ = cnts["h"] % 4
                rb = cnts["h"] % 2
                cnts["h"] += 1
                for k in range(8):
                    op("pe", lambda e, k=k, f=f: e.matmul(pb[pi_][:, :], w1c[wb_][:, k, f * 128:(f + 1) * 128],
                                                         zffn[:, k, blk * 512:(blk + 1) * 512], start=(k == 0), stop=(k == 7)),
                       r=["w1c%d" % wb_] + ["zffn%d" % (blk * 4 + i) for i in range(4)], w=[pbk[pi_]], sig=(k == 7))
                op("act", lambda e: e.activation(rl[rb][:], pb[pi_][:, :], AF.Relu), r=[pbk[pi_]], w=["rl%d" % rb])
                op("pool", lambda e, f=f: e.tensor_tensor(hid[hb][:, f, :], rl[rb][:], rl[rb][:], op=ALU.mult),
                   r=["rl%d" % rb], w=["hid%d_%d" % (hb, f)])

        def ffn_out(step):
            fc, blk = step // 4, step % 4
            wb_ = fc % 2
            hb = step % 2
            for ti in range(4):
                tt_ = blk * 4 + ti
                hk = "h%d" % tt_
                for half in range(2):
                    pi_ = 4 + cnts["o"] % 4
                    cnts["o"] += 1
                    for f in range(4):
                        op("pe", lambda e, f=f, half=half: e.matmul(pb[pi_][:, :], hid[hb][:, f, ti * 128:(ti + 1) * 128],
                                                                   w2c[wb_][:, f, half * 512:(half + 1) * 512],
                                                                   start=(f == 0), stop=(f == 3)),
                           r=["w2c%d" % wb_] + ["hid%d_%d" % (hb, i) for i in range(4)], w=[pbk[pi_]], sig=(f == 3))
                    op("dve", lambda e, half=half: e.tensor_tensor(h[:, tt_, half * 512:(half + 1) * 512],
                                                                   h[:, tt_, half * 512:(half + 1) * 512], pb[pi_][:, :], op=ALU.add),
                       r=[hk, pbk[pi_]], w=[hk])

        def final_norm(tt_):
            b = tt_ % 2
            hk = "h%d" % tt_
            ss = ssb[b]
            sk = "ssb%d" % b
            op("act", lambda e: e.activation(sq2[:, :], h[:, tt_, :], AF.Square, accum_out=ss[:, 0:1]), r=[hk], w=["sq2", sk])
            op("act", lambda e: e.activation(ss[:, 1:2], ss[:, 0:1], AF.Sqrt, bias=epst[:, 0:1], scale=1.0 / D), r=[sk, "epst"], w=[sk])
            op("dve", lambda e: e.reciprocal(ss[:, 2:3], ss[:, 1:2]), r=[sk], w=[sk])
            op("dve", lambda e: e.scalar_tensor_tensor(h[:, tt_, :], h[:, tt_, :], ss[:, 2:3], gfin[:], op0=ALU.mult, op1=ALU.mult),
               r=[hk, sk, "gfin"], w=[hk])
            P.dma("sp", out_d[tt_ * 128:(tt_ + 1) * 128, :], h[:, tt_, :], r=[hk], w=["out%d" % tt_], final=True)

        ffn_hidden(0)
        for step in range(32):
            if step % 4 == 0 and step // 4 + 1 < 8:
                load_w(step // 4 + 1)
            if step + 1 < 32:
                ffn_hidden(step + 1)
            ffn_out(step)
            if step >= 28:
                for ti in range(4):
                    final_norm((step - 28) * 4 + ti)
    P.emit()
    return nc


def _core_inputs(inp, b, half):
    f32 = np.float32
    x = inp["x"][b]
    meta = inp["meta_tokens"]
    z16 = np.zeros((16, D), f32)
    if half == 0:
        xl = np.concatenate([meta, x, z16], axis=0)
        dirs = [0, 1]
        convw = inp["conv_w"][0]
    else:
        full = np.concatenate([meta, x], axis=0)[::-1]
        xl = np.concatenate([z16, full], axis=0)
        dirs = [1, 0]
        convw = inp["conv_w"][0][::-1]
    xl = np.ascontiguousarray(xl, dtype=f32)

    def col8(v):
        return np.ascontiguousarray(v.reshape(8, 128).T)

    def col4(v):
        return np.ascontiguousarray(v.reshape(4, 128).T)

    pv = np.concatenate([col8(inp["norm_mix_g"][0]), col8(inp["norm_ffn_g"][0]), col4(inp["conv_b"][0]),
                         col4(inp["conv_ln_g"][0]), col4(inp["conv_ln_b"][0]), col4(inp["ssm_d"][0]),
                         col4(inp["ssm_glu_b"][0])], axis=1).astype(f32)
    cw = np.ascontiguousarray(convw.T.reshape(4, 128, 31).transpose(1, 0, 2)).astype(f32)

    lam = [inp["ssm_lam_re"][0][dirs], inp["ssm_lam_im"][0][dirs]]
    ldt = np.broadcast_to(inp["ssm_log_dt"][0][dirs][:, :, None], (2, 32, 64))
    lam3 = np.stack([lam[0], lam[1], ldt], axis=0)
    bri = np.stack([inp["ssm_b_re"][0][dirs], inp["ssm_b_im"][0][dirs]], axis=0)
    cri = np.stack([inp["ssm_c_re"][0][dirs], inp["ssm_c_im"][0][dirs]], axis=0)

    l5 = lam3.reshape(3, 2, 4, 8, 64)
    lam_c = np.broadcast_to(l5.transpose(3, 2, 0, 1, 4)[:, None, :, :, :, None, :],
                            (8, 16, 4, 3, 2, 2, 64)).reshape(128, 4, 3, 2, 128)
    b6 = bri.reshape(2, 2, 4, 8, 64, 16)
    b_c = np.zeros((8, 16, 4, 2, 2, 2, 64), f32)
    for gl in range(8):
        b_c[gl, :, :, :, :, gl % 2, :] = b6[:, :, :, gl, :, :].transpose(4, 2, 0, 1, 3)
    b_c = b_c.reshape(128, 4, 2, 2, 128)
    l6 = lam3.reshape(3, 2, 4, 4, 2, 64)
    lam_s = np.ascontiguousarray(l6.transpose(4, 5, 0, 2, 1, 3)).reshape(128, 3, 4, 2, 4)
    c7 = cri.reshape(2, 2, 4, 4, 2, 16, 64)
    c_s = np.zeros((2, 64, 4, 2, 2, 4, 2, 16), f32)
    b7 = bri.reshape(2, 2, 4, 4, 2, 64, 16)
    b_s = np.zeros((2, 64, 4, 2, 2, 4, 2, 16), f32)
    for gp in range(2):
        c_s[gp, :, :, :, :, :, gp, :] = c7[:, :, :, :, gp, :, :].transpose(5, 2, 0, 1, 3, 4)
        b_s[gp, :, :, :, :, :, gp, :] = b7[:, :, :, :, gp, :, :].transpose(4, 2, 0, 1, 3, 5)
    c_s = c_s.reshape(128, 4, 2, 2, 4, 32)
    b_s = b_s.reshape(128, 4, 2, 2, 4, 32)
    return {
        "xl": xl,
        "w_in": np.ascontiguousarray(inp["w_in"][0], dtype=f32),
        "w_out": np.ascontiguousarray(inp["w_out"][0], dtype=f32),
        "w_ff1": np.ascontiguousarray(inp["w_ff1"][0], dtype=f32),
        "w_ff2": np.ascontiguousarray(inp["w_ff2"][0], dtype=f32),
        "glu_w": np.ascontiguousarray(inp["ssm_glu_w"][0], dtype=f32),
        "pv": np.ascontiguousarray(pv),
        "cw": cw,
        "gfin": np.ascontiguousarray(inp["norm_final_g"], dtype=f32),
        "ident": np.eye(128, dtype=f32),
        "iota": np.ascontiguousarray(np.broadcast_to(np.arange(NBWD, dtype=f32)[None, :], (128, NBWD))),
        "lam_c": np.ascontiguousarray(lam_c, dtype=f32),
        "b_c": np.ascontiguousarray(b_c, dtype=f32),
        "lam_s": np.ascontiguousarray(lam_s, dtype=f32),
        "c_s": np.ascontiguousarray(c_s, dtype=f32),
        "b_s": np.ascontiguousarray(b_s, dtype=f32),
    }


_NC_CACHE = {}


def kernel(**inputs):
    inp = {k: np.asarray(v) for k, v in inputs.items()}
    if "nc" not in _NC_CACHE:
        _NC_CACHE["nc"] = _build()
    nc = _NC_CACHE["nc"]
    in_maps = [_core_inputs(inp, c // 2, c % 2) for c in range(8)]
    res = run_bass_kernel_spmd(nc, in_maps, core_ids=list(range(8)))
    out = np.empty((4, 4096, D), np.float32)
    for c in range(8):
        o = np.asarray(res.results[c]["out"])
        if c % 2 == 0:
            out[c // 2, 0:2048] = o
        else:
            out[c // 2, 2048:4096] = o[::-1]
    return out
```
